# Optimizing a Trainium2 kernel written in Bass

```python
import math
import jax, jax.numpy as jnp
from jax import lax
import numpy as np

D_MODEL = 1024
BATCH = 2
SEQ = 8192
DEPTH = 4
DEC_BATCH = 32
DEC_SEQ = 4
PAST_LEN = 8192
PAGE_SIZE = 128

HEAD_DIM = 64
SB_HEADS = 8
SB_WIDTH = SB_HEADS * HEAD_DIM
SB_BIAS_INIT = -6.0
LRU_WIDTH = D_MODEL // 2
LRU_BLOCKS = 8
LRU_BLOCK_W = LRU_WIDTH // LRU_BLOCKS
CONV_W = 4
RGLRU_C = 8.0
DIL_HEADS = D_MODEL // HEAD_DIM
DIL_WIDTH = DIL_HEADS * HEAD_DIM
DIL_PAIRS = ((128, 1), (512, 4), (2048, 16))
DIL_MAX_WINDOW = 2048
D_FF = 4 * D_MODEL
REL_BUCKETS = 32
REL_MAX_DIST = 2048
Q_BLOCK = 128
NORM_EPS = 1e-6
N_EVEN = (DEPTH + 1) // 2
N_ODD = DEPTH // 2
AB_IN_WIDTH = 3 * SB_WIDTH + 2 * LRU_WIDTH

kernel_name = 'hybrid_sb_rglru_dilated_decode_step'


def _rms_norm(x, g):
    xf = x.astype(jnp.float32)
    y = xf * lax.rsqrt(jnp.mean(xf * xf, axis=-1, keepdims=True) + NORM_EPS)
    return (y * g.astype(jnp.float32)).astype(x.dtype)


def _sq_relu_mlp(x, w1, w2):
    return jnp.square(jax.nn.relu(x @ w1)) @ w2


def _rel_bucket(dist):
    exact = REL_BUCKETS // 2
    df = jnp.maximum(dist.astype(jnp.float32), 1.0)
    large = exact + (jnp.log(df / exact) / math.log(REL_MAX_DIST / exact)
                     * (REL_BUCKETS - exact)).astype(jnp.int32)
    large = jnp.minimum(large, REL_BUCKETS - 1)
    return jnp.where(dist < exact, dist, large)


def _stick_breaking(q, k, v, sb_bias, q_offset):
    b, tq, h, hd = q.shape
    tk = k.shape[1]
    blk = Q_BLOCK if tq % Q_BLOCK == 0 else tq
    n_blk = tq // blk
    kpos = jnp.arange(tk)
    scale = 1.0 / math.sqrt(hd)
    bias = sb_bias.astype(jnp.float32)[None, :, None, None]

    def one_block(args):
        qb, start = args
        qpos = q_offset + start + jnp.arange(blk)
        z = jnp.einsum('bqhd,bkhd->bhqk', qb, k).astype(jnp.float32) * scale + bias
        mask = kpos[None, :] < qpos[:, None]
        sp = jnp.where(mask, jax.nn.softplus(z), 0.0)
        rest = lax.cumsum(sp, axis=3, reverse=True) - sp
        w = jnp.where(mask, jnp.exp(jax.nn.log_sigmoid(z) - rest), 0.0)
        return jnp.einsum('bhqk,bkhd->bqhd', w.astype(v.dtype), v)

    qs = q.reshape(b, n_blk, blk, h, hd).swapaxes(0, 1)
    starts = jnp.arange(n_blk) * blk
    o = lax.map(one_block, (qs, starts))
    return o.swapaxes(0, 1).reshape(b, tq, h, hd)


def _gather_pages(pool, page_table):
    pages = pool[page_table]
    b, n, p, h, d = pages.shape
    return pages.reshape(b, n * p, h, d)


def _causal_conv(x, state, w, bias):
    t = x.shape[1]
    xp = jnp.concatenate([state.astype(x.dtype), x], axis=1)
    y = bias + sum(w[j] * xp[:, j:j + t] for j in range(CONV_W))
    return y, xp[:, t:]


def _lin_combine(left, right):
    a_l, h_l = left
    a_r, h_r = right
    return a_l * a_r, a_r * h_l + h_r


def _rglru(xc, h0, wa, ba, wx, bx, lam):
    b, t, w = xc.shape
    xf = xc.astype(jnp.float32)
    xb = xf.reshape(b, t, LRU_BLOCKS, LRU_BLOCK_W)
    r = jax.nn.sigmoid(jnp.einsum('btnc,ncd->btnd', xb, wa.astype(jnp.float32)).reshape(b, t, w) + ba)
    i = jax.nn.sigmoid(jnp.einsum('btnc,ncd->btnd', xb, wx.astype(jnp.float32)).reshape(b, t, w) + bx)
    log_a = -RGLRU_C * jax.nn.softplus(-lam.astype(jnp.float32)) * r
    a = jnp.exp(log_a)
    inp = jnp.sqrt(-jnp.expm1(2.0 * log_a)) * (i * xf)
    inp = inp.at[:, 0].add(a[:, 0] * h0.astype(jnp.float32))
    _, h = lax.associative_scan(_lin_combine, (a, inp), axis=1)
    return h.astype(xc.dtype), h[:, -1].astype(xc.dtype)


def _even_mixer(u, past_k, past_v, conv_state, lru_state, q_offset,
                w_in, sb_bias, cw, cb, wa, ba, wx, bx, lam, w_out):
    b, t, _ = u.shape
    z = u @ w_in
    q, k, v, xr, g = jnp.split(z, [SB_WIDTH, 2 * SB_WIDTH, 3 * SB_WIDTH,
                                   3 * SB_WIDTH + LRU_WIDTH], axis=-1)
    q = q.reshape(b, t, SB_HEADS, HEAD_DIM)
    k = k.reshape(b, t, SB_HEADS, HEAD_DIM)
    v = v.reshape(b, t, SB_HEADS, HEAD_DIM)
    attn = _stick_breaking(q, jnp.concatenate([past_k.astype(k.dtype), k], axis=1),
                           jnp.concatenate([past_v.astype(v.dtype), v], axis=1), sb_bias, q_offset)
    xc, conv_new = _causal_conv(xr, conv_state, cw, cb)
    h, h_last = _rglru(xc, lru_state, wa, ba, wx, bx, lam)
    lru_out = jax.nn.gelu(g) * h
    out = jnp.concatenate([attn.reshape(b, t, SB_WIDTH).astype(u.dtype), lru_out], axis=-1) @ w_out
    return out, k, v, conv_new, h_last


def _dil_qkv(u, w_in):
    b, t, _ = u.shape
    q, k, v = jnp.split(u @ w_in, 3, axis=-1)
    shp = (b, t, DIL_HEADS, HEAD_DIM)
    return q.reshape(shp), k.reshape(shp), v.reshape(shp)


def _dil_branch_prompt(q, k, v, rel_bias, window, dil):
    b, s, h, hd = q.shape
    n_back = window // dil
    L = s // dil
    qb_len = math.gcd(Q_BLOCK, L)
    n_blk = L // qb_len
    kw = qb_len + n_back

    def to_res(t):
        return t.reshape((b, L, dil) + t.shape[2:]).swapaxes(1, 2).reshape((b * dil, L) + t.shape[2:])

    def from_res(t):
        return t.reshape((b, dil, L) + t.shape[3:]).swapaxes(1, 2).reshape((b, s) + t.shape[3:])

    pad = ((0, 0), (n_back, 0), (0, 0), (0, 0))
    kp = jnp.pad(to_res(k), pad)
    vp = jnp.pad(to_res(v), pad)
    kidx = jnp.arange(n_blk)[:, None] * qb_len + jnp.arange(kw)[None, :]
    kb, vb = kp[:, kidx], vp[:, kidx]
    qb = to_res(q).reshape(b * dil, n_blk, qb_len, h, hd)
    logits = jnp.einsum('gnqhd,gnkhd->gnhqk', qb, kb).astype(jnp.float32) / math.sqrt(hd)
    qi = jnp.arange(qb_len)[:, None]
    kk = jnp.arange(kw)[None, :]
    j = qi + n_back - kk
    lpos = jnp.arange(n_blk)[:, None, None] * qb_len + kk[None] - n_back
    valid = ((j >= 0) & (j <= n_back))[None] & (lpos >= 0)
    bias = rel_bias[_rel_bucket(jnp.clip(j, 0, n_back) * dil)].astype(jnp.float32)
    logits = logits + bias.transpose(2, 0, 1)
    logits = jnp.where(valid[None, :, None], logits, -jnp.inf)
    m = jnp.max(logits, axis=-1)
    p = jnp.exp(logits - m[..., None])
    den = jnp.sum(p, axis=-1)
    o = jnp.einsum('gnhqk,gnkhd->gnqhd', p.astype(vb.dtype), vb) / den.transpose(0, 1, 3, 2)[..., None]
    return from_res(o), from_res(m.transpose(0, 1, 3, 2)), from_res(den.transpose(0, 1, 3, 2))


def _dil_branch_sample(q, kcat, vcat, rel_bias, window, dil, buf_len):
    b, t, h, hd = q.shape
    n_back = window // dil
    j = jnp.arange(n_back + 1)
    idx = buf_len + jnp.arange(t)[:, None] - j[None, :] * dil
    valid = idx >= 0
    idx = jnp.maximum(idx, 0)
    kg, vg = kcat[:, idx], vcat[:, idx]
    logits = jnp.einsum('bqhd,bqjhd->bqhj', q, kg).astype(jnp.float32) / math.sqrt(hd)
    logits = logits + rel_bias[_rel_bucket(j * dil)].astype(jnp.float32).T
    logits = jnp.where(valid[None, :, None, :], logits, -jnp.inf)
    m = jnp.max(logits, axis=-1)
    p = jnp.exp(logits - m[..., None])
    den = jnp.sum(p, axis=-1)
    o = jnp.einsum('bqhj,bqjhd->bqhd', p.astype(vg.dtype), vg) / den[..., None]
    return o, m, den


def _merge_branches(outs):
    m_max = jnp.max(jnp.stack([m for _, m, _ in outs]), axis=0)
    num = 0.0
    tot = 0.0
    for o, m, den in outs:
        w = den * jnp.exp(m - m_max)
        num = num + w[..., None] * o
        tot = tot + w
    return num / tot[..., None]


def _odd_mixer_prompt(u, w_in, w_out, rel_bias):
    b, t, _ = u.shape
    q, k, v = _dil_qkv(u, w_in)
    outs = [_dil_branch_prompt(q, k, v, rel_bias, win, dil) for win, dil in DIL_PAIRS]
    o = _merge_branches(outs).astype(u.dtype).reshape(b, t, DIL_WIDTH)
    keep = min(DIL_MAX_WINDOW, t)
    return o @ w_out, k[:, t - keep:], v[:, t - keep:]


def _odd_mixer_sample(u, buf_k, buf_v, w_in, w_out, rel_bias):
    b, t, _ = u.shape
    q, k, v = _dil_qkv(u, w_in)
    kcat = jnp.concatenate([buf_k.astype(k.dtype), k], axis=1)
    vcat = jnp.concatenate([buf_v.astype(v.dtype), v], axis=1)
    buf_len = buf_k.shape[1]
    outs = [_dil_branch_sample(q, kcat, vcat, rel_bias, win, dil, buf_len) for win, dil in DIL_PAIRS]
    o = _merge_branches(outs).astype(u.dtype).reshape(b, t, DIL_WIDTH)
    return o @ w_out, k, v


def setup_inputs(seed: int = 0) -> dict:
    key = jax.random.key(seed)
    ks = jax.random.split(key, 32)
    f32 = jnp.float32
    n_pages = PAST_LEN // PAGE_SIZE
    n_used = DEC_BATCH * n_pages
    n_pool = n_used + (n_used + 3) // 4
    c_buf = min(DIL_MAX_WINDOW, PAST_LEN)

    def nrm(k, shape, scale):
        return scale * jax.random.normal(k, shape, f32)

    page_table = jax.random.permutation(ks[0], n_pool)[:n_used].reshape(DEC_BATCH, n_pages).astype(jnp.int32)
    a0 = jax.random.uniform(ks[20], (N_EVEN, LRU_WIDTH), f32, 0.9, 0.999)
    s0 = a0 ** (1.0 / RGLRU_C)
    lru_lambda = jnp.log(s0) - jnp.log1p(-s0)
    return {
        'x_prompt': nrm(ks[1], (BATCH, SEQ, D_MODEL), 1.0),
        'x_sample': nrm(ks[2], (DEC_BATCH, DEC_SEQ, D_MODEL), 1.0),
        'cache_sb_k': nrm(ks[3], (N_EVEN, n_pool, PAGE_SIZE, SB_HEADS, HEAD_DIM), 1.0),
        'cache_sb_v': nrm(ks[4], (N_EVEN, n_pool, PAGE_SIZE, SB_HEADS, HEAD_DIM), 1.0),
        'state_conv': nrm(ks[5], (N_EVEN, DEC_BATCH, CONV_W - 1, LRU_WIDTH), 1.0),
        'state_lru': nrm(ks[6], (N_EVEN, DEC_BATCH, LRU_WIDTH), 0.5),
        'cache_dil_k': nrm(ks[7], (N_ODD, DEC_BATCH, c_buf, DIL_HEADS, HEAD_DIM), 1.0),
        'cache_dil_v': nrm(ks[8], (N_ODD, DEC_BATCH, c_buf, DIL_HEADS, HEAD_DIM), 1.0),
        'page_table': page_table,
        'rel_bias': nrm(ks[9], (REL_BUCKETS, DIL_HEADS), 0.5),
        'norm_mix': 1.0 + nrm(ks[10], (DEPTH, D_MODEL), 0.02),
        'norm_ffn': 1.0 + nrm(ks[11], (DEPTH, D_MODEL), 0.02),
        'norm_final': 1.0 + nrm(ks[12], (D_MODEL,), 0.02),
        'w_in_ab': nrm(ks[13], (N_EVEN, D_MODEL, AB_IN_WIDTH), D_MODEL ** -0.5),
        'sb_bias': SB_BIAS_INIT + nrm(ks[26], (N_EVEN, SB_HEADS), 0.5),
        'conv_w': nrm(ks[14], (N_EVEN, CONV_W, LRU_WIDTH), CONV_W ** -0.5),
        'conv_b': nrm(ks[15], (N_EVEN, LRU_WIDTH), 0.01),
        'lru_wa': nrm(ks[16], (N_EVEN, LRU_BLOCKS, LRU_BLOCK_W, LRU_BLOCK_W), LRU_BLOCK_W ** -0.5),
        'lru_ba': nrm(ks[17], (N_EVEN, LRU_WIDTH), 0.01),
        'lru_wx': nrm(ks[18], (N_EVEN, LRU_BLOCKS, LRU_BLOCK_W, LRU_BLOCK_W), LRU_BLOCK_W ** -0.5),
        'lru_bx': nrm(ks[19], (N_EVEN, LRU_WIDTH), 0.01),
        'lru_lambda': lru_lambda,
        'w_out_ab': nrm(ks[21], (N_EVEN, SB_WIDTH + LRU_WIDTH, D_MODEL), (SB_WIDTH + LRU_WIDTH) ** -0.5),
        'w_in_c': nrm(ks[22], (N_ODD, D_MODEL, 3 * DIL_WIDTH), D_MODEL ** -0.5),
        'w_out_c': nrm(ks[23], (N_ODD, DIL_WIDTH, D_MODEL), DIL_WIDTH ** -0.5),
        'w_ff1': nrm(ks[24], (DEPTH, D_MODEL, D_FF), D_MODEL ** -0.5),
        'w_ff2': nrm(ks[25], (DEPTH, D_FF, D_MODEL), D_FF ** -0.5),
    }


def reference(x_prompt, x_sample, cache_sb_k, cache_sb_v, state_conv, state_lru,
              cache_dil_k, cache_dil_v, page_table, rel_bias, norm_mix, norm_ffn,
              norm_final, w_in_ab, sb_bias, conv_w, conv_b, lru_wa, lru_ba, lru_wx, lru_bx,
              lru_lambda, w_out_ab, w_in_c, w_out_c, w_ff1, w_ff2):
    n_pages = page_table.shape[1]
    past_len = n_pages * cache_sb_k.shape[2]
    b_p = x_prompt.shape[0]
    hp, hs = x_prompt, x_sample
    sb_k_p, sb_v_p, sb_k_s, sb_v_s = [], [], [], []
    conv_p, conv_s, lru_p, lru_s = [], [], [], []
    dk_p, dv_p, dk_s, dv_s = [], [], [], []
    for layer in range(DEPTH):
        i = layer // 2
        up = _rms_norm(hp, norm_mix[layer])
        us = _rms_norm(hs, norm_mix[layer])
        if layer % 2 == 0:
            ab = (w_in_ab[i], sb_bias[i], conv_w[i], conv_b[i], lru_wa[i], lru_ba[i],
                  lru_wx[i], lru_bx[i], lru_lambda[i], w_out_ab[i])
            no_past = jnp.zeros((b_p, 0, SB_HEADS, HEAD_DIM), hp.dtype)
            mp, kp_, vp_, cp_, lp_ = _even_mixer(
                up, no_past, no_past,
                jnp.zeros((b_p, CONV_W - 1, LRU_WIDTH), hp.dtype),
                jnp.zeros((b_p, LRU_WIDTH), hp.dtype), 0, *ab)
            ms, ks_, vs_, cs_, ls_ = _even_mixer(
                us, _gather_pages(cache_sb_k[i], page_table),
                _gather_pages(cache_sb_v[i], page_table),
                state_conv[i], state_lru[i], past_len, *ab)
            sb_k_p.append(kp_); sb_v_p.append(vp_); sb_k_s.append(ks_); sb_v_s.append(vs_)
            conv_p.append(cp_); conv_s.append(cs_); lru_p.append(lp_); lru_s.append(ls_)
        else:
            mp, kp_, vp_ = _odd_mixer_prompt(up, w_in_c[i], w_out_c[i], rel_bias)
            ms, ks_, vs_ = _odd_mixer_sample(us, cache_dil_k[i], cache_dil_v[i],
                                             w_in_c[i], w_out_c[i], rel_bias)
            dk_p.append(kp_); dv_p.append(vp_); dk_s.append(ks_); dv_s.append(vs_)
        hp = hp + mp
        hs = hs + ms
        hp = hp + _sq_relu_mlp(_rms_norm(hp, norm_ffn[layer]), w_ff1[layer], w_ff2[layer])
        hs = hs + _sq_relu_mlp(_rms_norm(hs, norm_ffn[layer]), w_ff1[layer], w_ff2[layer])
    y_prompt = _rms_norm(hp, norm_final)
    y_sample = _rms_norm(hs, norm_final)
    sb_k_prompt = jnp.stack(sb_k_p)
    sb_v_prompt = jnp.stack(sb_v_p)
    sb_k_sample = jnp.stack(sb_k_s)
    sb_v_sample = jnp.stack(sb_v_s)
    conv_prompt = jnp.stack(conv_p)
    conv_sample = jnp.stack(conv_s)
    lru_prompt = jnp.stack(lru_p)
    lru_sample = jnp.stack(lru_s)
    dil_k_prompt = jnp.stack(dk_p)
    dil_v_prompt = jnp.stack(dv_p)
    dil_k_sample = jnp.stack(dk_s)
    dil_v_sample = jnp.stack(dv_s)
    return (y_prompt, y_sample, sb_k_prompt, sb_v_prompt, sb_k_sample, sb_v_sample,
            conv_prompt, conv_sample, lru_prompt, lru_sample,
            dil_k_prompt, dil_v_prompt, dil_k_sample, dil_v_sample)
```

```python
import contextlib
import math

import numpy as np
import concourse.bass as bass
import concourse.mybir as mybir
from concourse.bass_utils import run_bass_kernel_spmd

F32 = mybir.dt.float32
BF16 = mybir.dt.bfloat16
I32 = mybir.dt.int32
AF = mybir.ActivationFunctionType
ALU = mybir.AluOpType
AX = mybir.AxisListType

NCORES = 8
HD = 64
NEG = -30000.0
DIL_PAIRS = ((128, 1), (512, 4), (2048, 16))
CBUF = 2048


class Buf:
    __slots__ = ("w", "r", "name", "psum")

    def __init__(self, name="", psum=False):
        self.w = None
        self.r = {}
        self.name = name
        self.psum = psum


class Builder:
    def __init__(self, nc):
        self.nc = nc
        self.streams = {k: [] for k in ("pe", "act", "dve", "pool", "sp")}
        self.eh = {"pe": nc.tensor, "act": nc.scalar, "dve": nc.vector, "pool": nc.gpsimd, "sp": nc.sync}
        self.sems = {}
        self.cnt = {}
        self.seen = {k: {} for k in self.streams}
        self.stack = contextlib.ExitStack()
        self.n_ops = 0
        self.n_alloc = 0
        for k in self.streams:
            if k != "sp":
                self._sem(k)
        self.dbufs = {}

    def _sem(self, key):
        if key not in self.sems:
            self.sems[key] = self.stack.enter_context(self.nc.semaphore("s_" + str(key)))
            self.cnt[key] = 0
        return self.sems[key]

    def sb(self, name, shape, dtype, stack=None):
        self.n_alloc += 1
        t = (stack or self.stack).enter_context(
            self.nc.sbuf_tensor("%s_%d" % (name, self.n_alloc), list(shape), dtype))
        return t

    def ps(self, name, shape, dtype=F32, stack=None):
        self.n_alloc += 1
        t = (stack or self.stack).enter_context(
            self.nc.psum_tensor("%s_%d" % (name, self.n_alloc), list(shape), dtype))
        return t

    def dbuf(self, key):
        b = self.dbufs.get(key)
        if b is None:
            b = self.dbufs[key] = Buf(str(key))
        return b

    def _wait(self, eng, dep):
        key, val = dep
        if self.seen[eng].get(key, 0) >= val:
            return
        self.seen[eng][key] = val
        self.eh[eng].wait_ge(self.sems[key], val)

    def _deps(self, eng, reads, writes, own_key):
        deps = []
        for b in reads:
            if b.w is not None:
                deps.append(b.w)
            if b.psum:
                for k, v in b.r.items():
                    if k != own_key:
                        deps.append((k, v))
        for b in writes:
            if b.w is not None:
                deps.append(b.w)
            for k, v in b.r.items():
                deps.append((k, v))
        return deps

    def op(self, eng, fn, reads=(), writes=(), pe_chain=False):
        deps = self._deps(eng, reads, writes, eng)
        for d in deps:
            if eng == "pe" and d[0] == "pe":
                continue
            self._wait(eng, d)
        self.cnt[eng] += 1
        ev = (eng, self.cnt[eng])
        fn(self.eh[eng]).then_inc(self.sems[eng], 1)
        for b in reads:
            b.r[eng] = ev[1]
        for b in writes:
            b.w = ev
            b.r = {}
        self.n_ops += 1
        return ev

    def dma(self, q, cls, fn, reads=(), writes=()):
        key = ("d", q, cls)
        self._sem(key)
        deps = self._deps(q, reads, writes, None)
        if self.cnt[key] > 0:
            deps.append((key, self.cnt[key]))
        for d in deps:
            self._wait(q, d)
        self.cnt[key] += 16
        ev = (key, self.cnt[key])
        fn(self.eh[q]).then_inc(self.sems[key], 16)
        for b in reads:
            b.r[key] = ev[1]
        for b in writes:
            b.w = ev
            b.r = {}
        self.n_ops += 1
        return ev

    def barrier(self):
        for eng in self.streams:
            for key, c in self.cnt.items():
                if c > 0 and key != eng:
                    self._wait(eng, (key, c))
        for b in self.dbufs.values():
            b.w = None
            b.r = {}

    def finish(self):
        self.barrier()
        self.stack.close()


def _rel_bucket_np(dist):
    dist = np.asarray(dist, np.int64)
    exact = 16
    df = np.maximum(dist.astype(np.float32), np.float32(1.0))
    large = exact + (np.log(df / np.float32(exact)) / np.float32(math.log(2048 / exact))
                     * np.float32(32 - exact)).astype(np.int32)
    large = np.minimum(large, 31)
    return np.where(dist < exact, dist, large).astype(np.int64)


def make_consts(NE=1, NPOOL=1):
    c = {}
    c["iotab"] = (np.arange(128)[:, None] + np.arange(NE)[None, :] * (NPOOL * 128)).astype(np.float32)
    c["ident"] = np.eye(128, dtype=np.float32)
    c["ones"] = np.ones((128, 128), np.float32)
    j = np.arange(128)[:, None]
    s = np.arange(128)[None, :]
    c["utri"] = (j >= s).astype(np.float32)
    c["nutri"] = -c["utri"]
    c["nones"] = -np.ones((128, 128), np.float32)
    c["flip"] = (j == 127 - s).astype(np.float32)
    t = np.arange(512)[None, :]
    c["sbmask"] = np.stack([((128 * jb + np.arange(128)[:, None]) < t).astype(np.float32)
                            for jb in (3, 2, 1, 0)], 0)
    oh = np.zeros((32, 3 * 129), np.float32)
    ohrev = np.zeros((32, 3 * 128), np.float32)
    for br, (win, dil) in enumerate(DIL_PAIRS):
        bk = _rel_bucket_np(np.arange(129) * dil)
        for jj in range(129):
            oh[bk[jj], br * 129 + jj] = 1.0
        for i in range(128):
            ohrev[bk[128 - i], br * 128 + i] = 1.0
    c["oh"] = oh
    c["ohrev"] = ohrev
    ohnew = np.zeros((32, 4 * 3 * 4), np.float32)
    negnew = np.full((4, 4 * 3), NEG, np.float32)
    for tq in range(4):
        for br, (win, dil) in enumerate(DIL_PAIRS):
            for tk in range(4):
                d = tq - tk
                if d >= 0 and d % dil == 0 and d // dil <= win // dil:
                    ohnew[_rel_bucket_np(d), (tq * 3 + br) * 4 + tk] = 1.0
                    negnew[tk, tq * 3 + br] = 0.0
    c["ohnew"] = ohnew
    c["negnew"] = negnew
    c["snew"] = (np.arange(4)[:, None] < np.arange(4)[None, :]).astype(np.float32)
    return c


class Prog:
    def __init__(self, T, NS, NPAGES, NPOOL, DEPTH, stop_after=None):
        self.T, self.NS, self.NPAGES, self.NPOOL, self.DEPTH = T, NS, NPAGES, NPOOL, DEPTH
        self.NTS = NS * 4
        self.TT = T + self.NTS
        self.NE = (DEPTH + 1) // 2
        self.NO = DEPTH // 2
        self.KEEP = min(2048, T)
        self.stop_after = stop_after
        nc = self.nc = bass.Bass("TRN2", target_bir_lowering=False)
        self.b = Builder(nc)
        self.din = {}
        self.dout = {}
        self.declare()

    def _in(self, name, shape, dtype=F32):
        self.din[name] = self.nc.dram_tensor(name, list(shape), dtype, kind="ExternalInput").ap()
        return self.din[name]

    def _out(self, name, shape, dtype=F32):
        self.dout[name] = self.nc.dram_tensor(name, list(shape), dtype, kind="ExternalOutput").ap()
        return self.dout[name]

    def _scr(self, name, shape, dtype=F32):
        return self.nc.dram_tensor(name, list(shape), dtype, kind="Internal").ap()

    def declare(self):
        T, NS, NTS, TT, NE, NO, D = self.T, self.NS, self.NTS, self.TT, self.NE, self.NO, self.DEPTH
        i, o, s = self._in, self._out, self._scr
        i("xp", [2, T, 1024]); i("xs", [NTS, 1024])
        i("poolk", [NE * self.NPOOL * 128, 512]); i("poolv", [NE * self.NPOOL * 128, 512])
        i("pt", [NS, self.NPAGES], I32)
        i("sconv", [NE, NS, 3, 512]); i("slru", [NE, NS, 512])
        i("cdk", [max(NO, 1), NS, CBUF, 1024]); i("cdv", [max(NO, 1), NS, CBUF, 1024])
        i("relb", [32, 16]); i("nmix", [D, 1024]); i("nffn", [D, 1024]); i("nfin", [1, 1024])
        i("w_in_ab", [NE, 1024, 2560]); i("sbb", [NE, 8]); i("convw", [NE, 4, 512]); i("convb", [NE, 512])
        i("wa", [NE, 8, 64, 64]); i("ba", [NE, 512]); i("wx", [NE, 8, 64, 64]); i("bx", [NE, 512])
        i("lam", [NE, 512]); i("w_out_ab", [NE, 1024, 1024])
        i("w_in_c", [max(NO, 1), 1024, 3072]); i("w_out_c", [max(NO, 1), 1024, 1024])
        i("w_ff1", [D, 1024, 4096]); i("w_ff2", [D, 4096, 1024])
        for k, v in make_consts(NE, self.NPOOL).items():
            i("c_" + k, v.shape)
        Q = T // 4
        o("y_p", [Q, 1024]); o("y_s", [NTS, 1024])
        o("sbk_p", [NE, Q, 512]); o("sbv_p", [NE, Q, 512])
        o("sbk_s", [NE, NTS, 512]); o("sbv_s", [NE, NTS, 512])
        o("conv_p", [NE, 3, 512]); o("conv_s", [NE, NS, 3, 512])
        o("lru_p", [NE, 512]); o("lru_s", [NE, NS, 512])
        o("dk_p", [max(NO, 1), self.KEEP // 4, 1024]); o("dv_p", [max(NO, 1), self.KEEP // 4, 1024])
        o("dk_s", [max(NO, 1), NTS, 1024]); o("dv_s", [max(NO, 1), NTS, 1024])
        self.XT = s("XT", [8, 128, TT])
        self.QT = s("QT", [8, 128, TT], BF16)
        self.KT = s("KT", [8, 128, TT], BF16)
        self.V16 = s("V16", [8, TT, 128], BF16)
        self.AT = s("AT", [8, 128, TT], BF16)
        Q4, K4 = T // 4, self.KEEP // 4
        self.SBK4 = s("SBK", [NE, 4, Q4, 512]); self.SBV4 = s("SBV", [NE, 4, Q4, 512])
        self.DKS4 = s("DKS", [max(NO, 1), 4, K4, 1024]); self.DVS4 = s("DVS", [max(NO, 1), 4, K4, 1024])
        self.YP4 = s("YP", [4, Q4, 1024])
        self.SBK = self.SBK4.rearrange("e a r f -> e (a r) f"); self.SBV = self.SBV4.rearrange("e a r f -> e (a r) f")
        self.DKS = self.DKS4.rearrange("e a r f -> e (a r) f"); self.DVS = self.DVS4.rearrange("e a r f -> e (a r) f")
        self.YP = self.YP4.rearrange("a r f -> (a r) f")
        self.ZS = s("ZS", [NTS, 3072])
        self.AS = s("AS", [NTS, 1024])
        self.TAB = s("TAB", [16, 3, 384])
        self.BTD = s("BTD", [3, 16, 2, 128, 128])
        self.WB = {}
        for nm, L, K, N in (("w_in_ab", NE, 1024, 2560), ("w_out_ab", NE, 1024, 1024),
                            ("w_in_c", NO, 1024, 3072), ("w_out_c", NO, 1024, 1024),
                            ("w_ff1", D, 1024, 4096), ("w_ff2", D, 4096, 1024)):
            if L > 0:
                self.WB[nm] = s("WB_" + nm, [L, K, N], BF16)

    def setup(self):
        b, g = self.b, {}
        self.g = g
        self.bg = {}

        def ctile(name, shape, dtype, src, q="sp", **kw):
            t = b.sb(name, shape, dtype)
            bf = Buf(name)
            b.dma(q, "const", lambda e: e.dma_start(out=t[:], in_=src, **kw), writes=[bf])
            g[name] = t
            self.bg[name] = bf
            return t

        d = self.din
        ctile("ident", [128, 128], F32, d["c_ident"])
        ctile("ones32", [128, 128], F32, d["c_ones"])
        ctile("onesb", [128, 128], BF16, d["c_ones"], q="pool")
        ctile("utrib", [128, 128], BF16, d["c_utri"], q="pool")
        ctile("nutrib", [128, 128], BF16, d["c_nutri"], q="pool")
        ctile("nonesb", [128, 128], BF16, d["c_nones"], q="pool")
        ctile("flip", [128, 128], F32, d["c_flip"])
        ctile("sbmask", [128, 4, 512], BF16, d["c_sbmask"].rearrange("j p t -> p j t"), q="pool")
        ctile("snew", [4, 4], F32, d["c_snew"])
        D = self.DEPTH
        for nm, src, L in (("gmix", d["nmix"], D), ("gffn", d["nffn"], D), ("gfin", d["nfin"], 1)):
            t = b.sb(nm, [128, L, 8], F32)
            bf = Buf(nm)
            for l in range(L):
                b.dma("sp", "const", lambda e: e.dma_start(out=t[:, l, :], in_=src[l].rearrange("(kc p) -> p kc", p=128),
                                                           allow_slow_non_contiguous=True), writes=[bf])
            g[nm] = t
            self.bg[nm] = bf
        cc = b.sb("ccol", [128, 4], F32)
        bc = Buf("ccol")
        b.op("dve", lambda e: e.memset(cc[:, 0:1], 1e-6), writes=[bc])
        b.op("dve", lambda e: e.memset(cc[:, 1:2], 1.0), writes=[bc])
        b.op("dve", lambda e: e.memset(cc[:, 2:4], 0.0), writes=[bc])
        g["ccol"] = cc
        self.bg["ccol"] = bc
        self.PS = [b.ps("ps%d" % i, [128, 1024], F32) for i in range(4)]
        self.PSB = [Buf("ps%d" % i, psum=True) for i in range(4)]
        self.pid = self.nc.gpsimd.partition_id()

    def convert_weights(self):
        b = self.b
        NB = 3
        with contextlib.ExitStack() as st:
            t32 = [b.sb("wc32", [128, 4096], F32, st) for _ in range(NB)]
            t16 = [b.sb("wc16", [128, 4096], BF16, st) for _ in range(NB)]
            b32 = [Buf() for _ in range(NB)]
            b16 = [Buf() for _ in range(NB)]
            n = 0
            for nm, wb in self.WB.items():
                src = self.din[nm]
                L, K, N = src.shape
                for l in range(L):
                    for kt in range(K // 128):
                        k = n % NB
                        sl = src[l, kt * 128:(kt + 1) * 128, :]
                        b.dma("sp", "wcv_l%d" % k, lambda e: e.dma_start(out=t32[k][:, 0:N], in_=sl), writes=[b32[k]])
                        eng = ("act", "dve", "pool")[n % 3]
                        if eng == "act":
                            b.op("act", lambda e: e.activation(out=t16[k][:, 0:N], in_=t32[k][:, 0:N], func=AF.Copy),
                                 reads=[b32[k]], writes=[b16[k]])
                        else:
                            b.op(eng, lambda e: e.tensor_copy(out=t16[k][:, 0:N], in_=t32[k][:, 0:N]),
                                 reads=[b32[k]], writes=[b16[k]])
                        b.dma("sp", "wcv_s%d" % k,
                              lambda e: e.dma_start(out=wb[l, kt * 128:(kt + 1) * 128, :], in_=t16[k][:, 0:N]),
                              reads=[b16[k]], writes=[self.b.dbuf(("WB", nm, l))])
                        n += 1
            b.barrier()

    def tiles(self):
        T = self.T
        res = [(i * 512, 512, False) for i in range(T // 512)]
        res.append((T, self.NTS, True))
        return res

    def xt_buf(self, c0):
        return self.b.dbuf(("XT", c0))

    def phase0(self):
        b, g, PS, PSB = self.b, self.g, self.PS, self.PSB
        d = self.din
        with contextlib.ExitStack() as st:
            xtm = b.sb("xtm", [128, 4, 1024], F32, st)
            bx = Buf("xtm")
            xT = b.sb("xT0", [128, 8, 512], F32, st)
            bxT = Buf("xT0")
            pseq = bass.ds(self.pid % 2, 1)
            for (c0, W, samp) in self.tiles():
                if not samp:
                    src = d["xp"][pseq, c0:c0 + W, :].rearrange("o (s p) f -> p (o s) f", p=128)
                    b.dma("pool", "x0", lambda e: e.dma_start(out=xtm[:], in_=src), writes=[bx])
                    subs = [(s, 128) for s in range(4)]
                else:
                    b.dma("pool", "x0", lambda e: e.dma_start(out=xtm[0:W, 0, :], in_=d["xs"]), writes=[bx])
                    subs = [(0, W)]
                for kc in range(8):
                    pz, pb = PS[kc % 4], PSB[kc % 4]
                    for (s, rows) in subs:
                        b.op("pe", lambda e: e.transpose(out=pz[:, s * 128:s * 128 + rows],
                                                         in_=xtm[0:rows, s, kc * 128:(kc + 1) * 128],
                                                         identity=g["ident"][0:rows, 0:rows]),
                             reads=[bx, self.bg["ident"]], writes=[pb])
                    eng = "dve" if kc % 2 == 0 else "act"
                    if eng == "dve":
                        b.op("dve", lambda e: e.tensor_copy(out=xT[:, kc, 0:W], in_=pz[:, 0:W]), reads=[pb], writes=[bxT])
                    else:
                        b.op("act", lambda e: e.activation(out=xT[:, kc, 0:W], in_=pz[:, 0:W], func=AF.Copy),
                             reads=[pb], writes=[bxT])
                b.dma("sp", "xst", lambda e: e.dma_start(
                    out=self.XT[:, :, c0:c0 + W].rearrange("k p t -> p k t"), in_=xT[:, :, 0:W]),
                    reads=[bxT], writes=[self.xt_buf(c0)])
            b.barrier()

    def norm(self, xT, bxT, W, gain, uT, buT, sq, bsq, rs, brs, ps, pb):
        b, g = self.b, self.g
        b.op("act", lambda e: e.activation(out=sq[:, :, 0:W], in_=xT[:, :, 0:W], func=AF.Square),
             reads=[bxT], writes=[bsq])
        for kc in range(8):
            b.op("pe", lambda e: e.matmul(out=ps[:, 0:W], lhsT=g["onesb"][:], rhs=sq[:, kc, 0:W],
                                          start=(kc == 0), stop=(kc == 7)),
                 reads=[bsq, self.bg["onesb"]], writes=[pb])
        b.op("act", lambda e: e.activation(out=rs[:, 0:W], in_=ps[:, 0:W], func=AF.Ln,
                                           bias=g["ccol"][:, 0:1], scale=1.0 / 1024.0),
             reads=[pb, self.bg["ccol"]], writes=[brs])
        b.op("act", lambda e: e.activation(out=rs[:, 0:W], in_=rs[:, 0:W], func=AF.Exp, scale=-0.5),
             reads=[brs], writes=[brs])
        for kc in range(8):
            b.op("dve", lambda e: e.scalar_tensor_tensor(out=uT[:, kc, 0:W], in0=xT[:, kc, 0:W],
                                                         scalar=gain[:, kc:kc + 1], in1=rs[:, 0:W],
                                                         op0=ALU.mult, op1=ALU.mult),
                 reads=[bxT, brs, self.bg["gmix"], self.bg["gffn"], self.bg["gfin"]], writes=[buT])

    def wload(self, wt, bw, wb2d, k0, nk, c0, nc_, cls):
        src = wb2d[k0:k0 + nk * 128, c0:c0 + nc_].rearrange("(kc p) n -> p kc n", p=128)
        self.b.dma("sp", cls, lambda e: e.dma_start(out=wt[:, 0:nk, 0:nc_], in_=src), writes=[bw])

    def p1_even(self, li):
        b, g, PS, PSB, d = self.b, self.g, self.PS, self.PSB, self.din
        ie = li // 2
        T, NS, NTS = self.T, self.NS, self.NTS
        wb = self.WB["w_in_ab"][ie]
        with contextlib.ExitStack() as st:
            sb = lambda n, s_, dt: b.sb(n, s_, dt, st)
            xT = sb("xT", [128, 8, 512], F32); bxT = Buf()
            sq = sb("sq", [128, 8, 512], BF16); bsq = Buf()
            rs = sb("rs", [128, 512], F32); brs = Buf()
            uT = sb("uT", [128, 8, 512], BF16); buT = Buf()
            wts = [sb("wt", [128, 8, 512], BF16) for _ in range(4)]; bws = [Buf() for _ in range(4)]
            st16 = [sb("st16", [128, 4, 512], BF16) for _ in range(2)]; bst16 = [Buf(), Buf()]
            tm32 = [sb("tm32", [128, 512], F32) for _ in range(2)]; btm32 = [Buf(), Buf()]
            tm16 = [sb("tm16", [128, 512], BF16) for _ in range(2)]; btm16 = [Buf(), Buf()]
            XR = sb("XR", [128, 4, 515], F32); bXR = Buf()
            XRs = sb("XRs", [128, 4, NS, 7], F32); bXRs = Buf()
            gb = sb("gb", [128, 4, 512], F32); bgb = Buf()
            hst = sb("hst", [128, 4], F32); bhst = Buf()
            hss = sb("hss", [128, 4, NS], F32); bhss = Buf()
            xc = sb("xc", [128, 512], F32); bxc = Buf()
            xc16 = sb("xc16", [128, 512], BF16); bxc16 = Buf()
            tA = sb("tA", [128, 512], F32); btA = Buf()
            tB = sb("tB", [128, 512], F32); btB = Buf()
            tC = sb("tC", [128, 512], F32); btC = Buf()
            tD = sb("tD", [128, 512], F32); btD = Buf()
            hh = sb("hh", [128, 512], F32); bhh = Buf()
            lt16 = sb("lt16", [128, 4, 512], BF16); blt = Buf()
            prm = sb("prm", [128, 4, 12], F32); bprm = Buf()
            BD = sb("BD", [128, 2, 4, 128], BF16); bBD = Buf()
            fm = lambda ap2: ap2.rearrange("(j p) -> p j", p=128)
            b.op("pool", lambda e: e.memset(BD[:], 0.0), writes=[bBD])
            for k in range(4):
                b.dma("sp", "prm", lambda e: e.dma_start(out=prm[:, :, k], in_=fm(d["convw"][ie, k]),
                                                         allow_slow_non_contiguous=True), writes=[bprm])
            for k, nm in ((4, "convb"), (5, "ba"), (6, "bx"), (7, "lam")):
                b.dma("sp", "prm", lambda e: e.dma_start(out=prm[:, :, k], in_=fm(d[nm][ie]),
                                                         allow_slow_non_contiguous=True), writes=[bprm])
            for wi, nm in enumerate(("wa", "wx")):
                for n in range(8):
                    j, h = n // 2, n % 2
                    b.dma("pool", "bd", lambda e: e.dma_start(
                        out=BD[h * 64:(h + 1) * 64, wi, j, h * 64:(h + 1) * 64], in_=d[nm][ie, n]),
                        writes=[bBD])
            b.op("act", lambda e: e.activation(out=prm[:, :, 8], in_=prm[:, :, 7], func=AF.Exp, scale=-1.0),
                 reads=[bprm], writes=[bprm])
            b.op("act", lambda e: e.activation(out=prm[:, :, 8], in_=prm[:, :, 8], func=AF.Ln,
                                               bias=g["ccol"][:, 1:2], scale=1.0),
                 reads=[bprm, self.bg["ccol"]], writes=[bprm])
            b.op("dve", lambda e: e.tensor_scalar(out=prm[:, :, 8], in0=prm[:, :, 8], scalar1=-8.0, scalar2=None,
                                                  op0=ALU.mult), reads=[bprm], writes=[bprm])
            b.op("dve", lambda e: e.memset(XR[:, :, 0:3], 0.0), writes=[bXR])
            b.op("dve", lambda e: e.memset(hst[:], 0.0), writes=[bhst])
            for j in range(4):
                for sg in range(NS):
                    b.dma("sp", "prm", lambda e: e.dma_start(
                        out=XRs[:, j, sg, 0:3], in_=d["sconv"][ie, sg, :, j * 128:(j + 1) * 128].rearrange("r p -> p r"),
                        allow_slow_non_contiguous=True), writes=[bXRs])
                b.dma("sp", "prm", lambda e: e.dma_start(
                    out=hss[:, j, :], in_=d["slru"][ie][:, j * 128:(j + 1) * 128].rearrange("s p -> p s"),
                    allow_slow_non_contiguous=True), writes=[bhss])
            nw = 0
            tl = self.tiles()
            import os
            DBG = int(os.environ.get("P1STOP", "99"))
            SKIP = int(os.environ.get("P1SKIP", "0"))
            if DBG <= 1:
                b.barrier(); return
            for ti, (c0, W, samp) in enumerate(tl):
                b.dma("sp", "xld", lambda e: e.dma_start(
                    out=xT[:, :, 0:W], in_=self.XT[:, :, c0:c0 + W].rearrange("k p t -> p k t")),
                    reads=[self.xt_buf(c0)], writes=[bxT])
                self.norm(xT, bxT, W, g["gmix"][:, li, :], uT, buT, sq, bsq, rs, brs, PS[0], PSB[0])
                if DBG <= 2:
                    continue
                subs = [(0, W)] if samp else [(s * 128, 128) for s in range(4)]
                for grp in range(5):
                    wt, bw = wts[nw % 4], bws[nw % 4]
                    self.wload(wt, bw, wb, 0, 8, grp * 512, 512, "w%d" % (nw % 4))
                    nw += 1
                    if (SKIP & 8) and samp:
                        continue
                    if grp in (0, 1, 3, 4) and not (SKIP & 1):
                        so, bso = st16[grp % 2], bst16[grp % 2]
                        for j in range(4):
                            pz, pb = PS[1 + (j % 2)], PSB[1 + (j % 2)]
                            for kc in range(8):
                                b.op("pe", lambda e: e.matmul(out=pz[:, 0:W], lhsT=wt[:, kc, j * 128:(j + 1) * 128],
                                                              rhs=uT[:, kc, 0:W], start=(kc == 0), stop=(kc == 7)),
                                     reads=[bw, buT], writes=[pb])
                            if grp == 0:
                                b.op("act", lambda e: e.activation(out=so[:, j, 0:W], in_=pz[:, 0:W],
                                                                   func=AF.Copy, scale=0.125),
                                     reads=[pb], writes=[bso])
                            elif grp == 1:
                                b.op("act", lambda e: e.activation(out=so[:, j, 0:W], in_=pz[:, 0:W], func=AF.Copy),
                                     reads=[pb], writes=[bso])
                            elif grp == 3:
                                if not samp:
                                    b.op("dve", lambda e: e.tensor_copy(out=XR[:, j, 3:3 + W], in_=pz[:, 0:W]),
                                         reads=[pb], writes=[bXR])
                                else:
                                    b.op("dve", lambda e: e.tensor_copy(
                                        out=XRs[:, j, :, 3:7], in_=pz[:, 0:W].rearrange("p (s t) -> p s t", t=4)),
                                        reads=[pb], writes=[bXRs])
                            else:
                                b.op("act", lambda e: e.activation(out=gb[:, j, 0:W], in_=pz[:, 0:W], func=AF.Copy),
                                     reads=[pb], writes=[bgb])
                        if grp in (0, 1):
                            dst = self.QT if grp == 0 else self.KT
                            b.dma("sp", "qk%d" % grp, lambda e: e.dma_start(
                                out=dst[0:4, :, c0:c0 + W].rearrange("k p t -> p k t"), in_=so[:, :, 0:W]),
                                reads=[bso], writes=[b.dbuf(("QT" if grp == 0 else "KT", c0))])
                    if (grp in (1, 2) or (samp and grp == 0)) and not (SKIP & 2):
                        for si, (s0, rows) in enumerate(subs):
                            pz, pb = PS[3], PSB[3]
                            for kc in range(8):
                                b.op("pe", lambda e: e.matmul(out=pz[0:rows, 0:512], lhsT=uT[:, kc, s0:s0 + rows],
                                                              rhs=wt[:, kc, :], start=(kc == 0), stop=(kc == 7)),
                                     reads=[bw, buT], writes=[pb])
                            t32, bt32 = tm32[si % 2], btm32[si % 2]
                            b.op("dve", lambda e: e.tensor_copy(out=t32[0:rows, :], in_=pz[0:rows, 0:512]),
                                 reads=[pb], writes=[bt32])
                            if samp:
                                if grp == 0:
                                    dstap, dkey = self.ZS[:, 0:512], ("ZS", 0)
                                else:
                                    nm = "sbk_s" if grp == 1 else "sbv_s"
                                    dstap, dkey = self.dout[nm][ie], (nm, ie)
                            else:
                                dstap = (self.SBK if grp == 1 else self.SBV)[ie, c0 + s0:c0 + s0 + rows, :]
                                dkey = ("SBK" if grp == 1 else "SBV", ie, c0)
                            b.dma("sp", "tm%d" % (si % 2), lambda e: e.dma_start(out=dstap, in_=t32[0:rows, :]),
                                  reads=[bt32], writes=[b.dbuf(dkey)])
                            if grp == 2 and not samp and not (SKIP & 4):
                                t16, bt16 = tm16[si % 2], btm16[si % 2]
                                b.op("act", lambda e: e.activation(out=t16[0:rows, :], in_=pz[0:rows, 0:512],
                                                                   func=AF.Copy), reads=[pb], writes=[bt16])
                                for hp in range(4):
                                    b.dma("sp", "tv%d" % (si % 2), lambda e: e.dma_start(
                                        out=self.V16[hp, c0 + s0:c0 + s0 + rows, :],
                                        in_=t16[0:rows, hp * 128:(hp + 1) * 128]),
                                        reads=[bt16], writes=[b.dbuf(("V16", c0))])
                if DBG <= 3:
                    continue
                nseg, L = (NS, 4) if samp else (1, 512)
                for j in range(4):
                    Xv = XRs[:, j, :, :] if samp else XR[:, j, :].rearrange("p (s c) -> p s c", s=1)
                    bX = bXRs if samp else bXR
                    xc3 = xc[:, 0:W].rearrange("p (s t) -> p s t", s=nseg)
                    b.op("dve", lambda e: e.tensor_scalar(out=xc3, in0=Xv[:, :, 0:L], scalar1=prm[:, j, 0:1],
                                                          scalar2=prm[:, j, 4:5], op0=ALU.mult, op1=ALU.add),
                         reads=[bX, bprm], writes=[bxc])
                    for k in range(1, 4):
                        b.op("dve", lambda e: e.scalar_tensor_tensor(out=xc3, in0=Xv[:, :, k:k + L],
                                                                     scalar=prm[:, j, k:k + 1], in1=xc3,
                                                                     op0=ALU.mult, op1=ALU.add),
                             reads=[bX, bprm, bxc], writes=[bxc])
                    b.op("act", lambda e: e.activation(out=xc16[:, 0:W], in_=xc[:, 0:W], func=AF.Copy),
                         reads=[bxc], writes=[bxc16])
                    pz, pb = PS[1 + (j % 2)], PSB[1 + (j % 2)]
                    for wi in range(2):
                        b.op("pe", lambda e: e.matmul(out=pz[:, wi * 512:wi * 512 + W], lhsT=BD[:, wi, j, :],
                                                      rhs=xc16[:, 0:W], start=True, stop=True),
                             reads=[bBD, bxc16], writes=[pb])
                    b.op("act", lambda e: e.activation(out=tA[:, 0:W], in_=pz[:, 0:W], func=AF.Sigmoid,
                                                       bias=prm[:, j, 5:6], scale=1.0), reads=[pb, bprm], writes=[btA])
                    b.op("act", lambda e: e.activation(out=tB[:, 0:W], in_=pz[:, 512:512 + W], func=AF.Sigmoid,
                                                       bias=prm[:, j, 6:7], scale=1.0), reads=[pb, bprm], writes=[btB])
                    b.op("act", lambda e: e.activation(out=tA[:, 0:W], in_=tA[:, 0:W], func=AF.Exp,
                                                       scale=prm[:, j, 8:9]), reads=[btA, bprm], writes=[btA])
                    b.op("dve", lambda e: e.tensor_tensor(out=tC[:, 0:W], in0=tA[:, 0:W], in1=tA[:, 0:W], op=ALU.mult),
                         reads=[btA], writes=[btC])
                    b.op("dve", lambda e: e.tensor_scalar(out=tC[:, 0:W], in0=tC[:, 0:W], scalar1=-1.0, scalar2=1.0,
                                                          op0=ALU.mult, op1=ALU.add), reads=[btC], writes=[btC])
                    b.op("dve", lambda e: e.tensor_scalar(out=tC[:, 0:W], in0=tC[:, 0:W], scalar1=1e-30, scalar2=None,
                                                          op0=ALU.max), reads=[btC], writes=[btC])
                    b.op("act", lambda e: e.activation(out=tC[:, 0:W], in_=tC[:, 0:W], func=AF.Sqrt),
                         reads=[btC], writes=[btC])
                    b.op("dve", lambda e: e.tensor_tensor(out=tB[:, 0:W], in0=tB[:, 0:W], in1=xc[:, 0:W], op=ALU.mult),
                         reads=[btB, bxc], writes=[btB])
                    b.op("dve", lambda e: e.tensor_tensor(out=tB[:, 0:W], in0=tB[:, 0:W], in1=tC[:, 0:W], op=ALU.mult),
                         reads=[btB, btC], writes=[btB])
                    for sg in range(nseg):
                        init = hss[:, j, sg:sg + 1] if samp else hst[:, j:j + 1]
                        bi_ = bhss if samp else bhst
                        b.op("dve", lambda e: e.tensor_tensor_scan(out=hh[:, sg * L:(sg + 1) * L],
                                                                   data0=tA[:, sg * L:(sg + 1) * L],
                                                                   data1=tB[:, sg * L:(sg + 1) * L],
                                                                   initial=init, op0=ALU.mult, op1=ALU.add),
                             reads=[btA, btB, bi_], writes=[bhh])
                        b.op("dve", lambda e: e.tensor_copy(out=init, in_=hh[:, (sg + 1) * L - 1:(sg + 1) * L]),
                             reads=[bhh], writes=[bi_])
                    gj = gb[:, j, 0:W]
                    b.op("pool", lambda e: e.tensor_tensor(out=tD[:, 0:W], in0=gj, in1=gj, op=ALU.mult),
                         reads=[bgb], writes=[btD])
                    b.op("pool", lambda e: e.tensor_scalar(out=tD[:, 0:W], in0=tD[:, 0:W], scalar1=0.044715 * 0.7978845608028654,
                                                           scalar2=0.7978845608028654, op0=ALU.mult, op1=ALU.add),
                         reads=[btD], writes=[btD])
                    b.op("pool", lambda e: e.tensor_tensor(out=tD[:, 0:W], in0=tD[:, 0:W], in1=gj, op=ALU.mult),
                         reads=[btD, bgb], writes=[btD])
                    b.op("act", lambda e: e.activation(out=tD[:, 0:W], in_=tD[:, 0:W], func=AF.Tanh),
                         reads=[btD], writes=[btD])
                    b.op("pool", lambda e: e.tensor_scalar(out=tD[:, 0:W], in0=tD[:, 0:W], scalar1=1.0, scalar2=0.5,
                                                           op0=ALU.add, op1=ALU.mult), reads=[btD], writes=[btD])
                    b.op("pool", lambda e: e.tensor_tensor(out=tD[:, 0:W], in0=tD[:, 0:W], in1=gj, op=ALU.mult),
                         reads=[btD, bgb], writes=[btD])
                    b.op("dve", lambda e: e.tensor_tensor(out=lt16[:, j, 0:W], in0=tD[:, 0:W], in1=hh[:, 0:W], op=ALU.mult),
                         reads=[btD, bhh], writes=[blt])
                    if not samp:
                        b.op("dve", lambda e: e.tensor_copy(out=XR[:, j, 0:3], in_=XR[:, j, 512:515]),
                             reads=[bXR], writes=[bXR])
                b.dma("sp", "lt", lambda e: e.dma_start(
                    out=self.AT[4:8, :, c0:c0 + W].rearrange("k p t -> p k t"), in_=lt16[:, :, 0:W]),
                    reads=[blt], writes=[b.dbuf(("ATl", c0))])
                if not samp and ti == len(tl) - 2:
                    for j in range(4):
                        b.dma("sp", "so", lambda e: e.dma_start(
                            out=self.dout["conv_p"][ie][:, j * 128:(j + 1) * 128].rearrange("r p -> p r"),
                            in_=XR[:, j, 0:3], allow_slow_non_contiguous=True),
                            reads=[bXR], writes=[b.dbuf(("conv_p", ie))])
                    b.dma("sp", "so", lambda e: e.dma_start(
                        out=self.dout["lru_p"][ie].rearrange("(j p) -> p j", p=128), in_=hst[:],
                        allow_slow_non_contiguous=True), reads=[bhst], writes=[b.dbuf(("lru_p", ie))])
                if samp:
                    for j in range(4):
                        for sg in range(NS):
                            b.dma("sp", "so", lambda e: e.dma_start(
                                out=self.dout["conv_s"][ie, sg, :, j * 128:(j + 1) * 128].rearrange("r p -> p r"),
                                in_=XRs[:, j, sg, 4:7], allow_slow_non_contiguous=True),
                                reads=[bXRs], writes=[b.dbuf(("conv_s", ie))])
                        b.dma("sp", "so", lambda e: e.dma_start(
                            out=self.dout["lru_s"][ie][:, j * 128:(j + 1) * 128].rearrange("s p -> p s"),
                            in_=hss[:, j, :], allow_slow_non_contiguous=True),
                            reads=[bhss], writes=[b.dbuf(("lru_s", ie))])
            b.barrier()

    def export(self):
        b = self.b
        Q = self.T // 4
        q = self.nc.sync.partition_id() // 2
        qs = bass.ds(q, 1)
        KQ = self.KEEP // 4
        done = self.done

        def chunks(n, step=512):
            return [(o, min(step, n - o)) for o in range(0, n, step)]

        for ie in range(self.NE):
            if ("p1", 2 * ie) not in done:
                continue
            for nm, src in (("sbk_p", self.SBK4), ("sbv_p", self.SBV4)):
                for (o, n) in chunks(Q):
                    b.dma("sp", "exp", lambda e: e.dma_start(out=self.dout[nm][ie:ie + 1, o:o + n, :],
                                                               in_=src[ie, qs, o:o + n, :]),
                          reads=[], writes=[b.dbuf(("o", nm, ie))])
        for io in range(self.NO):
            if ("p1", 2 * io + 1) not in done:
                continue
            for nm, src in (("dk_p", self.DKS4), ("dv_p", self.DVS4)):
                for (o, n) in chunks(KQ, 256):
                    b.dma("sp", "exp", lambda e: e.dma_start(out=self.dout[nm][io:io + 1, o:o + n, :],
                                                               in_=src[io, qs, o:o + n, :]),
                          reads=[], writes=[b.dbuf(("o", nm, io))])
        if ("p3", self.DEPTH - 1) in done:
            for (o, n) in chunks(Q, 256):
                b.dma("sp", "exp", lambda e: e.dma_start(out=self.dout["y_p"][o:o + n, :].unsqueeze(0),
                                                           in_=self.YP4[qs, o:o + n, :]),
                      reads=[], writes=[b.dbuf(("o", "y_p"))])

    def build(self):
        self.done = set()
        self.setup()
        if self.stop_after != "setup":
            self.convert_weights()
        if self.stop_after not in ("setup", "wcv"):
            self.phase0()
            if self.NO > 0:
                self.bias_tables()
        done = self.stop_after in ("setup", "wcv", "p0")
        for li in range(0 if not done else self.DEPTH, self.DEPTH):
            for ph in ("p1", "p2", "p3"):
                name = "%s_%d" % (ph, li)
                fn = getattr(self, "%s_%s" % (ph, "even" if li % 2 == 0 else "odd"), None)
                if fn is not None:
                    fn(li)
                    self.done.add((ph, li))
                if self.stop_after == name:
                    done = True
                    break
            if done:
                break
        self.b.barrier()
        self.export()
        self.b.finish()
        return self.nc


_PROG_CACHE = {}


def run(inputs, stop_after=None):
    f32 = lambda a: np.ascontiguousarray(np.asarray(a, dtype=np.float32))
    xpr = f32(inputs["x_prompt"])
    B, T, DM = xpr.shape
    xs = f32(inputs["x_sample"])
    DB = xs.shape[0]
    NS = DB // NCORES
    ptab = np.ascontiguousarray(np.asarray(inputs["page_table"], dtype=np.int32))
    NPAGES = ptab.shape[1]
    ck = f32(inputs["cache_sb_k"])
    NE, NPOOL = ck.shape[0], ck.shape[1]
    DEPTH = int(np.asarray(inputs["norm_mix"]).shape[0])
    NO = DEPTH // 2
    key = (T, NS, NPAGES, NPOOL, DEPTH, stop_after)
    if key not in _PROG_CACHE:
        _PROG_CACHE[key] = Prog(T, NS, NPAGES, NPOOL, DEPTH, stop_after)
        _PROG_CACHE[key].build()
    prog = _PROG_CACHE[key]
    rep = {
        "xp": xpr,
        "poolk": ck.reshape(NE * NPOOL * 128, 512),
        "poolv": f32(inputs["cache_sb_v"]).reshape(NE * NPOOL * 128, 512),
        "relb": f32(inputs["rel_bias"]), "nmix": f32(inputs["norm_mix"]), "nffn": f32(inputs["norm_ffn"]),
        "nfin": f32(inputs["norm_final"]).reshape(1, 1024),
        "w_in_ab": f32(inputs["w_in_ab"]), "sbb": f32(inputs["sb_bias"]), "convw": f32(inputs["conv_w"]),
        "convb": f32(inputs["conv_b"]), "wa": f32(inputs["lru_wa"]), "ba": f32(inputs["lru_ba"]),
        "wx": f32(inputs["lru_wx"]), "bx": f32(inputs["lru_bx"]), "lam": f32(inputs["lru_lambda"]),
        "w_out_ab": f32(inputs["w_out_ab"]), "w_in_c": f32(inputs["w_in_c"]), "w_out_c": f32(inputs["w_out_c"]),
        "w_ff1": f32(inputs["w_ff1"]), "w_ff2": f32(inputs["w_ff2"]),
    }
    for k, v in make_consts(NE, NPOOL).items():
        rep["c_" + k] = v
    cdk = f32(inputs["cache_dil_k"]).reshape(max(NO, 1), DB, CBUF, 1024)
    cdv = f32(inputs["cache_dil_v"]).reshape(max(NO, 1), DB, CBUF, 1024)
    sconv = f32(inputs["state_conv"])
    slru = f32(inputs["state_lru"])
    in_maps = []
    for c in range(NCORES):
        sl = slice(c * NS, (c + 1) * NS)
        m = dict(rep)
        m["xs"] = np.ascontiguousarray(xs[sl].reshape(NS * 4, 1024))
        m["pt"] = np.ascontiguousarray(ptab[sl])
        m["sconv"] = np.ascontiguousarray(sconv[:, sl])
        m["slru"] = np.ascontiguousarray(slru[:, sl])
        m["cdk"] = np.ascontiguousarray(cdk[:, sl])
        m["cdv"] = np.ascontiguousarray(cdv[:, sl])
        in_maps.append(m)
    res = run_bass_kernel_spmd(prog.nc, in_maps, core_ids=list(range(NCORES))).results
    Q = T // 4
    KEEP = min(2048, T)

    def prompt(nm, lead, rows, feat):
        out = np.zeros((lead, B, rows, feat), np.float32)
        qr = rows // 4
        for c in range(NCORES):
            bq, qq = c % 2, c // 2
            r = res[c][nm].reshape(lead, qr, feat)
            out[:, bq, qq * qr:(qq + 1) * qr] = r
        return out

    def sample(nm, shape_tail, lead=None):
        parts = [res[c][nm] for c in range(NCORES)]
        if lead is None:
            return np.concatenate([p.reshape((NS,) + shape_tail) for p in parts], 0)
        return np.concatenate([p.reshape((lead, NS) + shape_tail) for p in parts], 1)

    y_p = prompt("y_p", 1, T, 1024)[0]
    y_s = sample("y_s", (4, 1024))
    sbk_p = prompt("sbk_p", NE, T, 512).reshape(NE, B, T, 8, 64)
    sbv_p = prompt("sbv_p", NE, T, 512).reshape(NE, B, T, 8, 64)
    sbk_s = sample("sbk_s", (4, 8, 64), NE)
    sbv_s = sample("sbv_s", (4, 8, 64), NE)
    conv_p = np.stack([res[bq]["conv_p"] for bq in range(B)], 1)
    conv_s = sample("conv_s", (3, 512), NE)
    lru_p = np.stack([res[bq]["lru_p"] for bq in range(B)], 1)
    lru_s = sample("lru_s", (512,), NE)
    dk_p = prompt("dk_p", max(NO, 1), KEEP, 1024)[:NO].reshape(NO, B, KEEP, 16, 64)
    dv_p = prompt("dv_p", max(NO, 1), KEEP, 1024)[:NO].reshape(NO, B, KEEP, 16, 64)
    dk_s = sample("dk_s", (4, 16, 64), max(NO, 1))[:NO]
    dv_s = sample("dv_s", (4, 16, 64), max(NO, 1))[:NO]
    return (y_p, y_s, sbk_p, sbv_p, sbk_s, sbv_s, conv_p, conv_s, lru_p, lru_s, dk_p, dv_p, dk_s, dv_s)


def kernel(**inputs):
    return run(inputs)


def _p3(self, li):
    b, g, PS, PSB = self.b, self.g, self.PS, self.PSB
    even = li % 2 == 0
    i2 = li // 2
    wo = self.WB["w_out_ab" if even else "w_out_c"][i2]
    w1 = self.WB["w_ff1"][li]
    w2 = self.WB["w_ff2"][li]
    last = li == self.DEPTH - 1
    with contextlib.ExitStack() as st:
        sb = lambda n, s_, dt: b.sb(n, s_, dt, st)
        xTs = [sb("xT", [128, 8, 512], F32) for _ in range(2)]; bxTs = [Buf(), Buf()]
        cTs = [sb("cT", [128, 8, 512], BF16) for _ in range(2)]; bcTs = [Buf(), Buf()]
        sq = sb("sq", [128, 8, 512], BF16); bsq = Buf()
        rs = sb("rs", [128, 512], F32); brs = Buf()
        uT = sb("uT", [128, 8, 512], BF16); buT = Buf()
        hT = sb("hT", [128, 32, 512], BF16); bhT = Buf()
        wts = [sb("wt", [128, 8, 512], BF16) for _ in range(4)]; bws = [Buf() for _ in range(4)]
        sqv = [sb("sqv", [128, 512], F32) for _ in range(2)]; bsqv = [Buf(), Buf()]
        if last:
            yT = sb("yT", [128, 8, 512], F32); byT = Buf()
            ytm = [sb("ytm", [128, 1024], F32) for _ in range(2)]; bytm = [Buf(), Buf()]
        nw = 0
        tl3 = self.tiles()

        def loads(ti_):
            c0_, W_, _ = tl3[ti_]
            xT_, bxT_, cT_, bcT_ = xTs[ti_ % 2], bxTs[ti_ % 2], cTs[ti_ % 2], bcTs[ti_ % 2]
            b.dma("sp", "xld%d" % (ti_ % 2), lambda e: e.dma_start(
                out=xT_[:, :, 0:W_], in_=self.XT[:, :, c0_:c0_ + W_].rearrange("k p t -> p k t")),
                reads=[self.xt_buf(c0_)], writes=[bxT_])
            b.dma("sp", "cld%d" % (ti_ % 2), lambda e: e.dma_start(
                out=cT_[:, :, 0:W_], in_=self.AT[:, :, c0_:c0_ + W_].rearrange("k p t -> p k t")),
                reads=[b.dbuf(("ATl", c0_)), b.dbuf(("ATa", c0_))], writes=[bcT_])

        loads(0)
        for ti3, (c0, W, samp) in enumerate(tl3):
            xT, bxT, cT, bcT = xTs[ti3 % 2], bxTs[ti3 % 2], cTs[ti3 % 2], bcTs[ti3 % 2]
            if ti3 + 1 < len(tl3):
                loads(ti3 + 1)
            for grp in range(2):
                wt, bw = wts[nw % 4], bws[nw % 4]
                self.wload(wt, bw, wo, 0, 8, grp * 512, 512, "w%d" % (nw % 4)); nw += 1
                for j in range(4):
                    n = grp * 4 + j
                    pz, pb = PS[j % 4], PSB[j % 4]
                    for kc in range(8):
                        b.op("pe", lambda e: e.matmul(out=pz[:, 0:W], lhsT=wt[:, kc, j * 128:(j + 1) * 128],
                                                      rhs=cT[:, kc, 0:W], start=(kc == 0), stop=(kc == 7)),
                             reads=[bw, bcT], writes=[pb])
                    b.op("dve", lambda e: e.tensor_tensor(out=xT[:, n, 0:W], in0=pz[:, 0:W], in1=xT[:, n, 0:W], op=ALU.add),
                         reads=[pb, bxT], writes=[bxT])
            self.norm(xT, bxT, W, g["gffn"][:, li, :], uT, buT, sq, bsq, rs, brs, PS[0], PSB[0])
            for grp in range(8):
                wt, bw = wts[nw % 4], bws[nw % 4]
                self.wload(wt, bw, w1, 0, 8, grp * 512, 512, "w%d" % (nw % 4)); nw += 1
                for j in range(4):
                    n = grp * 4 + j
                    pz, pb = PS[j % 4], PSB[j % 4]
                    for kc in range(8):
                        b.op("pe", lambda e: e.matmul(out=pz[:, 0:W], lhsT=wt[:, kc, j * 128:(j + 1) * 128],
                                                      rhs=uT[:, kc, 0:W], start=(kc == 0), stop=(kc == 7)),
                             reads=[bw, buT], writes=[pb])
                    sv, bsv = sqv[j % 2], bsqv[j % 2]
                    b.op("act", lambda e: e.activation(out=sv[:, 0:W], in_=pz[:, 0:W], func=AF.Square),
                         reads=[pb], writes=[bsv])
                    b.op("dve", lambda e: e.scalar_tensor_tensor(out=hT[:, n, 0:W], in0=pz[:, 0:W], scalar=0.0,
                                                                 in1=sv[:, 0:W], op0=ALU.is_gt, op1=ALU.mult),
                         reads=[pb, bsv], writes=[bhT])
            for cg in range(2):
                for kq in range(4):
                    wt, bw = wts[nw % 4], bws[nw % 4]
                    self.wload(wt, bw, w2, kq * 1024, 8, cg * 512, 512, "w%d" % (nw % 4)); nw += 1
                    for j in range(4):
                        for kc in range(8):
                            b.op("pe", lambda e: e.matmul(out=PS[j][:, 0:W], lhsT=wt[:, kc, j * 128:(j + 1) * 128],
                                                          rhs=hT[:, kq * 8 + kc, 0:W],
                                                          start=(kq == 0 and kc == 0), stop=(kq == 3 and kc == 7)),
                                 reads=[bw, bhT], writes=[PSB[j]])
                for j in range(4):
                    n = cg * 4 + j
                    b.op("dve", lambda e: e.tensor_tensor(out=xT[:, n, 0:W], in0=PS[j][:, 0:W], in1=xT[:, n, 0:W], op=ALU.add),
                         reads=[PSB[j], bxT], writes=[bxT])
            if not last:
                b.dma("sp", "xst%d" % (ti3 % 2), lambda e: e.dma_start(
                    out=self.XT[:, :, c0:c0 + W].rearrange("k p t -> p k t"), in_=xT[:, :, 0:W]),
                    reads=[bxT], writes=[self.xt_buf(c0)])
            else:
                self.norm(xT, bxT, W, g["gfin"][:, 0, :], yT, byT, sq, bsq, rs, brs, PS[0], PSB[0])
                subs = [(0, W)] if samp else [(s * 128, 128) for s in range(4)]
                for si, (s0, rows) in enumerate(subs):
                    pz, pb = PS[1 + si % 2], PSB[1 + si % 2]
                    for kc in range(8):
                        b.op("pe", lambda e: e.transpose(out=pz[0:rows, kc * 128:(kc + 1) * 128],
                                                         in_=yT[:, kc, s0:s0 + rows], identity=g["ident"][:]),
                             reads=[byT, self.bg["ident"]], writes=[pb])
                    yt, byt = ytm[si % 2], bytm[si % 2]
                    b.op("act" if si % 2 else "dve",
                         (lambda e: e.activation(out=yt[0:rows, :], in_=pz[0:rows, :], func=AF.Copy)) if si % 2 else
                         (lambda e: e.tensor_copy(out=yt[0:rows, :], in_=pz[0:rows, :])),
                         reads=[pb], writes=[byt])
                    dst = self.dout["y_s"] if samp else self.YP[c0 + s0:c0 + s0 + rows, :]
                    b.dma("sp", "yst%d" % (si % 2), lambda e: e.dma_start(out=dst, in_=yt[0:rows, :]),
                          reads=[byt], writes=[b.dbuf(("Y", c0, s0))])
        b.barrier()


Prog.p3_even = _p3
Prog.p3_odd = _p3


def _p2_even(self, li):
    self.sb_prompt(li)
    self.sb_sample(li)


def _sb_prompt(self, li):
    b, g, PS, PSB, d = self.b, self.g, self.PS, self.PSB, self.din
    ie = li // 2
    T = self.T
    with contextlib.ExitStack() as st:
        sb = lambda n, s_, dt: b.sb(n, s_, dt, st)
        QTs = [sb("QTs", [128, 512], BF16) for _ in range(2)]; bQ = [Buf(), Buf()]
        KTs = [sb("KTs", [128, T], BF16) for _ in range(2)]; bK = [Buf(), Buf()]
        Vs = [sb("Vs", [128, T // 128, 128], BF16) for _ in range(2)]; bV = [Buf(), Buf()]
        E = [sb("E", [128, 1024], F32) for _ in range(2)]; bE = [Buf(), Buf()]
        SP = [sb("SP", [128, 1024], BF16) for _ in range(2)]; bSP = [Buf(), Buf()]
        SA = [sb("SA", [128, 512], F32) for _ in range(2)]; bSA = [Buf(), Buf()]
        SBt = [sb("SBt", [128, 512], F32) for _ in range(2)]; bSBt = [Buf(), Buf()]
        S16 = [sb("S16", [128, 1024], BF16) for _ in range(2)]; bS16 = [Buf(), Buf()]
        Wt = [sb("Wt", [128, 1024], BF16) for _ in range(2)]; bWt = [Buf(), Buf()]
        o16 = [sb("o16", [64, 512], BF16) for _ in range(2)]; bo16 = [Buf(), Buf()]
        sbias = sb("sbias", [128, 8], F32); bsb = Buf()
        b.dma("sp", "prm", lambda e: e.dma_start(out=sbias[:], in_=d["sbb"][ie:ie + 1, :].partition_broadcast(128)),
              writes=[bsb])
        nl = 0
        for ti in range(T // 512):
            c0 = ti * 512
            nblk = 4 * (ti + 1)
            for hp in range(4):
                q_, bq_ = QTs[nl % 2], bQ[nl % 2]
                k_, bk_ = KTs[nl % 2], bK[nl % 2]
                v_, bv_ = Vs[nl % 2], bV[nl % 2]
                cls = "a%d" % (nl % 2)
                nl += 1
                kdeps = [b.dbuf(("KT", cc * 512)) for cc in range(ti + 1)]
                vdeps = [b.dbuf(("V16", cc * 512)) for cc in range(ti + 1)]
                b.dma("sp", cls + "q", lambda e: e.dma_start(out=q_[:], in_=self.QT[hp, :, c0:c0 + 512]),
                      reads=[b.dbuf(("QT", c0))], writes=[bq_])
                b.dma("sp", cls + "k", lambda e: e.dma_start(out=k_[:, 0:nblk * 128], in_=self.KT[hp, :, 0:nblk * 128]),
                      reads=kdeps, writes=[bk_])
                b.dma("sp", cls + "v", lambda e: e.dma_start(
                    out=v_[:, 0:nblk, :], in_=self.V16[hp, 0:nblk * 128, :].rearrange("(k s) f -> s k f", s=128)),
                    reads=vdeps, writes=[bv_])
                CH = (0, 1)
                hsl = [slice(0, 64), slice(64, 128)]
                PZ = [PS[0], PS[1]]; bPZ = [PSB[0], PSB[1]]
                PO = [PS[2], PS[3]]; bPO = [PSB[2], PSB[3]]
                for c in CH:
                    b.op("dve", lambda e: e.memset(SA[c][:], 0.0), writes=[bSA[c]])
                npairs = nblk // 2
                for m in range(npairs):
                    kb = (nblk - 1 - 2 * m, nblk - 2 - 2 * m)
                    diag = kb[1] >= 4 * ti
                    mk = g["sbmask"][:, 2 * m:2 * m + 2, :].rearrange("p a t -> p (a t)") if diag else None
                    for c in CH:
                        hg = hp * 2 + c
                        for x in range(2):
                            b.op("pe", lambda e: e.matmul(out=PZ[c][:, x * 512:(x + 1) * 512],
                                                          lhsT=k_[hsl[c], kb[x] * 128:(kb[x] + 1) * 128],
                                                          rhs=q_[hsl[c], :], start=True, stop=True),
                                 reads=[bk_, bq_], writes=[bPZ[c]])
                    for c in CH:
                        hg = hp * 2 + c
                        b.op("act", lambda e: e.activation(out=E[c][:], in_=PZ[c][:], func=AF.Exp,
                                                           bias=sbias[:, hg:hg + 1], scale=1.0),
                             reads=[bPZ[c], bsb], writes=[bE[c]])
                    for c in CH:
                        b.op("act", lambda e: e.activation(out=SP[c][:], in_=E[c][:], func=AF.Ln,
                                                           bias=g["ccol"][:, 1:2], scale=1.0),
                             reads=[bE[c], self.bg["ccol"]], writes=[bSP[c]])
                        if diag:
                            b.op("pool", lambda e: e.tensor_tensor(out=SP[c][:], in0=SP[c][:], in1=mk, op=ALU.mult),
                                 reads=[bSP[c], self.bg["sbmask"]], writes=[bSP[c]])
                    for c in CH:
                        if m > 0:
                            b.op("pool", lambda e: e.tensor_copy(out=S16[c][:, 0:512], in_=SA[c][:]),
                                 reads=[bSA[c]], writes=[bS16[c]])
                        b.op("dve", lambda e: e.tensor_tensor(out=SBt[c][:], in0=SA[c][:], in1=SP[c][:, 0:512], op=ALU.add),
                             reads=[bSA[c], bSP[c]], writes=[bSBt[c]])
                        b.op("pool", lambda e: e.tensor_copy(out=S16[c][:, 512:1024], in_=SBt[c][:]),
                             reads=[bSBt[c]], writes=[bS16[c]])
                        b.op("dve", lambda e: e.tensor_tensor(out=SA[c][:], in0=SBt[c][:], in1=SP[c][:, 512:1024], op=ALU.add),
                             reads=[bSBt[c], bSP[c]], writes=[bSA[c]])
                    for c in CH:
                        for x in range(2):
                            has_c = not (m == 0 and x == 0)
                            b.op("pe", lambda e: e.matmul(out=PZ[c][:, x * 512:(x + 1) * 512],
                                                          lhsT=k_[hsl[c], kb[x] * 128:(kb[x] + 1) * 128],
                                                          rhs=q_[hsl[c], :], start=True, stop=False),
                                 reads=[bk_, bq_], writes=[bPZ[c]])
                            b.op("pe", lambda e: e.matmul(out=PZ[c][:, x * 512:(x + 1) * 512], lhsT=g["nutrib"][:],
                                                          rhs=SP[c][:, x * 512:(x + 1) * 512], start=False, stop=(not has_c)),
                                 reads=[bSP[c], self.bg["nutrib"]], writes=[bPZ[c]])
                            if has_c:
                                b.op("pe", lambda e: e.matmul(out=PZ[c][:, x * 512:(x + 1) * 512], lhsT=g["nonesb"][:],
                                                              rhs=S16[c][:, x * 512:(x + 1) * 512], start=False, stop=True),
                                     reads=[bS16[c], self.bg["nonesb"]], writes=[bPZ[c]])
                    for c in CH:
                        hg = hp * 2 + c
                        b.op("act", lambda e: e.activation(out=Wt[c][:], in_=PZ[c][:], func=AF.Exp,
                                                           bias=sbias[:, hg:hg + 1], scale=1.0),
                             reads=[bPZ[c], bsb], writes=[bWt[c]])
                        if diag:
                            b.op("pool", lambda e: e.tensor_tensor(out=Wt[c][:], in0=Wt[c][:], in1=mk, op=ALU.mult),
                                 reads=[bWt[c], self.bg["sbmask"]], writes=[bWt[c]])
                    for c in CH:
                        for x in range(2):
                            b.op("pe", lambda e: e.matmul(out=PO[c][0:64, 0:512], lhsT=v_[:, kb[x], hsl[c]],
                                                          rhs=Wt[c][:, x * 512:(x + 1) * 512],
                                                          start=(m == 0 and x == 0), stop=(m == npairs - 1 and x == 1)),
                                 reads=[bv_, bWt[c]], writes=[bPO[c]])
                for c in CH:
                    b.op("act", lambda e: e.activation(out=o16[c][:], in_=PO[c][0:64, 0:512], func=AF.Copy),
                         reads=[bPO[c]], writes=[bo16[c]])
                    b.dma("sp", "ao%d" % c, lambda e: e.dma_start(
                        out=self.AT[hp, c * 64:(c + 1) * 64, c0:c0 + 512], in_=o16[c][:]),
                        reads=[bo16[c]], writes=[b.dbuf(("ATa", c0))])
        b.barrier()


Prog.p2_even = _p2_even
Prog.sb_prompt = _sb_prompt


def _sb_sample(self, li):
    b, g, PS, PSB, d = self.b, self.g, self.PS, self.PSB, self.din
    ie = li // 2
    T, NS, NTS, NP = self.T, self.NS, self.NTS, self.NPAGES
    PG = min(8, NP)
    NG = NP // PG
    NC8 = NP * 8
    with contextlib.ExitStack() as st:
        sb = lambda n, s_, dt: b.sb(n, s_, dt, st)
        ptb = sb("ptb", [128, NP], I32); bptb = Buf()
        idx = sb("idx", [128, NP], I32); bidx = Buf()
        iob = sb("iob", [128, 1], F32); biob = Buf()
        sbias = sb("sbias", [128, 8], F32); bsb = Buf()
        qb = sb("qb", [128, 4, 512], F32); bqb = Buf()
        Kp = [sb("Kp", [128, PG, 512], F32) for _ in range(2)]; bKp = [Buf(), Buf()]
        prod = sb("prod", [128, PG * 512], F32); bprod = Buf()
        zall = sb("zall", [128, 4, NP, 8], F32); bz = Buf()
        SP = sb("SPs", [128, 4, NP, 8], BF16); bSP = Buf()
        tbo = sb("tbo", [128, NP, 8], F32); btbo = Buf()
        tb = [sb("tb", [128, NP, 8], F32) for _ in range(2)]; btb = [Buf(), Buf()]
        arg = sb("args", [128, NP, 8], F32); barg = Buf()
        Wt = sb("Wts", [128, 4, NP, 8], F32); bWt = Buf()
        pv16 = sb("pv16", [128, PG, 512], BF16); bpv = Buf()
        newtot = sb("newtot", [128, 4, 8], F32); bnt = Buf()
        Kn = sb("Kn", [4, 512], F32); bKn = Buf()
        Vn = sb("Vn", [4, 512], F32); bVn = Buf()
        qn = sb("qn", [4, 4, 512], F32); bqn = Buf()
        prodn = sb("prodn", [4, 4, 512], F32); bpn = Buf()
        zn = sb("zn", [4, 4, 8], F32); bzn = Buf()
        spn = sb("spn", [4, 4, 8], F32); bspn = Buf()
        spn16 = sb("spn16", [4, 4, 8], BF16); bspn16 = Buf()
        Wn = sb("Wn", [4, 4, 8], F32); bWn = Buf()
        pvn = sb("pvn", [4, 512], BF16); bpvn = Buf()
        snew3 = sb("snew3", [4, 4, 8], F32); bsn3 = Buf()
        orow = [sb("orow", [1, 512], F32) for _ in range(2)]; borow = [Buf(), Buf()]
        asb = sb("asb", [NTS, 512], F32); basb = Buf()
        at16 = sb("at16", [128, 4, NTS], BF16); bat = Buf()
        b.dma("sp", "prm", lambda e: e.dma_start(out=sbias[:], in_=d["sbb"][ie:ie + 1, :].partition_broadcast(128)),
              writes=[bsb])
        b.dma("sp", "prm", lambda e: e.dma_start(out=iob[:], in_=d["c_iotab"][:, ie:ie + 1], allow_slow_non_contiguous=True), writes=[biob])
        b.op("dve", lambda e: e.tensor_copy(out=snew3[:], in_=g["snew"][:].unsqueeze(2).to_broadcast([4, 4, 8])),
             reads=[self.bg["snew"]], writes=[bsn3])
        ng = 0
        no = 0
        POs = [(PS[2], PSB[2], 0), (PS[2], PSB[2], 512), (PS[3], PSB[3], 0), (PS[3], PSB[3], 512)]
        for s in range(NS):
            r0 = 4 * s
            b.dma("sp", "ss0", lambda e: e.dma_start(out=ptb[:], in_=d["pt"][s:s + 1, :].partition_broadcast(128)),
                  writes=[bptb])
            b.op("dve", lambda e: e.tensor_scalar(out=idx[:], in0=ptb[:], scalar1=128.0, scalar2=iob[:, 0:1],
                                                  op0=ALU.mult, op1=ALU.add), reads=[bptb, biob], writes=[bidx])
            b.dma("sp", "ss1", lambda e: e.dma_start(out=qb[:], in_=self.ZS[r0:r0 + 4, 0:512].partition_broadcast(128)),
                  reads=[b.dbuf(("ZS", 0))], writes=[bqb])
            b.dma("sp", "ss2", lambda e: e.dma_start(out=qn[:], in_=self.ZS[r0:r0 + 4, 0:512].partition_broadcast(4)),
                  reads=[b.dbuf(("ZS", 0))], writes=[bqn])
            b.dma("sp", "ss3", lambda e: e.dma_start(out=Kn[:], in_=self.dout["sbk_s"][ie, r0:r0 + 4, :]),
                  reads=[b.dbuf(("sbk_s", ie))], writes=[bKn])
            b.dma("sp", "ss4", lambda e: e.dma_start(out=Vn[:], in_=self.dout["sbv_s"][ie, r0:r0 + 4, :]),
                  reads=[b.dbuf(("sbv_s", ie))], writes=[bVn])
            for gi in range(NG):
                kp, bkp = Kp[ng % 2], bKp[ng % 2]
                for pgi in range(PG):
                    pg = gi * PG + pgi
                    b.dma("pool", "kp%d" % (ng % 2), lambda e: e.indirect_dma_start(
                        out=kp[:, pgi, :], out_offset=None, in_=d["poolk"],
                        in_offset=bass.IndirectOffsetOnAxis(ap=idx[:, pg:pg + 1], axis=0)),
                        reads=[bidx], writes=[bkp])
                ng += 1
                for t in range(4):
                    b.op("pool", lambda e: e.tensor_tensor(
                        out=prod[:].rearrange("p (a f) -> p a f", f=512), in0=kp[:],
                        in1=qb[:, t:t + 1, :].to_broadcast([128, PG, 512]), op=ALU.mult),
                        reads=[bkp, bqb], writes=[bprod])
                    b.op("dve", lambda e: e.tensor_reduce(
                        out=zall[:, t, gi * PG:(gi + 1) * PG, :], in_=prod[:].rearrange("p (a h x) -> p a h x", h=8, x=64),
                        axis=AX.X, op=ALU.add), reads=[bprod], writes=[bz])
            b.op("dve", lambda e: e.tensor_tensor(out=prodn[:], in0=qn[:], in1=Kn[:].unsqueeze(1).to_broadcast([4, 4, 512]),
                                                  op=ALU.mult), reads=[bqn, bKn], writes=[bpn])
            b.op("dve", lambda e: e.tensor_reduce(out=zn[:].rearrange("p t h -> p (t h)"),
                                                  in_=prodn[:].rearrange("p t (h x) -> p (t h) x", x=64),
                                                  axis=AX.X, op=ALU.add), reads=[bpn], writes=[bzn])
            b.op("dve", lambda e: e.scalar_tensor_tensor(out=zn[:], in0=zn[:], scalar=0.125,
                                                         in1=sbias[0:4, :].unsqueeze(1).to_broadcast([4, 4, 8]),
                                                         op0=ALU.mult, op1=ALU.add), reads=[bzn, bsb], writes=[bzn])
            b.op("act", lambda e: e.activation(out=spn[:], in_=zn[:], func=AF.Exp), reads=[bzn], writes=[bspn])
            b.op("act", lambda e: e.activation(out=spn[:], in_=spn[:], func=AF.Ln, bias=g["ccol"][0:4, 1:2], scale=1.0),
                 reads=[bspn, self.bg["ccol"]], writes=[bspn])
            b.op("dve", lambda e: e.tensor_tensor(out=spn16[:], in0=spn[:], in1=snew3[:], op=ALU.mult),
                 reads=[bspn, bsn3], writes=[bspn16])
            PR, bPR = PS[0], PSB[0]
            b.op("pe", lambda e: e.matmul(out=PR[0:4, 0:32], lhsT=g["nutrib"][0:4, 0:4], rhs=spn16[:].rearrange("p t h -> p (t h)"),
                                          start=True, stop=True), reads=[bspn16, self.bg["nutrib"]], writes=[bPR])
            b.op("pe", lambda e: e.matmul(out=PR[:, 512:544], lhsT=g["onesb"][0:4, :], rhs=spn16[:].rearrange("p t h -> p (t h)"),
                                          start=True, stop=True), reads=[bspn16, self.bg["onesb"]], writes=[bPR])
            b.op("dve", lambda e: e.tensor_copy(out=newtot[:].rearrange("p t h -> p (t h)"), in_=PR[:, 512:544]),
                 reads=[bPR], writes=[bnt])
            b.op("dve", lambda e: e.tensor_tensor(out=Wn[:].rearrange("p t h -> p (t h)"), in0=PR[0:4, 0:32],
                                                  in1=zn[:].rearrange("p t h -> p (t h)"), op=ALU.add),
                 reads=[bPR, bzn], writes=[bWn])
            b.op("act", lambda e: e.activation(out=Wn[:], in_=Wn[:], func=AF.Exp), reads=[bWn], writes=[bWn])
            b.op("dve", lambda e: e.tensor_tensor(out=Wn[:], in0=Wn[:], in1=snew3[:], op=ALU.mult),
                 reads=[bWn, bsn3], writes=[bWn])
            zf = zall[:].rearrange("p t a h -> p (t a) h")
            b.op("dve", lambda e: e.scalar_tensor_tensor(out=zf, in0=zf, scalar=0.125,
                                                         in1=sbias[:].unsqueeze(1).to_broadcast([128, 4 * NP, 8]),
                                                         op0=ALU.mult, op1=ALU.add), reads=[bz, bsb], writes=[bz])
            b.op("act", lambda e: e.activation(out=Wt[:], in_=zall[:], func=AF.Exp), reads=[bz], writes=[bWt])
            b.op("act", lambda e: e.activation(out=SP[:], in_=Wt[:], func=AF.Ln, bias=g["ccol"][:, 1:2], scale=1.0),
                 reads=[bWt, self.bg["ccol"]], writes=[bSP])
            for t in range(4):
                PRt, bPRt = PS[0], PSB[0]
                PTt, bPTt = PS[1], PSB[1]
                spt = SP[:, t].rearrange("p a h -> p (a h)")
                b.op("pe", lambda e: e.matmul(out=PRt[:, 0:NC8], lhsT=g["nutrib"][:], rhs=spt, start=True, stop=True),
                     reads=[bSP, self.bg["nutrib"]], writes=[bPRt])
                b.op("pe", lambda e: e.matmul(out=PTt[:, 0:NC8], lhsT=g["onesb"][:], rhs=spt, start=True, stop=True),
                     reads=[bSP, self.bg["onesb"]], writes=[bPTt])
                b.op("dve", lambda e: e.tensor_copy(out=tbo[:].rearrange("p a h -> p (a h)"), in_=PTt[:, 0:NC8]),
                     reads=[bPTt], writes=[btbo])
                cur, bcur = tbo, btbo
                k = 0
                dd = 1
                while dd < NP:
                    nxt, bnxt = tb[k % 2], btb[k % 2]
                    k += 1
                    b.op("dve", lambda e: e.tensor_tensor(out=nxt[:, 0:NP - dd, :], in0=cur[:, 0:NP - dd, :],
                                                          in1=cur[:, dd:NP, :], op=ALU.add), reads=[bcur], writes=[bnxt])
                    b.op("dve", lambda e: e.tensor_copy(out=nxt[:, NP - dd:NP, :], in_=cur[:, NP - dd:NP, :]),
                         reads=[bcur], writes=[bnxt])
                    cur, bcur = nxt, bnxt
                    dd *= 2
                b.op("dve", lambda e: e.tensor_tensor(out=arg[:].rearrange("p a h -> p (a h)"), in0=PRt[:, 0:NC8],
                                                      in1=zall[:, t].rearrange("p a h -> p (a h)"), op=ALU.add),
                     reads=[bPRt, bz], writes=[barg])
                if cur is not tbo:
                    b.op("dve", lambda e: e.tensor_tensor(out=arg[:], in0=arg[:], in1=cur[:], op=ALU.subtract),
                         reads=[barg, bcur], writes=[barg])
                    b.op("dve", lambda e: e.tensor_tensor(out=arg[:], in0=arg[:], in1=tbo[:], op=ALU.add),
                         reads=[barg, btbo], writes=[barg])
                b.op("dve", lambda e: e.tensor_tensor(out=arg[:], in0=arg[:],
                                                      in1=newtot[:, t:t + 1, :].to_broadcast([128, NP, 8]), op=ALU.subtract),
                     reads=[barg, bnt], writes=[barg])
                b.op("act", lambda e: e.activation(out=Wt[:, t], in_=arg[:], func=AF.Exp), reads=[barg], writes=[bWt])
            for gi in range(NG):
                vp, bvp = Kp[ng % 2], bKp[ng % 2]
                for pgi in range(PG):
                    pg = gi * PG + pgi
                    b.dma("pool", "kp%d" % (ng % 2), lambda e: e.indirect_dma_start(
                        out=vp[:, pgi, :], out_offset=None, in_=d["poolv"],
                        in_offset=bass.IndirectOffsetOnAxis(ap=idx[:, pg:pg + 1], axis=0)),
                        reads=[bidx], writes=[bvp])
                ng += 1
                for t in range(4):
                    PO, bPO, oc = POs[t]
                    b.op("dve", lambda e: e.tensor_tensor(
                        out=pv16[:].rearrange("p a (h x) -> p a h x", x=64), in0=vp[:].rearrange("p a (h x) -> p a h x", x=64),
                        in1=Wt[:, t, gi * PG:(gi + 1) * PG, :].unsqueeze(3).to_broadcast([128, PG, 8, 64]), op=ALU.mult),
                        reads=[bvp, bWt], writes=[bpv])
                    for pgi in range(PG):
                        b.op("pe", lambda e: e.matmul(out=PO[0:1, oc:oc + 512], lhsT=g["onesb"][:, 0:1], rhs=pv16[:, pgi, :],
                                                      start=(gi == 0 and pgi == 0), stop=False),
                             reads=[bpv, self.bg["onesb"]], writes=[bPO])
            for t in range(4):
                PO, bPO, oc = POs[t]
                b.op("dve", lambda e: e.tensor_tensor(
                    out=pvn[:].rearrange("p (h x) -> p h x", x=64), in0=Vn[:].rearrange("p (h x) -> p h x", x=64),
                    in1=Wn[:, t, :].unsqueeze(2).to_broadcast([4, 8, 64]), op=ALU.mult), reads=[bVn, bWn], writes=[bpvn])
                b.op("pe", lambda e: e.matmul(out=PO[0:1, oc:oc + 512], lhsT=g["onesb"][0:4, 0:1], rhs=pvn[:],
                                              start=False, stop=True), reads=[bpvn, self.bg["onesb"]], writes=[bPO])
                orw, borw = orow[no % 2], borow[no % 2]
                b.op("act", lambda e: e.activation(out=orw[:], in_=PO[0:1, oc:oc + 512], func=AF.Copy),
                     reads=[bPO], writes=[borw])
                b.dma("sp", "or%d" % (no % 2), lambda e: e.dma_start(out=self.AS[r0 + t:r0 + t + 1, 0:512], in_=orw[:]),
                      reads=[borw], writes=[b.dbuf(("AS", 0))])
                no += 1
        self.as_to_at(asb, basb, at16, bat, 4)
        b.barrier()


def _as_to_at(self, asb, basb, at16, bat, nchunk):
    b, g, PS, PSB = self.b, self.g, self.PS, self.PSB
    T, NTS = self.T, self.NTS
    b.dma("sp", "asl", lambda e: e.dma_start(out=asb[:], in_=self.AS[:, 0:nchunk * 128]),
          reads=[b.dbuf(("AS", 0))], writes=[basb])
    for j in range(nchunk):
        b.op("pe", lambda e: e.transpose(out=PS[0][:, j * NTS:(j + 1) * NTS], in_=asb[:, j * 128:(j + 1) * 128],
                                         identity=g["ident"][0:NTS, 0:NTS]),
             reads=[basb, self.bg["ident"]], writes=[PSB[0]])
    b.op("dve", lambda e: e.tensor_copy(out=at16[:].rearrange("p k t -> p (k t)"), in_=PS[0][:, 0:nchunk * NTS]),
         reads=[PSB[0]], writes=[bat])
    b.dma("sp", "ast", lambda e: e.dma_start(
        out=self.AT[0:nchunk, :, T:T + NTS].rearrange("k p t -> p k t"), in_=at16[:]),
        reads=[bat], writes=[b.dbuf(("ATa", T))])


Prog.sb_sample = _sb_sample
Prog.as_to_at = _as_to_at


def _bias_tables(self):
    b, g, PS, PSB, d = self.b, self.g, self.PS, self.PSB, self.din
    brv = b.sb("biasrev", [128, 3, 16], F32); bbrv = Buf()
    bnw = b.sb("biasnew", [4, 12, 16], F32); bbnw = Buf()
    g["biasrev"], self.bg["biasrev"] = brv, bbrv
    g["biasnew"], self.bg["biasnew"] = bnw, bbnw
    with contextlib.ExitStack() as st:
        sb = lambda n, s_, dt: b.sb(n, s_, dt, st)
        relb = sb("relb", [32, 16], F32); brl = Buf()
        oh = sb("oh", [32, 387], F32); boh = Buf()
        ohr = sb("ohr", [32, 384], F32); bohr = Buf()
        ohn = sb("ohn", [32, 48], F32); bohn = Buf()
        ngn = sb("ngn", [4, 12], F32); bngn = Buf()
        Et = sb("Et", [16, 3, 384], F32); bEt = Buf()
        Hk = [sb("Hk", [128, 2, 128], F32) for _ in range(2)]; bHk = [Buf(), Buf()]
        Bo = [sb("Bo", [128, 2, 128], F32) for _ in range(2)]; bBo = [Buf(), Buf()]
        for t_, bf_, nm in ((relb, brl, "relb"), (oh, boh, "c_oh"), (ohr, bohr, "c_ohrev"), (ohn, bohn, "c_ohnew"),
                            (ngn, bngn, "c_negnew")):
            b.dma("sp", "const", lambda e: e.dma_start(out=t_[:], in_=d[nm]), writes=[bf_])
        b.op("pe", lambda e: e.matmul(out=PS[0][0:16, 0:387], lhsT=relb[:], rhs=oh[:], start=True, stop=True),
             reads=[brl, boh], writes=[PSB[0]])
        b.op("dve", lambda e: e.memset(Et[:], NEG), writes=[bEt])
        for br in range(3):
            b.op("dve", lambda e: e.tensor_copy(out=Et[:, br, 127:256], in_=PS[0][0:16, br * 129:(br + 1) * 129]),
                 reads=[PSB[0]], writes=[bEt])
        b.dma("sp", "tab", lambda e: e.dma_start(out=self.TAB, in_=Et[:]), reads=[bEt], writes=[b.dbuf("TAB")])
        for br in range(3):
            b.op("pe", lambda e: e.matmul(out=PS[1][:, br * 16:(br + 1) * 16], lhsT=ohr[:, br * 128:(br + 1) * 128], rhs=relb[:],
                                          start=True, stop=True), reads=[bohr, brl], writes=[PSB[1]])
        b.op("dve", lambda e: e.tensor_copy(out=brv[:].rearrange("p a h -> p (a h)"), in_=PS[1][:, 0:48]),
             reads=[PSB[1]], writes=[bbrv])
        for k in range(12):
            b.op("pe", lambda e: e.matmul(out=PS[1][0:4, 512 + k * 16:512 + (k + 1) * 16], lhsT=ohn[:, k * 4:(k + 1) * 4], rhs=relb[:],
                                          start=True, stop=True), reads=[bohn, brl], writes=[PSB[1]])
        b.op("dve", lambda e: e.tensor_tensor(out=bnw[:], in0=PS[1][0:4, 512:704].rearrange("p (a h) -> p a h", h=16),
                                              in1=ngn[:].unsqueeze(2).to_broadcast([4, 12, 16]), op=ALU.add),
             reads=[PSB[1], bngn], writes=[bbnw])
        n = 0
        for br in range(3):
            for h in range(16):
                hk, bhk = Hk[n % 2], bHk[n % 2]
                bo, bbo = Bo[n % 2], bBo[n % 2]
                off = (h * 3 + br) * 384
                for x, o_ in ((0, 128), (1, 0)):
                    src = bass.AP(self.TAB.tensor, off + o_, [[1, 128], [1, 128]])
                    b.dma("sp", "hk%d" % (n % 2), lambda e: e.dma_start(out=hk[:, x, :], in_=src),
                          reads=[b.dbuf("TAB")], writes=[bhk])
                pz, pb = PS[2 + n % 2], PSB[2 + n % 2]
                b.op("pe", lambda e: e.matmul(out=pz[:, 0:256], lhsT=g["flip"][:], rhs=hk[:].rearrange("p x q -> p (x q)"),
                                              start=True, stop=True), reads=[bhk, self.bg["flip"]], writes=[pb])
                b.op("dve" if n % 2 else "act",
                     (lambda e: e.tensor_copy(out=bo[:].rearrange("p x q -> p (x q)"), in_=pz[:, 0:256])) if n % 2 else
                     (lambda e: e.activation(out=bo[:].rearrange("p x q -> p (x q)"), in_=pz[:, 0:256], func=AF.Copy)),
                     reads=[pb], writes=[bbo])
                b.dma("sp", "bo%d" % (n % 2), lambda e: e.dma_start(
                    out=self.BTD[br, h].rearrange("x s q -> s x q"), in_=bo[:]), reads=[bbo], writes=[b.dbuf("BTD")])
                n += 1
        b.barrier()


def _p1_odd(self, li):
    b, g, PS, PSB, d = self.b, self.g, self.PS, self.PSB, self.din
    io = li // 2
    T, NS, NTS, KEEP = self.T, self.NS, self.NTS, self.KEEP
    wb = self.WB["w_in_c"][io]
    with contextlib.ExitStack() as st:
        sb = lambda n, s_, dt: b.sb(n, s_, dt, st)
        xT = sb("xT", [128, 8, 512], F32); bxT = Buf()
        sq = sb("sq", [128, 8, 512], BF16); bsq = Buf()
        rs = sb("rs", [128, 512], F32); brs = Buf()
        uT = sb("uT", [128, 8, 512], BF16); buT = Buf()
        wts = [sb("wt", [128, 8, 512], BF16) for _ in range(4)]; bws = [Buf() for _ in range(4)]
        st16 = [sb("st16", [128, 4, 512], BF16) for _ in range(2)]; bst16 = [Buf(), Buf()]
        tm32 = [sb("tm32", [128, 512], F32) for _ in range(2)]; btm32 = [Buf(), Buf()]
        tm16 = [sb("tm16", [128, 512], BF16) for _ in range(2)]; btm16 = [Buf(), Buf()]
        nw = 0
        for ti, (c0, W, samp) in enumerate(self.tiles()):
            b.dma("sp", "xld", lambda e: e.dma_start(
                out=xT[:, :, 0:W], in_=self.XT[:, :, c0:c0 + W].rearrange("k p t -> p k t")),
                reads=[self.xt_buf(c0)], writes=[bxT])
            self.norm(xT, bxT, W, g["gmix"][:, li, :], uT, buT, sq, bsq, rs, brs, PS[0], PSB[0])
            subs = [(0, W)] if samp else [(s * 128, 128) for s in range(4)]
            for grp in range(6):
                wt, bw = wts[nw % 4], bws[nw % 4]
                self.wload(wt, bw, wb, 0, 8, grp * 512, 512, "w%d" % (nw % 4)); nw += 1
                kind, half = grp // 2, grp % 2
                if kind in (0, 1):
                    so, bso = st16[grp % 2], bst16[grp % 2]
                    for j in range(4):
                        pz, pb = PS[1 + (j % 2)], PSB[1 + (j % 2)]
                        for kc in range(8):
                            b.op("pe", lambda e: e.matmul(out=pz[:, 0:W], lhsT=wt[:, kc, j * 128:(j + 1) * 128],
                                                          rhs=uT[:, kc, 0:W], start=(kc == 0), stop=(kc == 7)),
                                 reads=[bw, buT], writes=[pb])
                        b.op("act", lambda e: e.activation(out=so[:, j, 0:W], in_=pz[:, 0:W], func=AF.Copy,
                                                           scale=(0.125 if kind == 0 else 1.0)), reads=[pb], writes=[bso])
                    dst = self.QT if kind == 0 else self.KT
                    b.dma("sp", "qk%d" % (grp % 2), lambda e: e.dma_start(
                        out=dst[half * 4:half * 4 + 4, :, c0:c0 + W].rearrange("k p t -> p k t"), in_=so[:, :, 0:W]),
                        reads=[bso], writes=[b.dbuf(("QT" if kind == 0 else "KT", c0))])
                if kind in (1, 2) or (samp and kind == 0):
                    for si, (s0, rows) in enumerate(subs):
                        pz, pb = PS[3], PSB[3]
                        for kc in range(8):
                            b.op("pe", lambda e: e.matmul(out=pz[0:rows, 0:512], lhsT=uT[:, kc, s0:s0 + rows],
                                                          rhs=wt[:, kc, :], start=(kc == 0), stop=(kc == 7)),
                                 reads=[bw, buT], writes=[pb])
                        tok = c0 + s0
                        t32, bt32 = tm32[si % 2], btm32[si % 2]
                        if samp or tok >= T - KEEP:
                            b.op("dve", lambda e: e.tensor_copy(out=t32[0:rows, :], in_=pz[0:rows, 0:512]),
                                 reads=[pb], writes=[bt32])
                            if samp:
                                if kind == 0:
                                    dstap, dkey = self.ZS[:, half * 512:(half + 1) * 512], ("ZS", 0)
                                else:
                                    nm = "dk_s" if kind == 1 else "dv_s"
                                    dstap, dkey = self.dout[nm][io][:, half * 512:(half + 1) * 512], (nm, io)
                            else:
                                r_ = tok - (T - KEEP)
                                dstap = (self.DKS if kind == 1 else self.DVS)[io, r_:r_ + rows, half * 512:(half + 1) * 512]
                                dkey = ("DKS" if kind == 1 else "DVS", io, c0)
                            b.dma("sp", "tm%d" % (si % 2), lambda e: e.dma_start(out=dstap, in_=t32[0:rows, :]),
                                  reads=[bt32], writes=[b.dbuf(dkey)])
                        if kind == 2 and not samp:
                            t16, bt16 = tm16[si % 2], btm16[si % 2]
                            b.op("act", lambda e: e.activation(out=t16[0:rows, :], in_=pz[0:rows, 0:512], func=AF.Copy),
                                 reads=[pb], writes=[bt16])
                            for hq in range(4):
                                b.dma("sp", "tv%d" % (si % 2), lambda e: e.dma_start(
                                    out=self.V16[half * 4 + hq, tok:tok + rows, :], in_=t16[0:rows, hq * 128:(hq + 1) * 128]),
                                    reads=[bt16], writes=[b.dbuf(("V16", c0))])
        b.barrier()


Prog.bias_tables = _bias_tables
Prog.p1_odd = _p1_odd


def _p2_odd(self, li):
    import os
    if os.environ.get("SKIPDP") != "1":
        self.dil_prompt(li)
    if os.environ.get("SKIPDS") != "1":
        self.dil_sample(li)


def _dil_prompt(self, li):
    b, g, PS, PSB, d = self.b, self.g, self.PS, self.PSB, self.din
    T = self.T
    NSB = T // 2048
    with contextlib.ExitStack() as st:
        sb = lambda n, s_, dt: b.sb(n, s_, dt, st)
        QTs = sb("QTd", [128, T], BF16); bQ = Buf()
        KTs = sb("KTd", [128, T], BF16); bK = Buf()
        Vc = sb("Vc", [128, 3, T // 128, 128], BF16); bV = Buf()
        BT = sb("BT", [128, 3, 2, 2, 128], F32); bBT = Buf()
        NUM = sb("NUM", [128, 2048], F32); bN = Buf()
        DEN = sb("DEN", [128, 2048], F32); bD = Buf()
        zb = [sb("zb", [128, 512], F32) for _ in range(2)]; bzb = [Buf(), Buf()]
        pT = [sb("pT", [128, 512], BF16) for _ in range(2)]; bpT = [Buf(), Buf()]
        o16 = sb("o16d", [128, 2048], BF16); bo = Buf()
        nu = 0
        for hp in range(8):
            b.dma("sp", "dq", lambda e: e.dma_start(out=QTs[:], in_=self.QT[hp, :, 0:T]),
                  reads=[b.dbuf(("QT", c * 512)) for c in range(T // 512)], writes=[bQ])
            b.dma("sp", "dk", lambda e: e.dma_start(out=KTs[:], in_=self.KT[hp, :, 0:T]),
                  reads=[b.dbuf(("KT", c * 512)) for c in range(T // 512)], writes=[bK])
            vdeps = [b.dbuf(("V16", c * 512)) for c in range(T // 512)]
            for br, (win, dil) in enumerate(DIL_PAIRS):
                nb = T // (128 * dil)
                for r in range(dil):
                    b.dma("sp", "dv", lambda e: e.dma_start(
                        out=Vc[:, br, r * nb:(r + 1) * nb, :],
                        in_=self.V16[hp, r:T:dil, :].rearrange("(k s) f -> s k f", s=128)),
                        reads=vdeps, writes=[bV])
                for h in range(2):
                    b.dma("sp", "db", lambda e: e.dma_start(
                        out=BT[:, br, h], in_=self.BTD[br, 2 * hp + h].rearrange("x s q -> s x q")),
                        reads=[b.dbuf("BTD")], writes=[bBT])
            for sbi in range(NSB):
                b.op("pool", lambda e: e.memset(NUM[:], 0.0), writes=[bN])
                b.op("pool", lambda e: e.memset(DEN[:], 0.0), writes=[bD])
                for br, (win, dil) in enumerate(DIL_PAIRS):
                    nb = T // (128 * dil)
                    bps = 16 // dil
                    for r in range(dil):
                        for m in range(bps):
                            gblk = sbi * bps + m
                            qstart = r + dil * 128 * gblk
                            qcols = slice(qstart, qstart + dil * 127 + 1, dil)
                            has_prev = gblk >= 1
                            pstart = qstart - dil * 128
                            pcols = slice(pstart, pstart + dil * 127 + 1, dil) if has_prev else qcols
                            PZ, bPZ = PS[nu % 2], PSB[nu % 2]
                            PO, bPO = PS[2 + nu % 2], PSB[2 + nu % 2]
                            z_, bz_ = zb[nu % 2], bzb[nu % 2]
                            p_, bp_ = pT[nu % 2], bpT[nu % 2]
                            nu += 1
                            for h in range(2):
                                hs = slice(h * 64, (h + 1) * 64)
                                for x, kc_ in ((0, pcols), (1, qcols)):
                                    b.op("pe", lambda e: e.matmul(out=PZ[:, h * 512 + x * 128:h * 512 + (x + 1) * 128],
                                                                  lhsT=KTs[hs, kc_], rhs=QTs[hs, qcols], start=True, stop=True),
                                         reads=[bK, bQ], writes=[bPZ])
                            b.op("dve", lambda e: e.tensor_tensor(out=z_[:].rearrange("s (h c) -> s h c", h=2),
                                                                  in0=PZ[:].rearrange("s (h c) -> s h c", h=2)[:, :, 0:256],
                                                                  in1=BT[:, br].rearrange("s h x q -> s h (x q)"), op=ALU.add),
                                 reads=[bPZ, bBT], writes=[bz_])
                            b.op("act", lambda e: e.activation(out=p_[:], in_=z_[:], func=AF.Exp), reads=[bz_], writes=[bp_])
                            xs = (0, 1) if has_prev else (1,)
                            for h in range(2):
                                for xi, x in enumerate(xs):
                                    vblk = Vc[:, br, r * nb + gblk - (1 - x), :]
                                    pc = p_[:, (h * 2 + x) * 128:(h * 2 + x + 1) * 128]
                                    b.op("pe", lambda e: e.matmul(out=PO[:, h * 128:(h + 1) * 128], lhsT=vblk, rhs=pc,
                                                                  start=(xi == 0), stop=(xi == len(xs) - 1)),
                                         reads=[bV, bp_], writes=[bPO])
                                for xi, x in enumerate(xs):
                                    pc = p_[:, (h * 2 + x) * 128:(h * 2 + x + 1) * 128]
                                    b.op("pe", lambda e: e.matmul(out=PO[:, 256 + h * 128:256 + (h + 1) * 128], lhsT=g["onesb"][:], rhs=pc,
                                                                  start=(xi == 0), stop=(xi == len(xs) - 1)),
                                         reads=[bp_, self.bg["onesb"]], writes=[bPO])
                            rel = r + dil * 128 * m
                            tc_ = slice(rel, rel + dil * 127 + 1, dil)
                            for h in range(2):
                                hs = slice(h * 64, (h + 1) * 64)
                                b.op("dve", lambda e: e.tensor_tensor(out=NUM[hs, tc_], in0=PO[hs, h * 128:(h + 1) * 128],
                                                                      in1=NUM[hs, tc_], op=ALU.add),
                                     reads=[bPO, bN], writes=[bN])
                                b.op("dve", lambda e: e.tensor_tensor(out=DEN[hs, tc_], in0=PO[hs, 256 + h * 128:256 + (h + 1) * 128],
                                                                      in1=DEN[hs, tc_], op=ALU.add),
                                     reads=[bPO, bD], writes=[bD])
                b.op("dve", lambda e: e.reciprocal(out=DEN[:], in_=DEN[:]), reads=[bD], writes=[bD])
                b.op("dve", lambda e: e.tensor_tensor(out=o16[:], in0=NUM[:], in1=DEN[:], op=ALU.mult),
                     reads=[bN, bD], writes=[bo])
                b.dma("sp", "do", lambda e: e.dma_start(out=self.AT[hp, :, sbi * 2048:(sbi + 1) * 2048], in_=o16[:]),
                      reads=[bo], writes=[b.dbuf(("ATa", sbi * 2048 + c * 512)) for c in range(4)] + [b.dbuf(("ATl", sbi * 2048 + c * 512)) for c in range(4)])
        b.barrier()


def _dil_sample(self, li):
    b, g, PS, PSB, d = self.b, self.g, self.PS, self.PSB, self.din
    io = li // 2
    T, NS, NTS = self.T, self.NS, self.NTS
    with contextlib.ExitStack() as st:
        sb = lambda n, s_, dt: b.sb(n, s_, dt, st)
        qb = sb("qbd", [128, 1024], F32); bqb = Buf()
        qn = sb("qnd", [4, 1024], F32); bqn = Buf()
        Kn = sb("Knd", [4, 1024], F32); bKn = Buf()
        Vn = sb("Vnd", [4, 1024], F32); bVn = Buf()
        Kd = [sb("Kd", [128, 1024], F32) for _ in range(2)]; bKd = [Buf(), Buf()]
        Vd = [sb("Vd", [128, 1024], F32) for _ in range(2)]; bVd = [Buf(), Buf()]
        prod = sb("prodd", [128, 1024], F32); bprod = Buf()
        z = sb("zd", [128, 16], F32); bz = Buf()
        p = sb("pd", [128, 16], F32); bp = Buf()
        pv16 = sb("pvd", [128, 1024], BF16); bpv = Buf()
        zn = sb("znd", [4, 16], F32); bzn = Buf()
        pn = sb("pnd", [4, 3, 16], F32); bpn = Buf()
        pns = sb("pns", [4, 16], F32); bpns = Buf()
        pvn = sb("pvnd", [4, 1024], BF16); bpvn = Buf()
        rden = sb("rden", [1, 16], F32); brd = Buf()
        orow = [sb("orowd", [1, 1024], F32) for _ in range(2)]; borow = [Buf(), Buf()]
        asb = sb("asbd", [NTS, 1024], F32); basb = Buf()
        at16 = sb("at16d", [128, 8, NTS], BF16); bat = Buf()
        nk = 0
        no = 0
        for s in range(NS):
            r0 = 4 * s
            b.dma("sp", "ss3", lambda e: e.dma_start(out=Kn[:], in_=self.dout["dk_s"][io, r0:r0 + 4, :]),
                  reads=[b.dbuf(("dk_s", io))], writes=[bKn])
            b.dma("sp", "ss4", lambda e: e.dma_start(out=Vn[:], in_=self.dout["dv_s"][io, r0:r0 + 4, :]),
                  reads=[b.dbuf(("dv_s", io))], writes=[bVn])
            for t in range(4):
                b.dma("sp", "ss1", lambda e: e.dma_start(out=qb[:], in_=self.ZS[r0 + t:r0 + t + 1, 0:1024].partition_broadcast(128)),
                      reads=[b.dbuf(("ZS", 0))], writes=[bqb])
                b.dma("sp", "ss2", lambda e: e.dma_start(out=qn[:], in_=self.ZS[r0 + t:r0 + t + 1, 0:1024].partition_broadcast(4)),
                      reads=[b.dbuf(("ZS", 0))], writes=[bqn])
                PN, bPN = PS[2], PSB[2]
                PD, bPD = PS[3], PSB[3]
                for br, (win, dil) in enumerate(DIL_PAIRS):
                    n = 128 - t if dil == 1 else 128
                    idx0 = CBUF + t - 128 * dil
                    kd, bkd = Kd[nk % 2], bKd[nk % 2]
                    vd, bvd = Vd[nk % 2], bVd[nk % 2]
                    b.dma("sp", "kd%d" % (nk % 2), lambda e: e.dma_start(out=kd[0:n, :], in_=d["cdk"][io, s, idx0:idx0 + (n - 1) * dil + 1:dil, :]),
                          writes=[bkd])
                    b.dma("sp", "vd%d" % (nk % 2), lambda e: e.dma_start(out=vd[0:n, :], in_=d["cdv"][io, s, idx0:idx0 + (n - 1) * dil + 1:dil, :]),
                          writes=[bvd])
                    nk += 1
                    b.op("dve", lambda e: e.tensor_tensor(out=prod[0:n, :], in0=kd[0:n, :], in1=qb[0:n, :], op=ALU.mult),
                         reads=[bkd, bqb], writes=[bprod])
                    b.op("dve", lambda e: e.tensor_reduce(out=z[0:n, :], in_=prod[0:n, :].rearrange("p (h x) -> p h x", x=64),
                                                          axis=AX.X, op=ALU.add), reads=[bprod], writes=[bz])
                    b.op("dve", lambda e: e.scalar_tensor_tensor(out=z[0:n, :], in0=z[0:n, :], scalar=0.125,
                                                                 in1=g["biasrev"][0:n, br, :], op0=ALU.mult, op1=ALU.add),
                         reads=[bz, self.bg["biasrev"]], writes=[bz])
                    b.op("act", lambda e: e.activation(out=p[0:n, :], in_=z[0:n, :], func=AF.Exp), reads=[bz], writes=[bp])
                    b.op("dve", lambda e: e.tensor_tensor(out=pv16[0:n, :].rearrange("p (h x) -> p h x", x=64),
                                                          in0=vd[0:n, :].rearrange("p (h x) -> p h x", x=64),
                                                          in1=p[0:n, :].unsqueeze(2).to_broadcast([n, 16, 64]), op=ALU.mult),
                         reads=[bvd, bp], writes=[bpv])
                    for c in range(2):
                        b.op("pe", lambda e: e.matmul(out=PN[0:1, c * 512:(c + 1) * 512], lhsT=g["onesb"][0:n, 0:1],
                                                      rhs=pv16[0:n, c * 512:(c + 1) * 512], start=(br == 0), stop=False),
                             reads=[bpv, self.bg["onesb"]], writes=[bPN])
                    b.op("pe", lambda e: e.matmul(out=PD[0:1, 0:16], lhsT=g["ones32"][0:n, 0:1], rhs=p[0:n, :],
                                                  start=(br == 0), stop=False), reads=[bp, self.bg["ones32"]], writes=[bPD])
                b.op("dve", lambda e: e.tensor_tensor(out=prod[0:4, :], in0=Kn[:], in1=qn[:], op=ALU.mult),
                     reads=[bKn, bqn], writes=[bprod])
                b.op("dve", lambda e: e.tensor_reduce(out=zn[:], in_=prod[0:4, :].rearrange("p (h x) -> p h x", x=64),
                                                      axis=AX.X, op=ALU.add), reads=[bprod], writes=[bzn])
                b.op("dve", lambda e: e.scalar_tensor_tensor(out=pn[:], in0=zn[:].unsqueeze(1).to_broadcast([4, 3, 16]), scalar=0.125,
                                                             in1=g["biasnew"][:, t * 3:(t + 1) * 3, :], op0=ALU.mult, op1=ALU.add),
                     reads=[bzn, self.bg["biasnew"]], writes=[bpn])
                b.op("act", lambda e: e.activation(out=pn[:], in_=pn[:], func=AF.Exp), reads=[bpn], writes=[bpn])
                b.op("dve", lambda e: e.tensor_tensor(out=pns[:], in0=pn[:, 0, :], in1=pn[:, 1, :], op=ALU.add),
                     reads=[bpn], writes=[bpns])
                b.op("dve", lambda e: e.tensor_tensor(out=pns[:], in0=pns[:], in1=pn[:, 2, :], op=ALU.add),
                     reads=[bpn, bpns], writes=[bpns])
                b.op("dve", lambda e: e.tensor_tensor(out=pvn[:].rearrange("p (h x) -> p h x", x=64),
                                                      in0=Vn[:].rearrange("p (h x) -> p h x", x=64),
                                                      in1=pns[:].unsqueeze(2).to_broadcast([4, 16, 64]), op=ALU.mult),
                     reads=[bVn, bpns], writes=[bpvn])
                for c in range(2):
                    b.op("pe", lambda e: e.matmul(out=PN[0:1, c * 512:(c + 1) * 512], lhsT=g["onesb"][0:4, 0:1],
                                                  rhs=pvn[:, c * 512:(c + 1) * 512], start=False, stop=True),
                         reads=[bpvn, self.bg["onesb"]], writes=[bPN])
                b.op("pe", lambda e: e.matmul(out=PD[0:1, 0:16], lhsT=g["ones32"][0:4, 0:1], rhs=pns[:], start=False, stop=True),
                     reads=[bpns, self.bg["ones32"]], writes=[bPD])
                b.op("dve", lambda e: e.reciprocal(out=rden[:], in_=PD[0:1, 0:16]), reads=[bPD], writes=[brd])
                orw, borw = orow[no % 2], borow[no % 2]
                b.op("dve", lambda e: e.tensor_tensor(out=orw[:].rearrange("p (h x) -> p h x", x=64),
                                                      in0=PN[0:1, :].rearrange("p (h x) -> p h x", x=64),
                                                      in1=rden[:].unsqueeze(2).to_broadcast([1, 16, 64]), op=ALU.mult),
                     reads=[bPN, brd], writes=[borw])
                b.dma("sp", "or%d" % (no % 2), lambda e: e.dma_start(out=self.AS[r0 + t:r0 + t + 1, :], in_=orw[:]),
                      reads=[borw], writes=[b.dbuf(("AS", 0))])
                no += 1
        self.as_to_at(asb, basb, at16, bat, 8)
        b.dbuf(("ATl", T))
        b.barrier()


Prog.p2_odd = _p2_odd
Prog.dil_prompt = _dil_prompt
Prog.dil_sample = _dil_sample
```

```python
import contextlib
import math

import numpy as np
import concourse.bass as bass
import concourse.mybir as mybir
from concourse.bass_utils import run_bass_kernel_spmd

F32 = mybir.dt.float32
BF16 = mybir.dt.bfloat16
I32 = mybir.dt.int32
AF = mybir.ActivationFunctionType
ALU = mybir.AluOpType
AX = mybir.AxisListType

NCORES = 8
HD = 64
NEG = -30000.0
DIL_PAIRS = ((128, 1), (512, 4), (2048, 16))
CBUF = 2048


class Buf:
    __slots__ = ("w", "r", "name", "psum")

    def __init__(self, name="", psum=False):
        self.w = None
        self.r = {}
        self.name = name
        self.psum = psum


class Builder:
    def __init__(self, nc):
        self.nc = nc
        self.streams = {k: [] for k in ("pe", "act", "dve", "pool", "sp")}
        self.eh = {"pe": nc.tensor, "act": nc.scalar, "dve": nc.vector, "pool": nc.gpsimd, "sp": nc.sync}
        self.sems = {}
        self.cnt = {}
        self.seen = {k: {} for k in self.streams}
        self.stack = contextlib.ExitStack()
        self.n_ops = 0
        self.n_alloc = 0
        for k in self.streams:
            if k != "sp":
                self._sem(k)
        self.dbufs = {}

    def _sem(self, key):
        if key not in self.sems:
            self.sems[key] = self.stack.enter_context(self.nc.semaphore("s_" + str(key)))
            self.cnt[key] = 0
        return self.sems[key]

    def sb(self, name, shape, dtype, stack=None):
        self.n_alloc += 1
        t = (stack or self.stack).enter_context(
            self.nc.sbuf_tensor("%s_%d" % (name, self.n_alloc), list(shape), dtype))
        return t

    def ps(self, name, shape, dtype=F32, stack=None):
        self.n_alloc += 1
        t = (stack or self.stack).enter_context(
            self.nc.psum_tensor("%s_%d" % (name, self.n_alloc), list(shape), dtype))
        return t

    def dbuf(self, key):
        b = self.dbufs.get(key)
        if b is None:
            b = self.dbufs[key] = Buf(str(key))
        return b

    def _wait(self, eng, dep):
        key, val = dep
        if self.seen[eng].get(key, 0) >= val:
            return
        self.seen[eng][key] = val
        self.eh[eng].wait_ge(self.sems[key], val)

    def _deps(self, eng, reads, writes, own_key):
        deps = []
        for b in reads:
            if b.w is not None:
                deps.append(b.w)
            if b.psum:
                for k, v in b.r.items():
                    if k != own_key:
                        deps.append((k, v))
        for b in writes:
            if b.w is not None:
                deps.append(b.w)
            for k, v in b.r.items():
                deps.append((k, v))
        return deps

    def op(self, eng, fn, reads=(), writes=(), pe_chain=False):
        deps = self._deps(eng, reads, writes, eng)
        for d in deps:
            if eng == "pe" and d[0] == "pe":
                continue
            self._wait(eng, d)
        self.cnt[eng] += 1
        ev = (eng, self.cnt[eng])
        fn(self.eh[eng]).then_inc(self.sems[eng], 1)
        for b in reads:
            b.r[eng] = ev[1]
        for b in writes:
            b.w = ev
            b.r = {}
        self.n_ops += 1
        return ev

    def dma(self, q, cls, fn, reads=(), writes=()):
        key = ("d", q, cls)
        self._sem(key)
        deps = self._deps(q, reads, writes, None)
        if self.cnt[key] > 0:
            deps.append((key, self.cnt[key]))
        for d in deps:
            self._wait(q, d)
        self.cnt[key] += 16
        ev = (key, self.cnt[key])
        fn(self.eh[q]).then_inc(self.sems[key], 16)
        for b in reads:
            b.r[key] = ev[1]
        for b in writes:
            b.w = ev
            b.r = {}
        self.n_ops += 1
        return ev

    def barrier(self):
        for eng in self.streams:
            for key, c in self.cnt.items():
                if c > 0 and key != eng:
                    self._wait(eng, (key, c))
        for b in self.dbufs.values():
            b.w = None
            b.r = {}

    def finish(self):
        self.barrier()
        self.stack.close()


def _rel_bucket_np(dist):
    dist = np.asarray(dist, np.int64)
    exact = 16
    df = np.maximum(dist.astype(np.float32), np.float32(1.0))
    large = exact + (np.log(df / np.float32(exact)) / np.float32(math.log(2048 / exact))
                     * np.float32(32 - exact)).astype(np.int32)
    large = np.minimum(large, 31)
    return np.where(dist < exact, dist, large).astype(np.int64)


def make_consts(NE=1, NPOOL=1):
    c = {}
    c["iotab"] = (np.arange(128)[:, None] + np.arange(NE)[None, :] * (NPOOL * 128)).astype(np.float32)
    c["ident"] = np.eye(128, dtype=np.float32)
    c["ones"] = np.ones((128, 128), np.float32)
    j = np.arange(128)[:, None]
    s = np.arange(128)[None, :]
    c["utri"] = (j >= s).astype(np.float32)
    c["nutri"] = -c["utri"]
    c["nones"] = -np.ones((128, 128), np.float32)
    c["flip"] = (j == 127 - s).astype(np.float32)
    t = np.arange(512)[None, :]
    c["sbmask"] = np.stack([((128 * jb + np.arange(128)[:, None]) < t).astype(np.float32)
                            for jb in (3, 2, 1, 0)], 0)
    oh = np.zeros((32, 3 * 129), np.float32)
    ohrev = np.zeros((32, 3 * 128), np.float32)
    for br, (win, dil) in enumerate(DIL_PAIRS):
        bk = _rel_bucket_np(np.arange(129) * dil)
        for jj in range(129):
            oh[bk[jj], br * 129 + jj] = 1.0
        for i in range(128):
            ohrev[bk[128 - i], br * 128 + i] = 1.0
    c["oh"] = oh
    c["ohrev"] = ohrev
    ohnew = np.zeros((32, 4 * 3 * 4), np.float32)
    negnew = np.full((4, 4 * 3), NEG, np.float32)
    for tq in range(4):
        for br, (win, dil) in enumerate(DIL_PAIRS):
            for tk in range(4):
                d = tq - tk
                if d >= 0 and d % dil == 0 and d // dil <= win // dil:
                    ohnew[_rel_bucket_np(d), (tq * 3 + br) * 4 + tk] = 1.0
                    negnew[tk, tq * 3 + br] = 0.0
    c["ohnew"] = ohnew
    c["negnew"] = negnew
    c["snew"] = (np.arange(4)[:, None] < np.arange(4)[None, :]).astype(np.float32)
    return c


class Prog:
    def __init__(self, T, NS, NPAGES, NPOOL, DEPTH, stop_after=None):
        self.T, self.NS, self.NPAGES, self.NPOOL, self.DEPTH = T, NS, NPAGES, NPOOL, DEPTH
        self.NTS = NS * 4
        self.TT = T + self.NTS
        self.NE = (DEPTH + 1) // 2
        self.NO = DEPTH // 2
        self.KEEP = min(2048, T)
        self.stop_after = stop_after
        nc = self.nc = bass.Bass("TRN2", target_bir_lowering=False)
        self.b = Builder(nc)
        self.din = {}
        self.dout = {}
        self.declare()

    def _in(self, name, shape, dtype=F32):
        self.din[name] = self.nc.dram_tensor(name, list(shape), dtype, kind="ExternalInput").ap()
        return self.din[name]

    def _out(self, name, shape, dtype=F32):
        self.dout[name] = self.nc.dram_tensor(name, list(shape), dtype, kind="ExternalOutput").ap()
        return self.dout[name]

    def _scr(self, name, shape, dtype=F32):
        return self.nc.dram_tensor(name, list(shape), dtype, kind="Internal").ap()

    def declare(self):
        T, NS, NTS, TT, NE, NO, D = self.T, self.NS, self.NTS, self.TT, self.NE, self.NO, self.DEPTH
        i, o, s = self._in, self._out, self._scr
        i("xp", [2, T, 1024]); i("xs", [NTS, 1024])
        i("poolk", [NE * self.NPOOL * 128, 512]); i("poolv", [NE * self.NPOOL * 128, 512])
        i("pt", [NS, self.NPAGES], I32)
        i("sconv", [NE, NS, 3, 512]); i("slru", [NE, NS, 512])
        i("cdk", [max(NO, 1), NS, CBUF, 1024]); i("cdv", [max(NO, 1), NS, CBUF, 1024])
        i("relb", [32, 16]); i("nmix", [D, 1024]); i("nffn", [D, 1024]); i("nfin", [1, 1024])
        i("w_in_ab", [NE, 1024, 2560]); i("sbb", [NE, 8]); i("convw", [NE, 4, 512]); i("convb", [NE, 512])
        i("wa", [NE, 8, 64, 64]); i("ba", [NE, 512]); i("wx", [NE, 8, 64, 64]); i("bx", [NE, 512])
        i("lam", [NE, 512]); i("w_out_ab", [NE, 1024, 1024])
        i("w_in_c", [max(NO, 1), 1024, 3072]); i("w_out_c", [max(NO, 1), 1024, 1024])
        i("w_ff1", [D, 1024, 4096]); i("w_ff2", [D, 4096, 1024])
        for k, v in make_consts(NE, self.NPOOL).items():
            i("c_" + k, v.shape)
        Q = T // 4
        o("y_p", [Q, 1024]); o("y_s", [NTS, 1024])
        o("sbk_p", [NE, Q, 512]); o("sbv_p", [NE, Q, 512])
        o("sbk_s", [NE, NTS, 512]); o("sbv_s", [NE, NTS, 512])
        o("conv_p", [NE, 3, 512]); o("conv_s", [NE, NS, 3, 512])
        o("lru_p", [NE, 512]); o("lru_s", [NE, NS, 512])
        o("dk_p", [max(NO, 1), self.KEEP // 4, 1024]); o("dv_p", [max(NO, 1), self.KEEP // 4, 1024])
        o("dk_s", [max(NO, 1), NTS, 1024]); o("dv_s", [max(NO, 1), NTS, 1024])
        self.XT = s("XT", [8, 128, TT])
        self.QT = s("QT", [8, 128, TT], BF16)
        self.KT = s("KT", [8, 128, TT], BF16)
        self.V16 = s("V16", [8, TT, 128], BF16)
        self.AT = s("AT", [8, 128, TT], BF16)
        Q4, K4 = T // 4, self.KEEP // 4
        self.SBK4 = s("SBK", [NE, 4, Q4, 512]); self.SBV4 = s("SBV", [NE, 4, Q4, 512])
        self.DKS4 = s("DKS", [max(NO, 1), 4, K4, 1024]); self.DVS4 = s("DVS", [max(NO, 1), 4, K4, 1024])
        self.YP4 = s("YP", [4, Q4, 1024])
        self.SBK = self.SBK4.rearrange("e a r f -> e (a r) f"); self.SBV = self.SBV4.rearrange("e a r f -> e (a r) f")
        self.DKS = self.DKS4.rearrange("e a r f -> e (a r) f"); self.DVS = self.DVS4.rearrange("e a r f -> e (a r) f")
        self.YP = self.YP4.rearrange("a r f -> (a r) f")
        self.ZS = s("ZS", [NTS, 3072])
        self.AS = s("AS", [NTS, 1024])
        self.TAB = s("TAB", [16, 3, 384])
        self.BTD = s("BTD", [3, 16, 2, 128, 128])
        self.WB = {}
        for nm, L, K, N in (("w_in_ab", NE, 1024, 2560), ("w_out_ab", NE, 1024, 1024),
                            ("w_in_c", NO, 1024, 3072), ("w_out_c", NO, 1024, 1024),
                            ("w_ff1", D, 1024, 4096), ("w_ff2", D, 4096, 1024)):
            if L > 0:
                self.WB[nm] = s("WB_" + nm, [L, K, N], BF16)

    def setup(self):
        b, g = self.b, {}
        self.g = g
        self.bg = {}

        def ctile(name, shape, dtype, src, q="sp", **kw):
            t = b.sb(name, shape, dtype)
            bf = Buf(name)
            b.dma(q, "const", lambda e: e.dma_start(out=t[:], in_=src, **kw), writes=[bf])
            g[name] = t
            self.bg[name] = bf
            return t

        d = self.din
        ctile("ident", [128, 128], F32, d["c_ident"])
        ctile("ones32", [128, 128], F32, d["c_ones"])
        ctile("onesb", [128, 128], BF16, d["c_ones"], q="pool")
        ctile("utrib", [128, 128], BF16, d["c_utri"], q="pool")
        ctile("nutrib", [128, 128], BF16, d["c_nutri"], q="pool")
        ctile("nonesb", [128, 128], BF16, d["c_nones"], q="pool")
        ctile("flip", [128, 128], F32, d["c_flip"])
        ctile("sbmask", [128, 4, 512], BF16, d["c_sbmask"].rearrange("j p t -> p j t"), q="pool")
        ctile("snew", [4, 4], F32, d["c_snew"])
        D = self.DEPTH
        for nm, src, L in (("gmix", d["nmix"], D), ("gffn", d["nffn"], D), ("gfin", d["nfin"], 1)):
            t = b.sb(nm, [128, L, 8], F32)
            bf = Buf(nm)
            for l in range(L):
                b.dma("sp", "const", lambda e: e.dma_start(out=t[:, l, :], in_=src[l].rearrange("(kc p) -> p kc", p=128),
                                                           allow_slow_non_contiguous=True), writes=[bf])
            g[nm] = t
            self.bg[nm] = bf
        cc = b.sb("ccol", [128, 4], F32)
        bc = Buf("ccol")
        b.op("dve", lambda e: e.memset(cc[:, 0:1], 1e-6), writes=[bc])
        b.op("dve", lambda e: e.memset(cc[:, 1:2], 1.0), writes=[bc])
        b.op("dve", lambda e: e.memset(cc[:, 2:4], 0.0), writes=[bc])
        g["ccol"] = cc
        self.bg["ccol"] = bc
        self.PS = [b.ps("ps%d" % i, [128, 1024], F32) for i in range(4)]
        self.PSB = [Buf("ps%d" % i, psum=True) for i in range(4)]
        self.pid = self.nc.gpsimd.partition_id()

    def convert_weights(self):
        b = self.b
        with contextlib.ExitStack() as st:
            tiles = [b.sb("wcv", [128, 4096], BF16, st) for _ in range(4)]
            bufs = [Buf("wcv%d" % i) for i in range(4)]
            n = 0
            for nm, wb in self.WB.items():
                src = self.din[nm]
                L, K, N = src.shape
                for l in range(L):
                    for kt in range(K // 128):
                        t, bf = tiles[n % 4], bufs[n % 4]
                        sl = src[l, kt * 128:(kt + 1) * 128, :]
                        b.dma("pool", "wcv_l%d" % (n % 4),
                              lambda e: e.dma_start(out=t[:, 0:N], in_=sl, max_dma_last_dim=4096),
                              writes=[bf])
                        b.dma("sp", "wcv_s%d" % (n % 4),
                              lambda e: e.dma_start(out=wb[l, kt * 128:(kt + 1) * 128, :], in_=t[:, 0:N]),
                              reads=[bf], writes=[self.b.dbuf(("WB", nm, l))])
                        n += 1
            b.barrier()

    def tiles(self):
        T = self.T
        res = [(i * 512, 512, False) for i in range(T // 512)]
        res.append((T, self.NTS, True))
        return res

    def xt_buf(self, c0):
        return self.b.dbuf(("XT", c0))

    def phase0(self):
        b, g, PS, PSB = self.b, self.g, self.PS, self.PSB
        d = self.din
        with contextlib.ExitStack() as st:
            xtm = b.sb("xtm", [128, 4, 1024], F32, st)
            bx = Buf("xtm")
            xT = b.sb("xT0", [128, 8, 512], F32, st)
            bxT = Buf("xT0")
            pseq = bass.ds(self.pid % 2, 1)
            for (c0, W, samp) in self.tiles():
                if not samp:
                    src = d["xp"][pseq, c0:c0 + W, :].rearrange("o (s p) f -> p (o s) f", p=128)
                    b.dma("pool", "x0", lambda e: e.dma_start(out=xtm[:], in_=src), writes=[bx])
                    subs = [(s, 128) for s in range(4)]
                else:
                    b.dma("pool", "x0", lambda e: e.dma_start(out=xtm[0:W, 0, :], in_=d["xs"]), writes=[bx])
                    subs = [(0, W)]
                for kc in range(8):
                    pz, pb = PS[kc % 4], PSB[kc % 4]
                    for (s, rows) in subs:
                        b.op("pe", lambda e: e.transpose(out=pz[:, s * 128:s * 128 + rows],
                                                         in_=xtm[0:rows, s, kc * 128:(kc + 1) * 128],
                                                         identity=g["ident"][0:rows, 0:rows]),
                             reads=[bx, self.bg["ident"]], writes=[pb])
                    eng = "dve" if kc % 2 == 0 else "act"
                    if eng == "dve":
                        b.op("dve", lambda e: e.tensor_copy(out=xT[:, kc, 0:W], in_=pz[:, 0:W]), reads=[pb], writes=[bxT])
                    else:
                        b.op("act", lambda e: e.activation(out=xT[:, kc, 0:W], in_=pz[:, 0:W], func=AF.Copy),
                             reads=[pb], writes=[bxT])
                b.dma("sp", "xst", lambda e: e.dma_start(
                    out=self.XT[:, :, c0:c0 + W].rearrange("k p t -> p k t"), in_=xT[:, :, 0:W]),
                    reads=[bxT], writes=[self.xt_buf(c0)])
            b.barrier()

    def norm(self, xT, bxT, W, gain, uT, buT, sq, bsq, rs, brs, ps, pb):
        b, g = self.b, self.g
        b.op("act", lambda e: e.activation(out=sq[:, :, 0:W], in_=xT[:, :, 0:W], func=AF.Square),
             reads=[bxT], writes=[bsq])
        for kc in range(8):
            b.op("pe", lambda e: e.matmul(out=ps[:, 0:W], lhsT=g["onesb"][:], rhs=sq[:, kc, 0:W],
                                          start=(kc == 0), stop=(kc == 7)),
                 reads=[bsq, self.bg["onesb"]], writes=[pb])
        b.op("act", lambda e: e.activation(out=rs[:, 0:W], in_=ps[:, 0:W], func=AF.Ln,
                                           bias=g["ccol"][:, 0:1], scale=1.0 / 1024.0),
             reads=[pb, self.bg["ccol"]], writes=[brs])
        b.op("act", lambda e: e.activation(out=rs[:, 0:W], in_=rs[:, 0:W], func=AF.Exp, scale=-0.5),
             reads=[brs], writes=[brs])
        for kc in range(8):
            b.op("dve", lambda e: e.scalar_tensor_tensor(out=uT[:, kc, 0:W], in0=xT[:, kc, 0:W],
                                                         scalar=gain[:, kc:kc + 1], in1=rs[:, 0:W],
                                                         op0=ALU.mult, op1=ALU.mult),
                 reads=[bxT, brs, self.bg["gmix"], self.bg["gffn"], self.bg["gfin"]], writes=[buT])

    def wload(self, wt, bw, wb2d, k0, nk, c0, nc_, cls):
        src = wb2d[k0:k0 + nk * 128, c0:c0 + nc_].rearrange("(kc p) n -> p kc n", p=128)
        self.b.dma("sp", cls, lambda e: e.dma_start(out=wt[:, 0:nk, 0:nc_], in_=src), writes=[bw])

    def p1_even(self, li):
        b, g, PS, PSB, d = self.b, self.g, self.PS, self.PSB, self.din
        ie = li // 2
        T, NS, NTS = self.T, self.NS, self.NTS
        wb = self.WB["w_in_ab"][ie]
        with contextlib.ExitStack() as st:
            sb = lambda n, s_, dt: b.sb(n, s_, dt, st)
            xT = sb("xT", [128, 8, 512], F32); bxT = Buf()
            sq = sb("sq", [128, 8, 512], BF16); bsq = Buf()
            rs = sb("rs", [128, 512], F32); brs = Buf()
            uT = sb("uT", [128, 8, 512], BF16); buT = Buf()
            wts = [sb("wt", [128, 8, 512], BF16) for _ in range(4)]; bws = [Buf() for _ in range(4)]
            st16 = [sb("st16", [128, 4, 512], BF16) for _ in range(2)]; bst16 = [Buf(), Buf()]
            tm32 = [sb("tm32", [128, 512], F32) for _ in range(2)]; btm32 = [Buf(), Buf()]
            tm16 = [sb("tm16", [128, 512], BF16) for _ in range(2)]; btm16 = [Buf(), Buf()]
            XR = sb("XR", [128, 4, 515], F32); bXR = Buf()
            XRs = sb("XRs", [128, 4, NS, 7], F32); bXRs = Buf()
            gb = sb("gb", [128, 4, 512], F32); bgb = Buf()
            hst = sb("hst", [128, 4], F32); bhst = Buf()
            hss = sb("hss", [128, 4, NS], F32); bhss = Buf()
            xc = sb("xc", [128, 512], F32); bxc = Buf()
            xc16 = sb("xc16", [128, 512], BF16); bxc16 = Buf()
            tA = sb("tA", [128, 512], F32); btA = Buf()
            tB = sb("tB", [128, 512], F32); btB = Buf()
            tC = sb("tC", [128, 512], F32); btC = Buf()
            tD = sb("tD", [128, 512], F32); btD = Buf()
            hh = sb("hh", [128, 512], F32); bhh = Buf()
            lt16 = sb("lt16", [128, 4, 512], BF16); blt = Buf()
            prm = sb("prm", [128, 4, 12], F32); bprm = Buf()
            BD = sb("BD", [128, 2, 4, 128], BF16); bBD = Buf()
            fm = lambda ap2: ap2.rearrange("(j p) -> p j", p=128)
            b.op("pool", lambda e: e.memset(BD[:], 0.0), writes=[bBD])
            for k in range(4):
                b.dma("sp", "prm", lambda e: e.dma_start(out=prm[:, :, k], in_=fm(d["convw"][ie, k]),
                                                         allow_slow_non_contiguous=True), writes=[bprm])
            for k, nm in ((4, "convb"), (5, "ba"), (6, "bx"), (7, "lam")):
                b.dma("sp", "prm", lambda e: e.dma_start(out=prm[:, :, k], in_=fm(d[nm][ie]),
                                                         allow_slow_non_contiguous=True), writes=[bprm])
            for wi, nm in enumerate(("wa", "wx")):
                for n in range(8):
                    j, h = n // 2, n % 2
                    b.dma("pool", "bd", lambda e: e.dma_start(
                        out=BD[h * 64:(h + 1) * 64, wi, j, h * 64:(h + 1) * 64], in_=d[nm][ie, n]),
                        writes=[bBD])
            b.op("act", lambda e: e.activation(out=prm[:, :, 8], in_=prm[:, :, 7], func=AF.Exp, scale=-1.0),
                 reads=[bprm], writes=[bprm])
            b.op("act", lambda e: e.activation(out=prm[:, :, 8], in_=prm[:, :, 8], func=AF.Ln,
                                               bias=g["ccol"][:, 1:2], scale=1.0),
                 reads=[bprm, self.bg["ccol"]], writes=[bprm])
            b.op("dve", lambda e: e.tensor_scalar(out=prm[:, :, 8], in0=prm[:, :, 8], scalar1=-8.0, scalar2=None,
                                                  op0=ALU.mult), reads=[bprm], writes=[bprm])
            b.op("dve", lambda e: e.memset(XR[:, :, 0:3], 0.0), writes=[bXR])
            b.op("dve", lambda e: e.memset(hst[:], 0.0), writes=[bhst])
            for j in range(4):
                for sg in range(NS):
                    b.dma("sp", "prm", lambda e: e.dma_start(
                        out=XRs[:, j, sg, 0:3], in_=d["sconv"][ie, sg, :, j * 128:(j + 1) * 128].rearrange("r p -> p r"),
                        allow_slow_non_contiguous=True), writes=[bXRs])
                b.dma("sp", "prm", lambda e: e.dma_start(
                    out=hss[:, j, :], in_=d["slru"][ie][:, j * 128:(j + 1) * 128].rearrange("s p -> p s"),
                    allow_slow_non_contiguous=True), writes=[bhss])
            nw = 0
            tl = self.tiles()
            import os
            DBG = int(os.environ.get("P1STOP", "99"))
            SKIP = int(os.environ.get("P1SKIP", "0"))
            if DBG <= 1:
                b.barrier(); return
            for ti, (c0, W, samp) in enumerate(tl):
                b.dma("sp", "xld", lambda e: e.dma_start(
                    out=xT[:, :, 0:W], in_=self.XT[:, :, c0:c0 + W].rearrange("k p t -> p k t")),
                    reads=[self.xt_buf(c0)], writes=[bxT])
                self.norm(xT, bxT, W, g["gmix"][:, li, :], uT, buT, sq, bsq, rs, brs, PS[0], PSB[0])
                if DBG <= 2:
                    continue
                subs = [(0, W)] if samp else [(s * 128, 128) for s in range(4)]
                for grp in range(5):
                    wt, bw = wts[nw % 4], bws[nw % 4]
                    self.wload(wt, bw, wb, 0, 8, grp * 512, 512, "w%d" % (nw % 4))
                    nw += 1
                    if (SKIP & 8) and samp:
                        continue
                    if grp in (0, 1, 3, 4) and not (SKIP & 1):
                        so, bso = st16[grp % 2], bst16[grp % 2]
                        for j in range(4):
                            pz, pb = PS[1 + (j % 2)], PSB[1 + (j % 2)]
                            for kc in range(8):
                                b.op("pe", lambda e: e.matmul(out=pz[:, 0:W], lhsT=wt[:, kc, j * 128:(j + 1) * 128],
                                                              rhs=uT[:, kc, 0:W], start=(kc == 0), stop=(kc == 7)),
                                     reads=[bw, buT], writes=[pb])
                            if grp == 0:
                                b.op("act", lambda e: e.activation(out=so[:, j, 0:W], in_=pz[:, 0:W],
                                                                   func=AF.Copy, scale=0.125),
                                     reads=[pb], writes=[bso])
                            elif grp == 1:
                                b.op("act", lambda e: e.activation(out=so[:, j, 0:W], in_=pz[:, 0:W], func=AF.Copy),
                                     reads=[pb], writes=[bso])
                            elif grp == 3:
                                if not samp:
                                    b.op("dve", lambda e: e.tensor_copy(out=XR[:, j, 3:3 + W], in_=pz[:, 0:W]),
                                         reads=[pb], writes=[bXR])
                                else:
                                    b.op("dve", lambda e: e.tensor_copy(
                                        out=XRs[:, j, :, 3:7], in_=pz[:, 0:W].rearrange("p (s t) -> p s t", t=4)),
                                        reads=[pb], writes=[bXRs])
                            else:
                                b.op("act", lambda e: e.activation(out=gb[:, j, 0:W], in_=pz[:, 0:W], func=AF.Copy),
                                     reads=[pb], writes=[bgb])
                        if grp in (0, 1):
                            dst = self.QT if grp == 0 else self.KT
                            b.dma("sp", "qk%d" % grp, lambda e: e.dma_start(
                                out=dst[0:4, :, c0:c0 + W].rearrange("k p t -> p k t"), in_=so[:, :, 0:W]),
                                reads=[bso], writes=[b.dbuf(("QT" if grp == 0 else "KT", c0))])
                    if (grp in (1, 2) or (samp and grp == 0)) and not (SKIP & 2):
                        for si, (s0, rows) in enumerate(subs):
                            pz, pb = PS[3], PSB[3]
                            for kc in range(8):
                                b.op("pe", lambda e: e.matmul(out=pz[0:rows, 0:512], lhsT=uT[:, kc, s0:s0 + rows],
                                                              rhs=wt[:, kc, :], start=(kc == 0), stop=(kc == 7)),
                                     reads=[bw, buT], writes=[pb])
                            t32, bt32 = tm32[si % 2], btm32[si % 2]
                            b.op("dve", lambda e: e.tensor_copy(out=t32[0:rows, :], in_=pz[0:rows, 0:512]),
                                 reads=[pb], writes=[bt32])
                            if samp:
                                if grp == 0:
                                    dstap, dkey = self.ZS[:, 0:512], ("ZS", 0)
                                else:
                                    nm = "sbk_s" if grp == 1 else "sbv_s"
                                    dstap, dkey = self.dout[nm][ie], (nm, ie)
                            else:
                                dstap = (self.SBK if grp == 1 else self.SBV)[ie, c0 + s0:c0 + s0 + rows, :]
                                dkey = ("SBK" if grp == 1 else "SBV", ie, c0)
                            b.dma("sp", "tm%d" % (si % 2), lambda e: e.dma_start(out=dstap, in_=t32[0:rows, :]),
                                  reads=[bt32], writes=[b.dbuf(dkey)])
                            if grp == 2 and not samp and not (SKIP & 4):
                                t16, bt16 = tm16[si % 2], btm16[si % 2]
                                b.op("act", lambda e: e.activation(out=t16[0:rows, :], in_=pz[0:rows, 0:512],
                                                                   func=AF.Copy), reads=[pb], writes=[bt16])
                                for hp in range(4):
                                    b.dma("sp", "tv%d" % (si % 2), lambda e: e.dma_start(
                                        out=self.V16[hp, c0 + s0:c0 + s0 + rows, :],
                                        in_=t16[0:rows, hp * 128:(hp + 1) * 128]),
                                        reads=[bt16], writes=[b.dbuf(("V16", c0))])
                if DBG <= 3:
                    continue
                nseg, L = (NS, 4) if samp else (1, 512)
                for j in range(4):
                    Xv = XRs[:, j, :, :] if samp else XR[:, j, :].rearrange("p (s c) -> p s c", s=1)
                    bX = bXRs if samp else bXR
                    xc3 = xc[:, 0:W].rearrange("p (s t) -> p s t", s=nseg)
                    b.op("dve", lambda e: e.tensor_scalar(out=xc3, in0=Xv[:, :, 0:L], scalar1=prm[:, j, 0:1],
                                                          scalar2=prm[:, j, 4:5], op0=ALU.mult, op1=ALU.add),
                         reads=[bX, bprm], writes=[bxc])
                    for k in range(1, 4):
                        b.op("dve", lambda e: e.scalar_tensor_tensor(out=xc3, in0=Xv[:, :, k:k + L],
                                                                     scalar=prm[:, j, k:k + 1], in1=xc3,
                                                                     op0=ALU.mult, op1=ALU.add),
                             reads=[bX, bprm, bxc], writes=[bxc])
                    b.op("act", lambda e: e.activation(out=xc16[:, 0:W], in_=xc[:, 0:W], func=AF.Copy),
                         reads=[bxc], writes=[bxc16])
                    pz, pb = PS[1 + (j % 2)], PSB[1 + (j % 2)]
                    for wi in range(2):
                        b.op("pe", lambda e: e.matmul(out=pz[:, wi * 512:wi * 512 + W], lhsT=BD[:, wi, j, :],
                                                      rhs=xc16[:, 0:W], start=True, stop=True),
                             reads=[bBD, bxc16], writes=[pb])
                    b.op("act", lambda e: e.activation(out=tA[:, 0:W], in_=pz[:, 0:W], func=AF.Sigmoid,
                                                       bias=prm[:, j, 5:6], scale=1.0), reads=[pb, bprm], writes=[btA])
                    b.op("act", lambda e: e.activation(out=tB[:, 0:W], in_=pz[:, 512:512 + W], func=AF.Sigmoid,
                                                       bias=prm[:, j, 6:7], scale=1.0), reads=[pb, bprm], writes=[btB])
                    b.op("act", lambda e: e.activation(out=tA[:, 0:W], in_=tA[:, 0:W], func=AF.Exp,
                                                       scale=prm[:, j, 8:9]), reads=[btA, bprm], writes=[btA])
                    b.op("dve", lambda e: e.tensor_tensor(out=tC[:, 0:W], in0=tA[:, 0:W], in1=tA[:, 0:W], op=ALU.mult),
                         reads=[btA], writes=[btC])
                    b.op("dve", lambda e: e.tensor_scalar(out=tC[:, 0:W], in0=tC[:, 0:W], scalar1=-1.0, scalar2=1.0,
                                                          op0=ALU.mult, op1=ALU.add), reads=[btC], writes=[btC])
                    b.op("dve", lambda e: e.tensor_scalar(out=tC[:, 0:W], in0=tC[:, 0:W], scalar1=1e-30, scalar2=None,
                                                          op0=ALU.max), reads=[btC], writes=[btC])
                    b.op("act", lambda e: e.activation(out=tC[:, 0:W], in_=tC[:, 0:W], func=AF.Sqrt),
                         reads=[btC], writes=[btC])
                    b.op("dve", lambda e: e.tensor_tensor(out=tB[:, 0:W], in0=tB[:, 0:W], in1=xc[:, 0:W], op=ALU.mult),
                         reads=[btB, bxc], writes=[btB])
                    b.op("dve", lambda e: e.tensor_tensor(out=tB[:, 0:W], in0=tB[:, 0:W], in1=tC[:, 0:W], op=ALU.mult),
                         reads=[btB, btC], writes=[btB])
                    for sg in range(nseg):
                        init = hss[:, j, sg:sg + 1] if samp else hst[:, j:j + 1]
                        bi_ = bhss if samp else bhst
                        b.op("dve", lambda e: e.tensor_tensor_scan(out=hh[:, sg * L:(sg + 1) * L],
                                                                   data0=tA[:, sg * L:(sg + 1) * L],
                                                                   data1=tB[:, sg * L:(sg + 1) * L],
                                                                   initial=init, op0=ALU.mult, op1=ALU.add),
                             reads=[btA, btB, bi_], writes=[bhh])
                        b.op("dve", lambda e: e.tensor_copy(out=init, in_=hh[:, (sg + 1) * L - 1:(sg + 1) * L]),
                             reads=[bhh], writes=[bi_])
                    gj = gb[:, j, 0:W]
                    b.op("pool", lambda e: e.tensor_tensor(out=tD[:, 0:W], in0=gj, in1=gj, op=ALU.mult),
                         reads=[bgb], writes=[btD])
                    b.op("pool", lambda e: e.tensor_scalar(out=tD[:, 0:W], in0=tD[:, 0:W], scalar1=0.044715 * 0.7978845608028654,
                                                           scalar2=0.7978845608028654, op0=ALU.mult, op1=ALU.add),
                         reads=[btD], writes=[btD])
                    b.op("pool", lambda e: e.tensor_tensor(out=tD[:, 0:W], in0=tD[:, 0:W], in1=gj, op=ALU.mult),
                         reads=[btD, bgb], writes=[btD])
                    b.op("act", lambda e: e.activation(out=tD[:, 0:W], in_=tD[:, 0:W], func=AF.Tanh),
                         reads=[btD], writes=[btD])
                    b.op("pool", lambda e: e.tensor_scalar(out=tD[:, 0:W], in0=tD[:, 0:W], scalar1=1.0, scalar2=0.5,
                                                           op0=ALU.add, op1=ALU.mult), reads=[btD], writes=[btD])
                    b.op("pool", lambda e: e.tensor_tensor(out=tD[:, 0:W], in0=tD[:, 0:W], in1=gj, op=ALU.mult),
                         reads=[btD, bgb], writes=[btD])
                    b.op("dve", lambda e: e.tensor_tensor(out=lt16[:, j, 0:W], in0=tD[:, 0:W], in1=hh[:, 0:W], op=ALU.mult),
                         reads=[btD, bhh], writes=[blt])
                    if not samp:
                        b.op("dve", lambda e: e.tensor_copy(out=XR[:, j, 0:3], in_=XR[:, j, 512:515]),
                             reads=[bXR], writes=[bXR])
                b.dma("sp", "lt", lambda e: e.dma_start(
                    out=self.AT[4:8, :, c0:c0 + W].rearrange("k p t -> p k t"), in_=lt16[:, :, 0:W]),
                    reads=[blt], writes=[b.dbuf(("ATl", c0))])
                if not samp and ti == len(tl) - 2:
                    for j in range(4):
                        b.dma("sp", "so", lambda e: e.dma_start(
                            out=self.dout["conv_p"][ie][:, j * 128:(j + 1) * 128].rearrange("r p -> p r"),
                            in_=XR[:, j, 0:3], allow_slow_non_contiguous=True),
                            reads=[bXR], writes=[b.dbuf(("conv_p", ie))])
                    b.dma("sp", "so", lambda e: e.dma_start(
                        out=self.dout["lru_p"][ie].rearrange("(j p) -> p j", p=128), in_=hst[:],
                        allow_slow_non_contiguous=True), reads=[bhst], writes=[b.dbuf(("lru_p", ie))])
                if samp:
                    for j in range(4):
                        for sg in range(NS):
                            b.dma("sp", "so", lambda e: e.dma_start(
                                out=self.dout["conv_s"][ie, sg, :, j * 128:(j + 1) * 128].rearrange("r p -> p r"),
                                in_=XRs[:, j, sg, 4:7], allow_slow_non_contiguous=True),
                                reads=[bXRs], writes=[b.dbuf(("conv_s", ie))])
                        b.dma("sp", "so", lambda e: e.dma_start(
                            out=self.dout["lru_s"][ie][:, j * 128:(j + 1) * 128].rearrange("s p -> p s"),
                            in_=hss[:, j, :], allow_slow_non_contiguous=True),
                            reads=[bhss], writes=[b.dbuf(("lru_s", ie))])
            b.barrier()

    def export(self):
        b = self.b
        Q = self.T // 4
        q = self.nc.sync.partition_id() // 2
        qs = bass.ds(q, 1)
        KQ = self.KEEP // 4
        done = self.done

        def chunks(n, step=512):
            return [(o, min(step, n - o)) for o in range(0, n, step)]

        for ie in range(self.NE):
            if ("p1", 2 * ie) not in done:
                continue
            for nm, src in (("sbk_p", self.SBK4), ("sbv_p", self.SBV4)):
                for (o, n) in chunks(Q):
                    b.dma("sp", "exp", lambda e: e.dma_start(out=self.dout[nm][ie:ie + 1, o:o + n, :],
                                                               in_=src[ie, qs, o:o + n, :]),
                          reads=[], writes=[b.dbuf(("o", nm, ie))])
        for io in range(self.NO):
            if ("p1", 2 * io + 1) not in done:
                continue
            for nm, src in (("dk_p", self.DKS4), ("dv_p", self.DVS4)):
                for (o, n) in chunks(KQ, 256):
                    b.dma("sp", "exp", lambda e: e.dma_start(out=self.dout[nm][io:io + 1, o:o + n, :],
                                                               in_=src[io, qs, o:o + n, :]),
                          reads=[], writes=[b.dbuf(("o", nm, io))])
        if ("p3", self.DEPTH - 1) in done:
            for (o, n) in chunks(Q, 256):
                b.dma("sp", "exp", lambda e: e.dma_start(out=self.dout["y_p"][o:o + n, :].unsqueeze(0),
                                                           in_=self.YP4[qs, o:o + n, :]),
                      reads=[], writes=[b.dbuf(("o", "y_p"))])

    def build(self):
        self.done = set()
        self.setup()
        if self.stop_after != "setup":
            self.convert_weights()
        if self.stop_after not in ("setup", "wcv"):
            self.phase0()
            if self.NO > 0:
                self.bias_tables()
        done = self.stop_after in ("setup", "wcv", "p0")
        for li in range(0 if not done else self.DEPTH, self.DEPTH):
            for ph in ("p1", "p2", "p3"):
                name = "%s_%d" % (ph, li)
                fn = getattr(self, "%s_%s" % (ph, "even" if li % 2 == 0 else "odd"), None)
                if fn is not None:
                    fn(li)
                    self.done.add((ph, li))
                if self.stop_after == name:
                    done = True
                    break
            if done:
                break
        self.b.barrier()
        self.export()
        self.b.finish()
        return self.nc


_PROG_CACHE = {}


def run(inputs, stop_after=None):
    f32 = lambda a: np.ascontiguousarray(np.asarray(a, dtype=np.float32))
    xpr = f32(inputs["x_prompt"])
    B, T, DM = xpr.shape
    xs = f32(inputs["x_sample"])
    DB = xs.shape[0]
    NS = DB // NCORES
    ptab = np.ascontiguousarray(np.asarray(inputs["page_table"], dtype=np.int32))
    NPAGES = ptab.shape[1]
    ck = f32(inputs["cache_sb_k"])
    NE, NPOOL = ck.shape[0], ck.shape[1]
    DEPTH = int(np.asarray(inputs["norm_mix"]).shape[0])
    NO = DEPTH // 2
    key = (T, NS, NPAGES, NPOOL, DEPTH, stop_after)
    if key not in _PROG_CACHE:
        _PROG_CACHE[key] = Prog(T, NS, NPAGES, NPOOL, DEPTH, stop_after)
        _PROG_CACHE[key].build()
    prog = _PROG_CACHE[key]
    rep = {
        "xp": xpr,
        "poolk": ck.reshape(NE * NPOOL * 128, 512),
        "poolv": f32(inputs["cache_sb_v"]).reshape(NE * NPOOL * 128, 512),
        "relb": f32(inputs["rel_bias"]), "nmix": f32(inputs["norm_mix"]), "nffn": f32(inputs["norm_ffn"]),
        "nfin": f32(inputs["norm_final"]).reshape(1, 1024),
        "w_in_ab": f32(inputs["w_in_ab"]), "sbb": f32(inputs["sb_bias"]), "convw": f32(inputs["conv_w"]),
        "convb": f32(inputs["conv_b"]), "wa": f32(inputs["lru_wa"]), "ba": f32(inputs["lru_ba"]),
        "wx": f32(inputs["lru_wx"]), "bx": f32(inputs["lru_bx"]), "lam": f32(inputs["lru_lambda"]),
        "w_out_ab": f32(inputs["w_out_ab"]), "w_in_c": f32(inputs["w_in_c"]), "w_out_c": f32(inputs["w_out_c"]),
        "w_ff1": f32(inputs["w_ff1"]), "w_ff2": f32(inputs["w_ff2"]),
    }
    for k, v in make_consts(NE, NPOOL).items():
        rep["c_" + k] = v
    cdk = f32(inputs["cache_dil_k"]).reshape(max(NO, 1), DB, CBUF, 1024)
    cdv = f32(inputs["cache_dil_v"]).reshape(max(NO, 1), DB, CBUF, 1024)
    sconv = f32(inputs["state_conv"])
    slru = f32(inputs["state_lru"])
    in_maps = []
    for c in range(NCORES):
        sl = slice(c * NS, (c + 1) * NS)
        m = dict(rep)
        m["xs"] = np.ascontiguousarray(xs[sl].reshape(NS * 4, 1024))
        m["pt"] = np.ascontiguousarray(ptab[sl])
        m["sconv"] = np.ascontiguousarray(sconv[:, sl])
        m["slru"] = np.ascontiguousarray(slru[:, sl])
        m["cdk"] = np.ascontiguousarray(cdk[:, sl])
        m["cdv"] = np.ascontiguousarray(cdv[:, sl])
        in_maps.append(m)
    res = run_bass_kernel_spmd(prog.nc, in_maps, core_ids=list(range(NCORES))).results
    Q = T // 4
    KEEP = min(2048, T)

    def prompt(nm, lead, rows, feat):
        out = np.zeros((lead, B, rows, feat), np.float32)
        qr = rows // 4
        for c in range(NCORES):
            bq, qq = c % 2, c // 2
            r = res[c][nm].reshape(lead, qr, feat)
            out[:, bq, qq * qr:(qq + 1) * qr] = r
        return out

    def sample(nm, shape_tail, lead=None):
        parts = [res[c][nm] for c in range(NCORES)]
        if lead is None:
            return np.concatenate([p.reshape((NS,) + shape_tail) for p in parts], 0)
        return np.concatenate([p.reshape((lead, NS) + shape_tail) for p in parts], 1)

    y_p = prompt("y_p", 1, T, 1024)[0]
    y_s = sample("y_s", (4, 1024))
    sbk_p = prompt("sbk_p", NE, T, 512).reshape(NE, B, T, 8, 64)
    sbv_p = prompt("sbv_p", NE, T, 512).reshape(NE, B, T, 8, 64)
    sbk_s = sample("sbk_s", (4, 8, 64), NE)
    sbv_s = sample("sbv_s", (4, 8, 64), NE)
    conv_p = np.stack([res[bq]["conv_p"] for bq in range(B)], 1)
    conv_s = sample("conv_s", (3, 512), NE)
    lru_p = np.stack([res[bq]["lru_p"] for bq in range(B)], 1)
    lru_s = sample("lru_s", (512,), NE)
    dk_p = prompt("dk_p", max(NO, 1), KEEP, 1024)[:NO].reshape(NO, B, KEEP, 16, 64)
    dv_p = prompt("dv_p", max(NO, 1), KEEP, 1024)[:NO].reshape(NO, B, KEEP, 16, 64)
    dk_s = sample("dk_s", (4, 16, 64), max(NO, 1))[:NO]
    dv_s = sample("dv_s", (4, 16, 64), max(NO, 1))[:NO]
    return (y_p, y_s, sbk_p, sbv_p, sbk_s, sbv_s, conv_p, conv_s, lru_p, lru_s, dk_p, dv_p, dk_s, dv_s)


def kernel(**inputs):
    return run(inputs)


def _p3(self, li):
    b, g, PS, PSB = self.b, self.g, self.PS, self.PSB
    even = li % 2 == 0
    i2 = li // 2
    wo = self.WB["w_out_ab" if even else "w_out_c"][i2]
    w1 = self.WB["w_ff1"][li]
    w2 = self.WB["w_ff2"][li]
    last = li == self.DEPTH - 1
    with contextlib.ExitStack() as st:
        sb = lambda n, s_, dt: b.sb(n, s_, dt, st)
        xT = sb("xT", [128, 8, 512], F32); bxT = Buf()
        cT = sb("cT", [128, 8, 512], BF16); bcT = Buf()
        sq = sb("sq", [128, 8, 512], BF16); bsq = Buf()
        rs = sb("rs", [128, 512], F32); brs = Buf()
        uT = sb("uT", [128, 8, 512], BF16); buT = Buf()
        hT = sb("hT", [128, 32, 512], BF16); bhT = Buf()
        wts = [sb("wt", [128, 8, 512], BF16) for _ in range(4)]; bws = [Buf() for _ in range(4)]
        sqv = [sb("sqv", [128, 512], F32) for _ in range(2)]; bsqv = [Buf(), Buf()]
        if last:
            yT = sb("yT", [128, 8, 512], F32); byT = Buf()
            ytm = [sb("ytm", [128, 1024], F32) for _ in range(2)]; bytm = [Buf(), Buf()]
        nw = 0
        for (c0, W, samp) in self.tiles():
            b.dma("sp", "xld", lambda e: e.dma_start(
                out=xT[:, :, 0:W], in_=self.XT[:, :, c0:c0 + W].rearrange("k p t -> p k t")),
                reads=[self.xt_buf(c0)], writes=[bxT])
            b.dma("sp", "cld", lambda e: e.dma_start(
                out=cT[:, :, 0:W], in_=self.AT[:, :, c0:c0 + W].rearrange("k p t -> p k t")),
                reads=[b.dbuf(("ATl", c0)), b.dbuf(("ATa", c0))], writes=[bcT])
            for grp in range(2):
                wt, bw = wts[nw % 4], bws[nw % 4]
                self.wload(wt, bw, wo, 0, 8, grp * 512, 512, "w%d" % (nw % 4)); nw += 1
                for j in range(4):
                    n = grp * 4 + j
                    pz, pb = PS[j % 4], PSB[j % 4]
                    for kc in range(8):
                        b.op("pe", lambda e: e.matmul(out=pz[:, 0:W], lhsT=wt[:, kc, j * 128:(j + 1) * 128],
                                                      rhs=cT[:, kc, 0:W], start=(kc == 0), stop=(kc == 7)),
                             reads=[bw, bcT], writes=[pb])
                    b.op("dve", lambda e: e.tensor_tensor(out=xT[:, n, 0:W], in0=pz[:, 0:W], in1=xT[:, n, 0:W], op=ALU.add),
                         reads=[pb, bxT], writes=[bxT])
            self.norm(xT, bxT, W, g["gffn"][:, li, :], uT, buT, sq, bsq, rs, brs, PS[0], PSB[0])
            for grp in range(8):
                wt, bw = wts[nw % 4], bws[nw % 4]
                self.wload(wt, bw, w1, 0, 8, grp * 512, 512, "w%d" % (nw % 4)); nw += 1
                for j in range(4):
                    n = grp * 4 + j
                    pz, pb = PS[j % 4], PSB[j % 4]
                    for kc in range(8):
                        b.op("pe", lambda e: e.matmul(out=pz[:, 0:W], lhsT=wt[:, kc, j * 128:(j + 1) * 128],
                                                      rhs=uT[:, kc, 0:W], start=(kc == 0), stop=(kc == 7)),
                             reads=[bw, buT], writes=[pb])
                    sv, bsv = sqv[j % 2], bsqv[j % 2]
                    b.op("act", lambda e: e.activation(out=sv[:, 0:W], in_=pz[:, 0:W], func=AF.Square),
                         reads=[pb], writes=[bsv])
                    b.op("dve", lambda e: e.scalar_tensor_tensor(out=hT[:, n, 0:W], in0=pz[:, 0:W], scalar=0.0,
                                                                 in1=sv[:, 0:W], op0=ALU.is_gt, op1=ALU.mult),
                         reads=[pb, bsv], writes=[bhT])
            for cg in range(2):
                for kq in range(4):
                    wt, bw = wts[nw % 4], bws[nw % 4]
                    self.wload(wt, bw, w2, kq * 1024, 8, cg * 512, 512, "w%d" % (nw % 4)); nw += 1
                    for j in range(4):
                        for kc in range(8):
                            b.op("pe", lambda e: e.matmul(out=PS[j][:, 0:W], lhsT=wt[:, kc, j * 128:(j + 1) * 128],
                                                          rhs=hT[:, kq * 8 + kc, 0:W],
                                                          start=(kq == 0 and kc == 0), stop=(kq == 3 and kc == 7)),
                                 reads=[bw, bhT], writes=[PSB[j]])
                for j in range(4):
                    n = cg * 4 + j
                    b.op("dve", lambda e: e.tensor_tensor(out=xT[:, n, 0:W], in0=PS[j][:, 0:W], in1=xT[:, n, 0:W], op=ALU.add),
                         reads=[PSB[j], bxT], writes=[bxT])
            if not last:
                b.dma("sp", "xst", lambda e: e.dma_start(
                    out=self.XT[:, :, c0:c0 + W].rearrange("k p t -> p k t"), in_=xT[:, :, 0:W]),
                    reads=[bxT], writes=[self.xt_buf(c0)])
            else:
                self.norm(xT, bxT, W, g["gfin"][:, 0, :], yT, byT, sq, bsq, rs, brs, PS[0], PSB[0])
                subs = [(0, W)] if samp else [(s * 128, 128) for s in range(4)]
                for si, (s0, rows) in enumerate(subs):
                    pz, pb = PS[1 + si % 2], PSB[1 + si % 2]
                    for kc in range(8):
                        b.op("pe", lambda e: e.transpose(out=pz[0:rows, kc * 128:(kc + 1) * 128],
                                                         in_=yT[:, kc, s0:s0 + rows], identity=g["ident"][:]),
                             reads=[byT, self.bg["ident"]], writes=[pb])
                    yt, byt = ytm[si % 2], bytm[si % 2]
                    b.op("act" if si % 2 else "dve",
                         (lambda e: e.activation(out=yt[0:rows, :], in_=pz[0:rows, :], func=AF.Copy)) if si % 2 else
                         (lambda e: e.tensor_copy(out=yt[0:rows, :], in_=pz[0:rows, :])),
                         reads=[pb], writes=[byt])
                    dst = self.dout["y_s"] if samp else self.YP[c0 + s0:c0 + s0 + rows, :]
                    b.dma("sp", "yst%d" % (si % 2), lambda e: e.dma_start(out=dst, in_=yt[0:rows, :]),
                          reads=[byt], writes=[b.dbuf(("Y", c0, s0))])
        b.barrier()


Prog.p3_even = _p3
Prog.p3_odd = _p3


def _p2_even(self, li):
    self.sb_prompt(li)
    self.sb_sample(li)


def _sb_prompt(self, li):
    b, g, PS, PSB, d = self.b, self.g, self.PS, self.PSB, self.din
    ie = li // 2
    T = self.T
    with contextlib.ExitStack() as st:
        sb = lambda n, s_, dt: b.sb(n, s_, dt, st)
        QTs = [sb("QTs", [128, 512], BF16) for _ in range(2)]; bQ = [Buf(), Buf()]
        KTs = [sb("KTs", [128, T], BF16) for _ in range(2)]; bK = [Buf(), Buf()]
        Vs = [sb("Vs", [128, T // 128, 128], BF16) for _ in range(2)]; bV = [Buf(), Buf()]
        E = [sb("E", [128, 1024], F32) for _ in range(2)]; bE = [Buf(), Buf()]
        SP = [sb("SP", [128, 1024], BF16) for _ in range(2)]; bSP = [Buf(), Buf()]
        SA = [sb("SA", [128, 512], F32) for _ in range(2)]; bSA = [Buf(), Buf()]
        SBt = [sb("SBt", [128, 512], F32) for _ in range(2)]; bSBt = [Buf(), Buf()]
        S16 = [sb("S16", [128, 1024], BF16) for _ in range(2)]; bS16 = [Buf(), Buf()]
        Wt = [sb("Wt", [128, 1024], BF16) for _ in range(2)]; bWt = [Buf(), Buf()]
        o16 = [sb("o16", [64, 512], BF16) for _ in range(2)]; bo16 = [Buf(), Buf()]
        sbias = sb("sbias", [128, 8], F32); bsb = Buf()
        b.dma("sp", "prm", lambda e: e.dma_start(out=sbias[:], in_=d["sbb"][ie:ie + 1, :].partition_broadcast(128)),
              writes=[bsb])
        nl = 0
        for ti in range(T // 512):
            c0 = ti * 512
            nblk = 4 * (ti + 1)
            for hp in range(4):
                q_, bq_ = QTs[nl % 2], bQ[nl % 2]
                k_, bk_ = KTs[nl % 2], bK[nl % 2]
                v_, bv_ = Vs[nl % 2], bV[nl % 2]
                cls = "a%d" % (nl % 2)
                nl += 1
                kdeps = [b.dbuf(("KT", cc * 512)) for cc in range(ti + 1)]
                vdeps = [b.dbuf(("V16", cc * 512)) for cc in range(ti + 1)]
                b.dma("sp", cls + "q", lambda e: e.dma_start(out=q_[:], in_=self.QT[hp, :, c0:c0 + 512]),
                      reads=[b.dbuf(("QT", c0))], writes=[bq_])
                b.dma("sp", cls + "k", lambda e: e.dma_start(out=k_[:, 0:nblk * 128], in_=self.KT[hp, :, 0:nblk * 128]),
                      reads=kdeps, writes=[bk_])
                b.dma("sp", cls + "v", lambda e: e.dma_start(
                    out=v_[:, 0:nblk, :], in_=self.V16[hp, 0:nblk * 128, :].rearrange("(k s) f -> s k f", s=128)),
                    reads=vdeps, writes=[bv_])
                CH = (0, 1)
                hsl = [slice(0, 64), slice(64, 128)]
                PZ = [PS[0], PS[1]]; bPZ = [PSB[0], PSB[1]]
                PO = [PS[2], PS[3]]; bPO = [PSB[2], PSB[3]]
                for c in CH:
                    b.op("dve", lambda e: e.memset(SA[c][:], 0.0), writes=[bSA[c]])
                npairs = nblk // 2
                for m in range(npairs):
                    kb = (nblk - 1 - 2 * m, nblk - 2 - 2 * m)
                    diag = kb[1] >= 4 * ti
                    mk = g["sbmask"][:, 2 * m:2 * m + 2, :].rearrange("p a t -> p (a t)") if diag else None
                    for c in CH:
                        hg = hp * 2 + c
                        for x in range(2):
                            b.op("pe", lambda e: e.matmul(out=PZ[c][:, x * 512:(x + 1) * 512],
                                                          lhsT=k_[hsl[c], kb[x] * 128:(kb[x] + 1) * 128],
                                                          rhs=q_[hsl[c], :], start=True, stop=True),
                                 reads=[bk_, bq_], writes=[bPZ[c]])
                    for c in CH:
                        hg = hp * 2 + c
                        b.op("act", lambda e: e.activation(out=E[c][:], in_=PZ[c][:], func=AF.Exp,
                                                           bias=sbias[:, hg:hg + 1], scale=1.0),
                             reads=[bPZ[c], bsb], writes=[bE[c]])
                    for c in CH:
                        b.op("act", lambda e: e.activation(out=SP[c][:], in_=E[c][:], func=AF.Ln,
                                                           bias=g["ccol"][:, 1:2], scale=1.0),
                             reads=[bE[c], self.bg["ccol"]], writes=[bSP[c]])
                        if diag:
                            b.op("pool", lambda e: e.tensor_tensor(out=SP[c][:], in0=SP[c][:], in1=mk, op=ALU.mult),
                                 reads=[bSP[c], self.bg["sbmask"]], writes=[bSP[c]])
                    for c in CH:
                        if m > 0:
                            b.op("pool", lambda e: e.tensor_copy(out=S16[c][:, 0:512], in_=SA[c][:]),
                                 reads=[bSA[c]], writes=[bS16[c]])
                        b.op("dve", lambda e: e.tensor_tensor(out=SBt[c][:], in0=SA[c][:], in1=SP[c][:, 0:512], op=ALU.add),
                             reads=[bSA[c], bSP[c]], writes=[bSBt[c]])
                        b.op("pool", lambda e: e.tensor_copy(out=S16[c][:, 512:1024], in_=SBt[c][:]),
                             reads=[bSBt[c]], writes=[bS16[c]])
                        b.op("dve", lambda e: e.tensor_tensor(out=SA[c][:], in0=SBt[c][:], in1=SP[c][:, 512:1024], op=ALU.add),
                             reads=[bSBt[c], bSP[c]], writes=[bSA[c]])
                    for c in CH:
                        for x in range(2):
                            has_c = not (m == 0 and x == 0)
                            b.op("pe", lambda e: e.matmul(out=PZ[c][:, x * 512:(x + 1) * 512],
                                                          lhsT=k_[hsl[c], kb[x] * 128:(kb[x] + 1) * 128],
                                                          rhs=q_[hsl[c], :], start=True, stop=False),
                                 reads=[bk_, bq_], writes=[bPZ[c]])
                            b.op("pe", lambda e: e.matmul(out=PZ[c][:, x * 512:(x + 1) * 512], lhsT=g["nutrib"][:],
                                                          rhs=SP[c][:, x * 512:(x + 1) * 512], start=False, stop=(not has_c)),
                                 reads=[bSP[c], self.bg["nutrib"]], writes=[bPZ[c]])
                            if has_c:
                                b.op("pe", lambda e: e.matmul(out=PZ[c][:, x * 512:(x + 1) * 512], lhsT=g["nonesb"][:],
                                                              rhs=S16[c][:, x * 512:(x + 1) * 512], start=False, stop=True),
                                     reads=[bS16[c], self.bg["nonesb"]], writes=[bPZ[c]])
                    for c in CH:
                        hg = hp * 2 + c
                        b.op("act", lambda e: e.activation(out=Wt[c][:], in_=PZ[c][:], func=AF.Exp,
                                                           bias=sbias[:, hg:hg + 1], scale=1.0),
                             reads=[bPZ[c], bsb], writes=[bWt[c]])
                        if diag:
                            b.op("pool", lambda e: e.tensor_tensor(out=Wt[c][:], in0=Wt[c][:], in1=mk, op=ALU.mult),
                                 reads=[bWt[c], self.bg["sbmask"]], writes=[bWt[c]])
                    for c in CH:
                        for x in range(2):
                            b.op("pe", lambda e: e.matmul(out=PO[c][0:64, 0:512], lhsT=v_[:, kb[x], hsl[c]],
                                                          rhs=Wt[c][:, x * 512:(x + 1) * 512],
                                                          start=(m == 0 and x == 0), stop=(m == npairs - 1 and x == 1)),
                                 reads=[bv_, bWt[c]], writes=[bPO[c]])
                for c in CH:
                    b.op("act", lambda e: e.activation(out=o16[c][:], in_=PO[c][0:64, 0:512], func=AF.Copy),
                         reads=[bPO[c]], writes=[bo16[c]])
                    b.dma("sp", "ao%d" % c, lambda e: e.dma_start(
                        out=self.AT[hp, c * 64:(c + 1) * 64, c0:c0 + 512], in_=o16[c][:]),
                        reads=[bo16[c]], writes=[b.dbuf(("ATa", c0))])
        b.barrier()


Prog.p2_even = _p2_even
Prog.sb_prompt = _sb_prompt


def _sb_sample(self, li):
    b, g, PS, PSB, d = self.b, self.g, self.PS, self.PSB, self.din
    ie = li // 2
    T, NS, NTS, NP = self.T, self.NS, self.NTS, self.NPAGES
    PG = min(8, NP)
    NG = NP // PG
    NC8 = NP * 8
    with contextlib.ExitStack() as st:
        sb = lambda n, s_, dt: b.sb(n, s_, dt, st)
        ptb = sb("ptb", [128, NP], I32); bptb = Buf()
        idx = sb("idx", [128, NP], I32); bidx = Buf()
        iob = sb("iob", [128, 1], F32); biob = Buf()
        sbias = sb("sbias", [128, 8], F32); bsb = Buf()
        qb = sb("qb", [128, 4, 512], F32); bqb = Buf()
        Kp = [sb("Kp", [128, PG, 512], F32) for _ in range(2)]; bKp = [Buf(), Buf()]
        prod = sb("prod", [128, PG * 512], F32); bprod = Buf()
        zall = sb("zall", [128, 4, NP, 8], F32); bz = Buf()
        SP = sb("SPs", [128, 4, NP, 8], BF16); bSP = Buf()
        tbo = sb("tbo", [128, NP, 8], F32); btbo = Buf()
        tb = [sb("tb", [128, NP, 8], F32) for _ in range(2)]; btb = [Buf(), Buf()]
        arg = sb("args", [128, NP, 8], F32); barg = Buf()
        Wt = sb("Wts", [128, 4, NP, 8], F32); bWt = Buf()
        pv16 = sb("pv16", [128, PG, 512], BF16); bpv = Buf()
        newtot = sb("newtot", [128, 4, 8], F32); bnt = Buf()
        Kn = sb("Kn", [4, 512], F32); bKn = Buf()
        Vn = sb("Vn", [4, 512], F32); bVn = Buf()
        qn = sb("qn", [4, 4, 512], F32); bqn = Buf()
        prodn = sb("prodn", [4, 4, 512], F32); bpn = Buf()
        zn = sb("zn", [4, 4, 8], F32); bzn = Buf()
        spn = sb("spn", [4, 4, 8], F32); bspn = Buf()
        spn16 = sb("spn16", [4, 4, 8], BF16); bspn16 = Buf()
        Wn = sb("Wn", [4, 4, 8], F32); bWn = Buf()
        pvn = sb("pvn", [4, 512], BF16); bpvn = Buf()
        snew3 = sb("snew3", [4, 4, 8], F32); bsn3 = Buf()
        orow = [sb("orow", [1, 512], F32) for _ in range(2)]; borow = [Buf(), Buf()]
        asb = sb("asb", [NTS, 512], F32); basb = Buf()
        at16 = sb("at16", [128, 4, NTS], BF16); bat = Buf()
        b.dma("sp", "prm", lambda e: e.dma_start(out=sbias[:], in_=d["sbb"][ie:ie + 1, :].partition_broadcast(128)),
              writes=[bsb])
        b.dma("sp", "prm", lambda e: e.dma_start(out=iob[:], in_=d["c_iotab"][:, ie:ie + 1], allow_slow_non_contiguous=True), writes=[biob])
        b.op("dve", lambda e: e.tensor_copy(out=snew3[:], in_=g["snew"][:].unsqueeze(2).to_broadcast([4, 4, 8])),
             reads=[self.bg["snew"]], writes=[bsn3])
        ng = 0
        no = 0
        POs = [(PS[2], PSB[2], 0), (PS[2], PSB[2], 512), (PS[3], PSB[3], 0), (PS[3], PSB[3], 512)]
        for s in range(NS):
            r0 = 4 * s
            b.dma("sp", "ss0", lambda e: e.dma_start(out=ptb[:], in_=d["pt"][s:s + 1, :].partition_broadcast(128)),
                  writes=[bptb])
            b.op("dve", lambda e: e.tensor_scalar(out=idx[:], in0=ptb[:], scalar1=128.0, scalar2=iob[:, 0:1],
                                                  op0=ALU.mult, op1=ALU.add), reads=[bptb, biob], writes=[bidx])
            b.dma("sp", "ss1", lambda e: e.dma_start(out=qb[:], in_=self.ZS[r0:r0 + 4, 0:512].partition_broadcast(128)),
                  reads=[b.dbuf(("ZS", 0))], writes=[bqb])
            b.dma("sp", "ss2", lambda e: e.dma_start(out=qn[:], in_=self.ZS[r0:r0 + 4, 0:512].partition_broadcast(4)),
                  reads=[b.dbuf(("ZS", 0))], writes=[bqn])
            b.dma("sp", "ss3", lambda e: e.dma_start(out=Kn[:], in_=self.dout["sbk_s"][ie, r0:r0 + 4, :]),
                  reads=[b.dbuf(("sbk_s", ie))], writes=[bKn])
            b.dma("sp", "ss4", lambda e: e.dma_start(out=Vn[:], in_=self.dout["sbv_s"][ie, r0:r0 + 4, :]),
                  reads=[b.dbuf(("sbv_s", ie))], writes=[bVn])
            for gi in range(NG):
                kp, bkp = Kp[ng % 2], bKp[ng % 2]
                for pgi in range(PG):
                    pg = gi * PG + pgi
                    b.dma("pool", "kp%d" % (ng % 2), lambda e: e.indirect_dma_start(
                        out=kp[:, pgi, :], out_offset=None, in_=d["poolk"],
                        in_offset=bass.IndirectOffsetOnAxis(ap=idx[:, pg:pg + 1], axis=0)),
                        reads=[bidx], writes=[bkp])
                ng += 1
                for t in range(4):
                    b.op("pool", lambda e: e.tensor_tensor(
                        out=prod[:].rearrange("p (a f) -> p a f", f=512), in0=kp[:],
                        in1=qb[:, t:t + 1, :].to_broadcast([128, PG, 512]), op=ALU.mult),
                        reads=[bkp, bqb], writes=[bprod])
                    b.op("dve", lambda e: e.tensor_reduce(
                        out=zall[:, t, gi * PG:(gi + 1) * PG, :], in_=prod[:].rearrange("p (a h x) -> p a h x", h=8, x=64),
                        axis=AX.X, op=ALU.add), reads=[bprod], writes=[bz])
            b.op("dve", lambda e: e.tensor_tensor(out=prodn[:], in0=qn[:], in1=Kn[:].unsqueeze(1).to_broadcast([4, 4, 512]),
                                                  op=ALU.mult), reads=[bqn, bKn], writes=[bpn])
            b.op("dve", lambda e: e.tensor_reduce(out=zn[:].rearrange("p t h -> p (t h)"),
                                                  in_=prodn[:].rearrange("p t (h x) -> p (t h) x", x=64),
                                                  axis=AX.X, op=ALU.add), reads=[bpn], writes=[bzn])
            b.op("dve", lambda e: e.scalar_tensor_tensor(out=zn[:], in0=zn[:], scalar=0.125,
                                                         in1=sbias[0:4, :].unsqueeze(1).to_broadcast([4, 4, 8]),
                                                         op0=ALU.mult, op1=ALU.add), reads=[bzn, bsb], writes=[bzn])
            b.op("act", lambda e: e.activation(out=spn[:], in_=zn[:], func=AF.Exp), reads=[bzn], writes=[bspn])
            b.op("act", lambda e: e.activation(out=spn[:], in_=spn[:], func=AF.Ln, bias=g["ccol"][0:4, 1:2], scale=1.0),
                 reads=[bspn, self.bg["ccol"]], writes=[bspn])
            b.op("dve", lambda e: e.tensor_tensor(out=spn16[:], in0=spn[:], in1=snew3[:], op=ALU.mult),
                 reads=[bspn, bsn3], writes=[bspn16])
            PR, bPR = PS[0], PSB[0]
            b.op("pe", lambda e: e.matmul(out=PR[0:4, 0:32], lhsT=g["nutrib"][0:4, 0:4], rhs=spn16[:].rearrange("p t h -> p (t h)"),
                                          start=True, stop=True), reads=[bspn16, self.bg["nutrib"]], writes=[bPR])
            b.op("pe", lambda e: e.matmul(out=PR[:, 512:544], lhsT=g["onesb"][0:4, :], rhs=spn16[:].rearrange("p t h -> p (t h)"),
                                          start=True, stop=True), reads=[bspn16, self.bg["onesb"]], writes=[bPR])
            b.op("dve", lambda e: e.tensor_copy(out=newtot[:].rearrange("p t h -> p (t h)"), in_=PR[:, 512:544]),
                 reads=[bPR], writes=[bnt])
            b.op("dve", lambda e: e.tensor_tensor(out=Wn[:].rearrange("p t h -> p (t h)"), in0=PR[0:4, 0:32],
                                                  in1=zn[:].rearrange("p t h -> p (t h)"), op=ALU.add),
                 reads=[bPR, bzn], writes=[bWn])
            b.op("act", lambda e: e.activation(out=Wn[:], in_=Wn[:], func=AF.Exp), reads=[bWn], writes=[bWn])
            b.op("dve", lambda e: e.tensor_tensor(out=Wn[:], in0=Wn[:], in1=snew3[:], op=ALU.mult),
                 reads=[bWn, bsn3], writes=[bWn])
            zf = zall[:].rearrange("p t a h -> p (t a) h")
            b.op("dve", lambda e: e.scalar_tensor_tensor(out=zf, in0=zf, scalar=0.125,
                                                         in1=sbias[:].unsqueeze(1).to_broadcast([128, 4 * NP, 8]),
                                                         op0=ALU.mult, op1=ALU.add), reads=[bz, bsb], writes=[bz])
            b.op("act", lambda e: e.activation(out=Wt[:], in_=zall[:], func=AF.Exp), reads=[bz], writes=[bWt])
            b.op("act", lambda e: e.activation(out=SP[:], in_=Wt[:], func=AF.Ln, bias=g["ccol"][:, 1:2], scale=1.0),
                 reads=[bWt, self.bg["ccol"]], writes=[bSP])
            for t in range(4):
                PRt, bPRt = PS[0], PSB[0]
                PTt, bPTt = PS[1], PSB[1]
                spt = SP[:, t].rearrange("p a h -> p (a h)")
                b.op("pe", lambda e: e.matmul(out=PRt[:, 0:NC8], lhsT=g["nutrib"][:], rhs=spt, start=True, stop=True),
                     reads=[bSP, self.bg["nutrib"]], writes=[bPRt])
                b.op("pe", lambda e: e.matmul(out=PTt[:, 0:NC8], lhsT=g["onesb"][:], rhs=spt, start=True, stop=True),
                     reads=[bSP, self.bg["onesb"]], writes=[bPTt])
                b.op("dve", lambda e: e.tensor_copy(out=tbo[:].rearrange("p a h -> p (a h)"), in_=PTt[:, 0:NC8]),
                     reads=[bPTt], writes=[btbo])
                cur, bcur = tbo, btbo
                k = 0
                dd = 1
                while dd < NP:
                    nxt, bnxt = tb[k % 2], btb[k % 2]
                    k += 1
                    b.op("dve", lambda e: e.tensor_tensor(out=nxt[:, 0:NP - dd, :], in0=cur[:, 0:NP - dd, :],
                                                          in1=cur[:, dd:NP, :], op=ALU.add), reads=[bcur], writes=[bnxt])
                    b.op("dve", lambda e: e.tensor_copy(out=nxt[:, NP - dd:NP, :], in_=cur[:, NP - dd:NP, :]),
                         reads=[bcur], writes=[bnxt])
                    cur, bcur = nxt, bnxt
                    dd *= 2
                b.op("dve", lambda e: e.tensor_tensor(out=arg[:].rearrange("p a h -> p (a h)"), in0=PRt[:, 0:NC8],
                                                      in1=zall[:, t].rearrange("p a h -> p (a h)"), op=ALU.add),
                     reads=[bPRt, bz], writes=[barg])
                if cur is not tbo:
                    b.op("dve", lambda e: e.tensor_tensor(out=arg[:], in0=arg[:], in1=cur[:], op=ALU.subtract),
                         reads=[barg, bcur], writes=[barg])
                    b.op("dve", lambda e: e.tensor_tensor(out=arg[:], in0=arg[:], in1=tbo[:], op=ALU.add),
                         reads=[barg, btbo], writes=[barg])
                b.op("dve", lambda e: e.tensor_tensor(out=arg[:], in0=arg[:],
                                                      in1=newtot[:, t:t + 1, :].to_broadcast([128, NP, 8]), op=ALU.subtract),
                     reads=[barg, bnt], writes=[barg])
                b.op("act", lambda e: e.activation(out=Wt[:, t], in_=arg[:], func=AF.Exp), reads=[barg], writes=[bWt])
            for gi in range(NG):
                vp, bvp = Kp[ng % 2], bKp[ng % 2]
                for pgi in range(PG):
                    pg = gi * PG + pgi
                    b.dma("pool", "kp%d" % (ng % 2), lambda e: e.indirect_dma_start(
                        out=vp[:, pgi, :], out_offset=None, in_=d["poolv"],
                        in_offset=bass.IndirectOffsetOnAxis(ap=idx[:, pg:pg + 1], axis=0)),
                        reads=[bidx], writes=[bvp])
                ng += 1
                for t in range(4):
                    PO, bPO, oc = POs[t]
                    b.op("dve", lambda e: e.tensor_tensor(
                        out=pv16[:].rearrange("p a (h x) -> p a h x", x=64), in0=vp[:].rearrange("p a (h x) -> p a h x", x=64),
                        in1=Wt[:, t, gi * PG:(gi + 1) * PG, :].unsqueeze(3).to_broadcast([128, PG, 8, 64]), op=ALU.mult),
                        reads=[bvp, bWt], writes=[bpv])
                    for pgi in range(PG):
                        b.op("pe", lambda e: e.matmul(out=PO[0:1, oc:oc + 512], lhsT=g["onesb"][:, 0:1], rhs=pv16[:, pgi, :],
                                                      start=(gi == 0 and pgi == 0), stop=False),
                             reads=[bpv, self.bg["onesb"]], writes=[bPO])
            for t in range(4):
                PO, bPO, oc = POs[t]
                b.op("dve", lambda e: e.tensor_tensor(
                    out=pvn[:].rearrange("p (h x) -> p h x", x=64), in0=Vn[:].rearrange("p (h x) -> p h x", x=64),
                    in1=Wn[:, t, :].unsqueeze(2).to_broadcast([4, 8, 64]), op=ALU.mult), reads=[bVn, bWn], writes=[bpvn])
                b.op("pe", lambda e: e.matmul(out=PO[0:1, oc:oc + 512], lhsT=g["onesb"][0:4, 0:1], rhs=pvn[:],
                                              start=False, stop=True), reads=[bpvn, self.bg["onesb"]], writes=[bPO])
                orw, borw = orow[no % 2], borow[no % 2]
                b.op("act", lambda e: e.activation(out=orw[:], in_=PO[0:1, oc:oc + 512], func=AF.Copy),
                     reads=[bPO], writes=[borw])
                b.dma("sp", "or%d" % (no % 2), lambda e: e.dma_start(out=self.AS[r0 + t:r0 + t + 1, 0:512], in_=orw[:]),
                      reads=[borw], writes=[b.dbuf(("AS", 0))])
                no += 1
        self.as_to_at(asb, basb, at16, bat, 4)
        b.barrier()


def _as_to_at(self, asb, basb, at16, bat, nchunk):
    b, g, PS, PSB = self.b, self.g, self.PS, self.PSB
    T, NTS = self.T, self.NTS
    b.dma("sp", "asl", lambda e: e.dma_start(out=asb[:], in_=self.AS[:, 0:nchunk * 128]),
          reads=[b.dbuf(("AS", 0))], writes=[basb])
    for j in range(nchunk):
        b.op("pe", lambda e: e.transpose(out=PS[0][:, j * NTS:(j + 1) * NTS], in_=asb[:, j * 128:(j + 1) * 128],
                                         identity=g["ident"][0:NTS, 0:NTS]),
             reads=[basb, self.bg["ident"]], writes=[PSB[0]])
    b.op("dve", lambda e: e.tensor_copy(out=at16[:].rearrange("p k t -> p (k t)"), in_=PS[0][:, 0:nchunk * NTS]),
         reads=[PSB[0]], writes=[bat])
    b.dma("sp", "ast", lambda e: e.dma_start(
        out=self.AT[0:nchunk, :, T:T + NTS].rearrange("k p t -> p k t"), in_=at16[:]),
        reads=[bat], writes=[b.dbuf(("ATa", T))])


Prog.sb_sample = _sb_sample
Prog.as_to_at = _as_to_at


def _bias_tables(self):
    b, g, PS, PSB, d = self.b, self.g, self.PS, self.PSB, self.din
    brv = b.sb("biasrev", [128, 3, 16], F32); bbrv = Buf()
    bnw = b.sb("biasnew", [4, 12, 16], F32); bbnw = Buf()
    g["biasrev"], self.bg["biasrev"] = brv, bbrv
    g["biasnew"], self.bg["biasnew"] = bnw, bbnw
    with contextlib.ExitStack() as st:
        sb = lambda n, s_, dt: b.sb(n, s_, dt, st)
        relb = sb("relb", [32, 16], F32); brl = Buf()
        oh = sb("oh", [32, 387], F32); boh = Buf()
        ohr = sb("ohr", [32, 384], F32); bohr = Buf()
        ohn = sb("ohn", [32, 48], F32); bohn = Buf()
        ngn = sb("ngn", [4, 12], F32); bngn = Buf()
        Et = sb("Et", [16, 3, 384], F32); bEt = Buf()
        Hk = [sb("Hk", [128, 2, 128], F32) for _ in range(2)]; bHk = [Buf(), Buf()]
        Bo = [sb("Bo", [128, 2, 128], F32) for _ in range(2)]; bBo = [Buf(), Buf()]
        for t_, bf_, nm in ((relb, brl, "relb"), (oh, boh, "c_oh"), (ohr, bohr, "c_ohrev"), (ohn, bohn, "c_ohnew"),
                            (ngn, bngn, "c_negnew")):
            b.dma("sp", "const", lambda e: e.dma_start(out=t_[:], in_=d[nm]), writes=[bf_])
        b.op("pe", lambda e: e.matmul(out=PS[0][0:16, 0:387], lhsT=relb[:], rhs=oh[:], start=True, stop=True),
             reads=[brl, boh], writes=[PSB[0]])
        b.op("dve", lambda e: e.memset(Et[:], NEG), writes=[bEt])
        for br in range(3):
            b.op("dve", lambda e: e.tensor_copy(out=Et[:, br, 127:256], in_=PS[0][0:16, br * 129:(br + 1) * 129]),
                 reads=[PSB[0]], writes=[bEt])
        b.dma("sp", "tab", lambda e: e.dma_start(out=self.TAB, in_=Et[:]), reads=[bEt], writes=[b.dbuf("TAB")])
        for br in range(3):
            b.op("pe", lambda e: e.matmul(out=PS[1][:, br * 16:(br + 1) * 16], lhsT=ohr[:, br * 128:(br + 1) * 128], rhs=relb[:],
                                          start=True, stop=True), reads=[bohr, brl], writes=[PSB[1]])
        b.op("dve", lambda e: e.tensor_copy(out=brv[:].rearrange("p a h -> p (a h)"), in_=PS[1][:, 0:48]),
             reads=[PSB[1]], writes=[bbrv])
        for k in range(12):
            b.op("pe", lambda e: e.matmul(out=PS[1][0:4, 512 + k * 16:512 + (k + 1) * 16], lhsT=ohn[:, k * 4:(k + 1) * 4], rhs=relb[:],
                                          start=True, stop=True), reads=[bohn, brl], writes=[PSB[1]])
        b.op("dve", lambda e: e.tensor_tensor(out=bnw[:], in0=PS[1][0:4, 512:704].rearrange("p (a h) -> p a h", h=16),
                                              in1=ngn[:].unsqueeze(2).to_broadcast([4, 12, 16]), op=ALU.add),
             reads=[PSB[1], bngn], writes=[bbnw])
        n = 0
        for br in range(3):
            for h in range(16):
                hk, bhk = Hk[n % 2], bHk[n % 2]
                bo, bbo = Bo[n % 2], bBo[n % 2]
                off = (h * 3 + br) * 384
                for x, o_ in ((0, 128), (1, 0)):
                    src = bass.AP(self.TAB.tensor, off + o_, [[1, 128], [1, 128]])
                    b.dma("sp", "hk%d" % (n % 2), lambda e: e.dma_start(out=hk[:, x, :], in_=src),
                          reads=[b.dbuf("TAB")], writes=[bhk])
                pz, pb = PS[2 + n % 2], PSB[2 + n % 2]
                b.op("pe", lambda e: e.matmul(out=pz[:, 0:256], lhsT=g["flip"][:], rhs=hk[:].rearrange("p x q -> p (x q)"),
                                              start=True, stop=True), reads=[bhk, self.bg["flip"]], writes=[pb])
                b.op("dve" if n % 2 else "act",
                     (lambda e: e.tensor_copy(out=bo[:].rearrange("p x q -> p (x q)"), in_=pz[:, 0:256])) if n % 2 else
                     (lambda e: e.activation(out=bo[:].rearrange("p x q -> p (x q)"), in_=pz[:, 0:256], func=AF.Copy)),
                     reads=[pb], writes=[bbo])
                b.dma("sp", "bo%d" % (n % 2), lambda e: e.dma_start(
                    out=self.BTD[br, h].rearrange("x s q -> s x q"), in_=bo[:]), reads=[bbo], writes=[b.dbuf("BTD")])
                n += 1
        b.barrier()


def _p1_odd(self, li):
    b, g, PS, PSB, d = self.b, self.g, self.PS, self.PSB, self.din
    io = li // 2
    T, NS, NTS, KEEP = self.T, self.NS, self.NTS, self.KEEP
    wb = self.WB["w_in_c"][io]
    with contextlib.ExitStack() as st:
        sb = lambda n, s_, dt: b.sb(n, s_, dt, st)
        xT = sb("xT", [128, 8, 512], F32); bxT = Buf()
        sq = sb("sq", [128, 8, 512], BF16); bsq = Buf()
        rs = sb("rs", [128, 512], F32); brs = Buf()
        uT = sb("uT", [128, 8, 512], BF16); buT = Buf()
        wts = [sb("wt", [128, 8, 512], BF16) for _ in range(4)]; bws = [Buf() for _ in range(4)]
        st16 = [sb("st16", [128, 4, 512], BF16) for _ in range(2)]; bst16 = [Buf(), Buf()]
        tm32 = [sb("tm32", [128, 512], F32) for _ in range(2)]; btm32 = [Buf(), Buf()]
        tm16 = [sb("tm16", [128, 512], BF16) for _ in range(2)]; btm16 = [Buf(), Buf()]
        nw = 0
        for ti, (c0, W, samp) in enumerate(self.tiles()):
            b.dma("sp", "xld", lambda e: e.dma_start(
                out=xT[:, :, 0:W], in_=self.XT[:, :, c0:c0 + W].rearrange("k p t -> p k t")),
                reads=[self.xt_buf(c0)], writes=[bxT])
            self.norm(xT, bxT, W, g["gmix"][:, li, :], uT, buT, sq, bsq, rs, brs, PS[0], PSB[0])
            subs = [(0, W)] if samp else [(s * 128, 128) for s in range(4)]
            for grp in range(6):
                wt, bw = wts[nw % 4], bws[nw % 4]
                self.wload(wt, bw, wb, 0, 8, grp * 512, 512, "w%d" % (nw % 4)); nw += 1
                kind, half = grp // 2, grp % 2
                if kind in (0, 1):
                    so, bso = st16[grp % 2], bst16[grp % 2]
                    for j in range(4):
                        pz, pb = PS[1 + (j % 2)], PSB[1 + (j % 2)]
                        for kc in range(8):
                            b.op("pe", lambda e: e.matmul(out=pz[:, 0:W], lhsT=wt[:, kc, j * 128:(j + 1) * 128],
                                                          rhs=uT[:, kc, 0:W], start=(kc == 0), stop=(kc == 7)),
                                 reads=[bw, buT], writes=[pb])
                        b.op("act", lambda e: e.activation(out=so[:, j, 0:W], in_=pz[:, 0:W], func=AF.Copy,
                                                           scale=(0.125 if kind == 0 else 1.0)), reads=[pb], writes=[bso])
                    dst = self.QT if kind == 0 else self.KT
                    b.dma("sp", "qk%d" % (grp % 2), lambda e: e.dma_start(
                        out=dst[half * 4:half * 4 + 4, :, c0:c0 + W].rearrange("k p t -> p k t"), in_=so[:, :, 0:W]),
                        reads=[bso], writes=[b.dbuf(("QT" if kind == 0 else "KT", c0))])
                if kind in (1, 2) or (samp and kind == 0):
                    for si, (s0, rows) in enumerate(subs):
                        pz, pb = PS[3], PSB[3]
                        for kc in range(8):
                            b.op("pe", lambda e: e.matmul(out=pz[0:rows, 0:512], lhsT=uT[:, kc, s0:s0 + rows],
                                                          rhs=wt[:, kc, :], start=(kc == 0), stop=(kc == 7)),
                                 reads=[bw, buT], writes=[pb])
                        tok = c0 + s0
                        t32, bt32 = tm32[si % 2], btm32[si % 2]
                        if samp or tok >= T - KEEP:
                            b.op("dve", lambda e: e.tensor_copy(out=t32[0:rows, :], in_=pz[0:rows, 0:512]),
                                 reads=[pb], writes=[bt32])
                            if samp:
                                if kind == 0:
                                    dstap, dkey = self.ZS[:, half * 512:(half + 1) * 512], ("ZS", 0)
                                else:
                                    nm = "dk_s" if kind == 1 else "dv_s"
                                    dstap, dkey = self.dout[nm][io][:, half * 512:(half + 1) * 512], (nm, io)
                            else:
                                r_ = tok - (T - KEEP)
                                dstap = (self.DKS if kind == 1 else self.DVS)[io, r_:r_ + rows, half * 512:(half + 1) * 512]
                                dkey = ("DKS" if kind == 1 else "DVS", io, c0)
                            b.dma("sp", "tm%d" % (si % 2), lambda e: e.dma_start(out=dstap, in_=t32[0:rows, :]),
                                  reads=[bt32], writes=[b.dbuf(dkey)])
                        if kind == 2 and not samp:
                            t16, bt16 = tm16[si % 2], btm16[si % 2]
                            b.op("act", lambda e: e.activation(out=t16[0:rows, :], in_=pz[0:rows, 0:512], func=AF.Copy),
                                 reads=[pb], writes=[bt16])
                            for hq in range(4):
                                b.dma("sp", "tv%d" % (si % 2), lambda e: e.dma_start(
                                    out=self.V16[half * 4 + hq, tok:tok + rows, :], in_=t16[0:rows, hq * 128:(hq + 1) * 128]),
                                    reads=[bt16], writes=[b.dbuf(("V16", c0))])
        b.barrier()


Prog.bias_tables = _bias_tables
Prog.p1_odd = _p1_odd


def _p2_odd(self, li):
    import os
    if os.environ.get("SKIPDP") != "1":
        self.dil_prompt(li)
    if os.environ.get("SKIPDS") != "1":
        self.dil_sample(li)


def _dil_prompt(self, li):
    b, g, PS, PSB, d = self.b, self.g, self.PS, self.PSB, self.din
    T = self.T
    NSB = T // 2048
    with contextlib.ExitStack() as st:
        sb = lambda n, s_, dt: b.sb(n, s_, dt, st)
        QTs = sb("QTd", [128, T], BF16); bQ = Buf()
        KTs = sb("KTd", [128, T], BF16); bK = Buf()
        Vc = sb("Vc", [128, 3, T // 128, 128], BF16); bV = Buf()
        BT = sb("BT", [128, 3, 2, 2, 128], F32); bBT = Buf()
        ND = sb("ND", [128, 2, 2048], F32); bN = Buf()
        bD = bN
        zb = [sb("zb", [128, 512], F32) for _ in range(2)]; bzb = [Buf(), Buf()]
        pT = [sb("pT", [128, 512], BF16) for _ in range(2)]; bpT = [Buf(), Buf()]
        o16 = sb("o16d", [128, 2048], BF16); bo = Buf()
        nu = 0
        for hp in range(8):
            b.dma("sp", "dq", lambda e: e.dma_start(out=QTs[:], in_=self.QT[hp, :, 0:T]),
                  reads=[b.dbuf(("QT", c * 512)) for c in range(T // 512)], writes=[bQ])
            b.dma("sp", "dk", lambda e: e.dma_start(out=KTs[:], in_=self.KT[hp, :, 0:T]),
                  reads=[b.dbuf(("KT", c * 512)) for c in range(T // 512)], writes=[bK])
            vdeps = [b.dbuf(("V16", c * 512)) for c in range(T // 512)]
            for br, (win, dil) in enumerate(DIL_PAIRS):
                nb = T // (128 * dil)
                for r in range(dil):
                    b.dma("sp", "dv", lambda e: e.dma_start(
                        out=Vc[:, br, r * nb:(r + 1) * nb, :],
                        in_=self.V16[hp, r:T:dil, :].rearrange("(k s) f -> s k f", s=128)),
                        reads=vdeps, writes=[bV])
                for h in range(2):
                    b.dma("sp", "db", lambda e: e.dma_start(
                        out=BT[:, br, h], in_=self.BTD[br, 2 * hp + h].rearrange("x s q -> s x q")),
                        reads=[b.dbuf("BTD")], writes=[bBT])
            for sbi in range(NSB):
                b.op("pool", lambda e: e.memset(ND[:], 0.0), writes=[bN])
                for br, (win, dil) in enumerate(DIL_PAIRS):
                    nb = T // (128 * dil)
                    bps = 16 // dil
                    for r in range(dil):
                        for m in range(bps):
                            gblk = sbi * bps + m
                            qstart = r + dil * 128 * gblk
                            qcols = slice(qstart, qstart + dil * 127 + 1, dil)
                            has_prev = gblk >= 1
                            pstart = qstart - dil * 128
                            pcols = slice(pstart, pstart + dil * 127 + 1, dil) if has_prev else qcols
                            PZ, bPZ = PS[nu % 2], PSB[nu % 2]
                            PO, bPO = PS[2 + nu % 2], PSB[2 + nu % 2]
                            z_, bz_ = zb[nu % 2], bzb[nu % 2]
                            p_, bp_ = pT[nu % 2], bpT[nu % 2]
                            nu += 1
                            for h in range(2):
                                hs = slice(h * 64, (h + 1) * 64)
                                for x, kc_ in ((0, pcols), (1, qcols)):
                                    b.op("pe", lambda e: e.matmul(out=PZ[:, h * 512 + x * 128:h * 512 + (x + 1) * 128],
                                                                  lhsT=KTs[hs, kc_], rhs=QTs[hs, qcols], start=True, stop=True),
                                         reads=[bK, bQ], writes=[bPZ])
                            b.op("dve", lambda e: e.tensor_tensor(out=z_[:].rearrange("s (h c) -> s h c", h=2),
                                                                  in0=PZ[:].rearrange("s (h c) -> s h c", h=2)[:, :, 0:256],
                                                                  in1=BT[:, br].rearrange("s h x q -> s h (x q)"), op=ALU.add),
                                 reads=[bPZ, bBT], writes=[bz_])
                            b.op("act", lambda e: e.activation(out=p_[:], in_=z_[:], func=AF.Exp), reads=[bz_], writes=[bp_])
                            xs = (0, 1) if has_prev else (1,)
                            for h in range(2):
                                for xi, x in enumerate(xs):
                                    vblk = Vc[:, br, r * nb + gblk - (1 - x), :]
                                    pc = p_[:, (h * 2 + x) * 128:(h * 2 + x + 1) * 128]
                                    b.op("pe", lambda e: e.matmul(out=PO[:, h * 128:(h + 1) * 128], lhsT=vblk, rhs=pc,
                                                                  start=(xi == 0), stop=(xi == len(xs) - 1)),
                                         reads=[bV, bp_], writes=[bPO])
                                for xi, x in enumerate(xs):
                                    pc = p_[:, (h * 2 + x) * 128:(h * 2 + x + 1) * 128]
                                    b.op("pe", lambda e: e.matmul(out=PO[:, 256 + h * 128:256 + (h + 1) * 128], lhsT=g["onesb"][:], rhs=pc,
                                                                  start=(xi == 0), stop=(xi == len(xs) - 1)),
                                         reads=[bp_, self.bg["onesb"]], writes=[bPO])
                            rel = r + dil * 128 * m
                            tc_ = slice(rel, rel + dil * 127 + 1, dil)
                            for h in range(2):
                                hs = slice(h * 64, (h + 1) * 64)
                                b.op("dve", lambda e: e.tensor_tensor(
                                    out=ND[hs, :, tc_],
                                    in0=PO[hs, 0:512].rearrange("p (g c) -> p g c", c=256)[:, :, h * 128:(h + 1) * 128],
                                    in1=ND[hs, :, tc_], op=ALU.add), reads=[bPO, bN], writes=[bN])
                b.op("dve", lambda e: e.reciprocal(out=ND[:, 1, :], in_=ND[:, 1, :]), reads=[bN], writes=[bN])
                b.op("dve", lambda e: e.tensor_tensor(out=o16[:], in0=ND[:, 0, :], in1=ND[:, 1, :], op=ALU.mult),
                     reads=[bN], writes=[bo])
                b.dma("sp", "do", lambda e: e.dma_start(out=self.AT[hp, :, sbi * 2048:(sbi + 1) * 2048], in_=o16[:]),
                      reads=[bo], writes=[b.dbuf(("ATa", sbi * 2048 + c * 512)) for c in range(4)] + [b.dbuf(("ATl", sbi * 2048 + c * 512)) for c in range(4)])
        b.barrier()


def _dil_sample(self, li):
    b, g, PS, PSB, d = self.b, self.g, self.PS, self.PSB, self.din
    io = li // 2
    T, NS, NTS = self.T, self.NS, self.NTS
    with contextlib.ExitStack() as st:
        sb = lambda n, s_, dt: b.sb(n, s_, dt, st)
        qb = sb("qbd", [128, 1024], F32); bqb = Buf()
        qn = sb("qnd", [4, 1024], F32); bqn = Buf()
        Kn = sb("Knd", [4, 1024], F32); bKn = Buf()
        Vn = sb("Vnd", [4, 1024], F32); bVn = Buf()
        Kd = [sb("Kd", [128, 1024], F32) for _ in range(2)]; bKd = [Buf(), Buf()]
        Vd = [sb("Vd", [128, 1024], F32) for _ in range(2)]; bVd = [Buf(), Buf()]
        prod = sb("prodd", [128, 1024], F32); bprod = Buf()
        z = sb("zd", [128, 16], F32); bz = Buf()
        p = sb("pd", [128, 16], F32); bp = Buf()
        pv16 = sb("pvd", [128, 1024], BF16); bpv = Buf()
        zn = sb("znd", [4, 16], F32); bzn = Buf()
        pn = sb("pnd", [4, 3, 16], F32); bpn = Buf()
        pns = sb("pns", [4, 16], F32); bpns = Buf()
        pvn = sb("pvnd", [4, 1024], BF16); bpvn = Buf()
        rden = sb("rden", [1, 16], F32); brd = Buf()
        orow = [sb("orowd", [1, 1024], F32) for _ in range(2)]; borow = [Buf(), Buf()]
        asb = sb("asbd", [NTS, 1024], F32); basb = Buf()
        at16 = sb("at16d", [128, 8, NTS], BF16); bat = Buf()
        nk = 0
        no = 0
        for s in range(NS):
            r0 = 4 * s
            b.dma("sp", "ss3", lambda e: e.dma_start(out=Kn[:], in_=self.dout["dk_s"][io, r0:r0 + 4, :]),
                  reads=[b.dbuf(("dk_s", io))], writes=[bKn])
            b.dma("sp", "ss4", lambda e: e.dma_start(out=Vn[:], in_=self.dout["dv_s"][io, r0:r0 + 4, :]),
                  reads=[b.dbuf(("dv_s", io))], writes=[bVn])
            for t in range(4):
                b.dma("sp", "ss1", lambda e: e.dma_start(out=qb[:], in_=self.ZS[r0 + t:r0 + t + 1, 0:1024].partition_broadcast(128)),
                      reads=[b.dbuf(("ZS", 0))], writes=[bqb])
                b.dma("sp", "ss2", lambda e: e.dma_start(out=qn[:], in_=self.ZS[r0 + t:r0 + t + 1, 0:1024].partition_broadcast(4)),
                      reads=[b.dbuf(("ZS", 0))], writes=[bqn])
                PN, bPN = PS[2], PSB[2]
                PD, bPD = PS[3], PSB[3]
                for br, (win, dil) in enumerate(DIL_PAIRS):
                    n = 128 - t if dil == 1 else 128
                    idx0 = CBUF + t - 128 * dil
                    kd, bkd = Kd[nk % 2], bKd[nk % 2]
                    vd, bvd = Vd[nk % 2], bVd[nk % 2]
                    b.dma("sp", "kd%d" % (nk % 2), lambda e: e.dma_start(out=kd[0:n, :], in_=d["cdk"][io, s, idx0:idx0 + (n - 1) * dil + 1:dil, :]),
                          writes=[bkd])
                    b.dma("sp", "vd%d" % (nk % 2), lambda e: e.dma_start(out=vd[0:n, :], in_=d["cdv"][io, s, idx0:idx0 + (n - 1) * dil + 1:dil, :]),
                          writes=[bvd])
                    nk += 1
                    b.op("dve", lambda e: e.tensor_tensor(out=prod[0:n, :], in0=kd[0:n, :], in1=qb[0:n, :], op=ALU.mult),
                         reads=[bkd, bqb], writes=[bprod])
                    b.op("dve", lambda e: e.tensor_reduce(out=z[0:n, :], in_=prod[0:n, :].rearrange("p (h x) -> p h x", x=64),
                                                          axis=AX.X, op=ALU.add), reads=[bprod], writes=[bz])
                    b.op("dve", lambda e: e.scalar_tensor_tensor(out=z[0:n, :], in0=z[0:n, :], scalar=0.125,
                                                                 in1=g["biasrev"][0:n, br, :], op0=ALU.mult, op1=ALU.add),
                         reads=[bz, self.bg["biasrev"]], writes=[bz])
                    b.op("act", lambda e: e.activation(out=p[0:n, :], in_=z[0:n, :], func=AF.Exp), reads=[bz], writes=[bp])
                    b.op("dve", lambda e: e.tensor_tensor(out=pv16[0:n, :].rearrange("p (h x) -> p h x", x=64),
                                                          in0=vd[0:n, :].rearrange("p (h x) -> p h x", x=64),
                                                          in1=p[0:n, :].unsqueeze(2).to_broadcast([n, 16, 64]), op=ALU.mult),
                         reads=[bvd, bp], writes=[bpv])
                    for c in range(2):
                        b.op("pe", lambda e: e.matmul(out=PN[0:1, c * 512:(c + 1) * 512], lhsT=g["onesb"][0:n, 0:1],
                                                      rhs=pv16[0:n, c * 512:(c + 1) * 512], start=(br == 0), stop=False),
                             reads=[bpv, self.bg["onesb"]], writes=[bPN])
                    b.op("pe", lambda e: e.matmul(out=PD[0:1, 0:16], lhsT=g["ones32"][0:n, 0:1], rhs=p[0:n, :],
                                                  start=(br == 0), stop=False), reads=[bp, self.bg["ones32"]], writes=[bPD])
                b.op("dve", lambda e: e.tensor_tensor(out=prod[0:4, :], in0=Kn[:], in1=qn[:], op=ALU.mult),
                     reads=[bKn, bqn], writes=[bprod])
                b.op("dve", lambda e: e.tensor_reduce(out=zn[:], in_=prod[0:4, :].rearrange("p (h x) -> p h x", x=64),
                                                      axis=AX.X, op=ALU.add), reads=[bprod], writes=[bzn])
                b.op("dve", lambda e: e.scalar_tensor_tensor(out=pn[:], in0=zn[:].unsqueeze(1).to_broadcast([4, 3, 16]), scalar=0.125,
                                                             in1=g["biasnew"][:, t * 3:(t + 1) * 3, :], op0=ALU.mult, op1=ALU.add),
                     reads=[bzn, self.bg["biasnew"]], writes=[bpn])
                b.op("act", lambda e: e.activation(out=pn[:], in_=pn[:], func=AF.Exp), reads=[bpn], writes=[bpn])
                b.op("dve", lambda e: e.tensor_tensor(out=pns[:], in0=pn[:, 0, :], in1=pn[:, 1, :], op=ALU.add),
                     reads=[bpn], writes=[bpns])
                b.op("dve", lambda e: e.tensor_tensor(out=pns[:], in0=pns[:], in1=pn[:, 2, :], op=ALU.add),
                     reads=[bpn, bpns], writes=[bpns])
                b.op("dve", lambda e: e.tensor_tensor(out=pvn[:].rearrange("p (h x) -> p h x", x=64),
                                                      in0=Vn[:].rearrange("p (h x) -> p h x", x=64),
                                                      in1=pns[:].unsqueeze(2).to_broadcast([4, 16, 64]), op=ALU.mult),
                     reads=[bVn, bpns], writes=[bpvn])
                for c in range(2):
                    b.op("pe", lambda e: e.matmul(out=PN[0:1, c * 512:(c + 1) * 512], lhsT=g["onesb"][0:4, 0:1],
                                                  rhs=pvn[:, c * 512:(c + 1) * 512], start=False, stop=True),
                         reads=[bpvn, self.bg["onesb"]], writes=[bPN])
                b.op("pe", lambda e: e.matmul(out=PD[0:1, 0:16], lhsT=g["ones32"][0:4, 0:1], rhs=pns[:], start=False, stop=True),
                     reads=[bpns, self.bg["ones32"]], writes=[bPD])
                b.op("dve", lambda e: e.reciprocal(out=rden[:], in_=PD[0:1, 0:16]), reads=[bPD], writes=[brd])
                orw, borw = orow[no % 2], borow[no % 2]
                b.op("dve", lambda e: e.tensor_tensor(out=orw[:].rearrange("p (h x) -> p h x", x=64),
                                                      in0=PN[0:1, :].rearrange("p (h x) -> p h x", x=64),
                                                      in1=rden[:].unsqueeze(2).to_broadcast([1, 16, 64]), op=ALU.mult),
                     reads=[bPN, brd], writes=[borw])
                b.dma("sp", "or%d" % (no % 2), lambda e: e.dma_start(out=self.AS[r0 + t:r0 + t + 1, :], in_=orw[:]),
                      reads=[borw], writes=[b.dbuf(("AS", 0))])
                no += 1
        self.as_to_at(asb, basb, at16, bat, 8)
        b.dbuf(("ATl", T))
        b.barrier()


Prog.p2_odd = _p2_odd
Prog.dil_prompt = _dil_prompt
Prog.dil_sample = _dil_sample
```
